# Optimizing a Trainium2 kernel written in Bass

```python
import math
import jax
import jax.numpy as jnp
from jax import lax
import numpy as np

D_MODEL = 1024
BATCH = 8
SEQ = 4096
DEPTH = 2

EPS = 1e-6
POOL_WINDOWS = (2, 4, 8, 16)
POOL_GROUP = D_MODEL // 16
POOL_WIDTH = POOL_GROUP * len(POOL_WINDOWS)
M_HEADS = 4
M_HEAD_DIM = D_MODEL // 16
M_WIDTH = M_HEADS * M_HEAD_DIM
M_CONV = 4
M_CHUNK = 64
D_HEADS = 4
D_HEAD_DIM = D_MODEL // 16
D_V_DIM = 2 * D_HEAD_DIM
D_QK_WIDTH = D_HEADS * 2 * D_HEAD_DIM
D_WIDTH = D_HEADS * D_V_DIM
Q_BLOCK = 128
ROPE_THETA = 10000.0
N_BRANCH = 3
D_FF = ((8 * D_MODEL // 3 + 127) // 128) * 128
IN_SPLITS = (POOL_WIDTH, M_WIDTH, M_WIDTH, M_WIDTH, M_WIDTH, 2 * M_HEADS, D_QK_WIDTH, D_QK_WIDTH, D_WIDTH, N_BRANCH * D_MODEL)
N_IN = sum(IN_SPLITS)

kernel_name = "hybrid_pool_mlstm_diffattn_macaron"


def rmsnorm(x, g):
    xf = x.astype(jnp.float32)
    y = xf * lax.rsqrt(jnp.mean(xf * xf, axis=-1, keepdims=True) + EPS)
    return (y * g).astype(x.dtype)


def swiglu(x, w13, w2):
    g, u = jnp.split(x @ w13, 2, axis=-1)
    return (jax.nn.silu(g) * u) @ w2


def rope_tables(positions, dim):
    inv = 1.0 / (ROPE_THETA ** (jnp.arange(0, dim, 2, dtype=jnp.float32) / dim))
    ang = positions.astype(jnp.float32)[..., None] * inv
    return jnp.cos(ang)[:, :, None, :], jnp.sin(ang)[:, :, None, :]


def apply_rope(x, cos, sin):
    x1, x2 = jnp.split(x.astype(jnp.float32), 2, axis=-1)
    return jnp.concatenate([x1 * cos - x2 * sin, x2 * cos + x1 * sin], axis=-1).astype(x.dtype)


def pool_mixer(u, w_grp, scale):
    B, S, _ = u.shape
    uf = u.astype(jnp.float32)
    cs = jnp.concatenate([jnp.zeros((B, 1, POOL_WIDTH), jnp.float32), jnp.cumsum(uf, axis=1)], axis=1)
    t = jnp.arange(S)
    outs = []
    for g, w in enumerate(POOL_WINDOWS):
        c = cs[:, :, g * POOL_GROUP:(g + 1) * POOL_GROUP]
        lo = jnp.maximum(t + 1 - w, 0)
        win_sum = c[:, 1:] - c[:, lo]
        cnt = jnp.minimum(t + 1, w).astype(jnp.float32)[None, :, None]
        outs.append(win_sum / cnt - uf[:, :, g * POOL_GROUP:(g + 1) * POOL_GROUP])
    pooled = jnp.stack(outs, axis=2).astype(u.dtype)
    mixed = jnp.einsum('bsgc,gcd->bsgd', pooled, w_grp)
    return mixed.reshape(B, S, POOL_WIDTH) * scale


def causal_depthwise_conv(x, w, b):
    C = x.shape[-1]
    y = lax.conv_general_dilated(x, w[:, None, :], window_strides=(1,), padding=((M_CONV - 1, 0),),
                                 dimension_numbers=('NWC', 'WIO', 'NWC'), feature_group_count=C)
    return y + b


def mlstm_chunkwise(q, k, v, i_pre, f_pre):
    B, H, S, dh = q.shape
    L = M_CHUNK
    NC = S // L
    qf = q.astype(jnp.float32)
    kf = k.astype(jnp.float32) / math.sqrt(dh)
    vf = v.astype(jnp.float32)
    logf = jax.nn.log_sigmoid(f_pre)

    def chunks(a):
        return jnp.moveaxis(a.reshape((B, H, NC, L) + a.shape[3:]), 2, 0)

    causal = jnp.tril(jnp.ones((L, L), dtype=bool))

    def step(carry, xs):
        Cm, nv, m = carry
        qc, kc, vc, ic, lfc = xs
        bcum = jnp.cumsum(lfc, axis=-1)
        dmat = bcum[..., :, None] - bcum[..., None, :] + ic[..., None, :]
        dmat = jnp.where(causal, dmat, -jnp.inf)
        inter = bcum + m[..., None]
        m_t = jnp.maximum(inter, jnp.max(dmat, axis=-1))
        w_intra = jnp.exp(dmat - m_t[..., None])
        w_inter = jnp.exp(inter - m_t)
        sc = jnp.einsum('bhtd,bhsd->bhts', qc, kc) * w_intra
        num = jnp.einsum('bhts,bhsd->bhtd', sc, vc) + w_inter[..., None] * jnp.einsum('bhed,bhtd->bhte', Cm, qc)
        den = jnp.sum(sc, axis=-1) + w_inter * jnp.einsum('bhd,bhtd->bht', nv, qc)
        h = num / jnp.maximum(jnp.abs(den), jnp.exp(-m_t))[..., None]
        b_last = bcum[..., -1]
        g = b_last[..., None] - bcum + ic
        m_new = jnp.maximum(b_last + m, jnp.max(g, axis=-1))
        decay = jnp.exp(b_last + m - m_new)
        wk = jnp.exp(g - m_new[..., None])
        C_new = decay[..., None, None] * Cm + jnp.einsum('bhs,bhse,bhsd->bhed', wk, vc, kc)
        n_new = decay[..., None] * nv + jnp.einsum('bhs,bhsd->bhd', wk, kc)
        return (C_new, n_new, m_new), h

    init = (jnp.zeros((B, H, dh, dh), jnp.float32), jnp.zeros((B, H, dh), jnp.float32), jnp.zeros((B, H), jnp.float32))
    _, hs = lax.scan(step, init, (chunks(qf), chunks(kf), chunks(vf), chunks(i_pre), chunks(logf)))
    return jnp.moveaxis(hs, 0, 2).reshape(B, H, S, dh)


def mlstm_branch(m_q, m_k, m_v, m_o, m_if, conv_w, conv_b, gate_b, norm_g):
    B, S, _ = m_q.shape
    qk = jax.nn.silu(causal_depthwise_conv(jnp.concatenate([m_q, m_k], axis=-1), conv_w, conv_b))
    q, k = jnp.split(qk, 2, axis=-1)

    def heads(a):
        return a.reshape(B, S, M_HEADS, M_HEAD_DIM).transpose(0, 2, 1, 3)

    gates = (m_if + gate_b).astype(jnp.float32).transpose(0, 2, 1)
    i_pre, f_pre = gates[:, :M_HEADS], gates[:, M_HEADS:]
    h = mlstm_chunkwise(heads(q), heads(k), heads(m_v), i_pre, f_pre)
    mu = jnp.mean(h, axis=-1, keepdims=True)
    var = jnp.mean(jnp.square(h - mu), axis=-1, keepdims=True)
    h = (h - mu) * lax.rsqrt(var + EPS)
    h = h.transpose(0, 2, 1, 3).reshape(B, S, M_WIDTH) * norm_g
    return (jax.nn.sigmoid(m_o.astype(jnp.float32)) * h).astype(m_q.dtype)


def diff_attention(q, k, v, lam, norm_g, lambda_init):
    B, S = q.shape[0], q.shape[1]
    qh = q.transpose(0, 2, 1, 3) * (D_HEAD_DIM ** -0.5)
    kh = k.transpose(0, 2, 1, 3)
    vh = v.transpose(0, 2, 1, 3)
    lamf = lam.astype(jnp.float32)
    lam_full = jnp.exp(jnp.sum(lamf[0] * lamf[1])) - jnp.exp(jnp.sum(lamf[2] * lamf[3])) + lambda_init
    NB = S // Q_BLOCK
    qb = qh.reshape(B, 2 * D_HEADS, NB, Q_BLOCK, D_HEAD_DIM).transpose(2, 0, 1, 3, 4)
    kpos = jnp.arange(S)

    def block(args):
        qblk, start = args
        s = jnp.einsum('bhqd,bhkd->bhqk', qblk, kh).astype(jnp.float32)
        qpos = start + jnp.arange(Q_BLOCK)
        s = jnp.where(kpos[None, :] <= qpos[:, None], s, -jnp.inf)
        p = jax.nn.softmax(s, axis=-1).reshape(B, D_HEADS, 2, Q_BLOCK, S)
        a = p[:, :, 0] - lam_full * p[:, :, 1]
        return jnp.einsum('bhqk,bhkd->bhqd', a.astype(vh.dtype), vh)

    out = lax.map(block, (qb, jnp.arange(NB, dtype=jnp.int32) * Q_BLOCK))
    out = out.transpose(1, 2, 0, 3, 4).reshape(B, D_HEADS, S, D_V_DIM).astype(jnp.float32)
    out = out * lax.rsqrt(jnp.mean(out * out, axis=-1, keepdims=True) + EPS) * norm_g
    out = out * (1.0 - lambda_init)
    return out.transpose(0, 2, 1, 3).reshape(B, S, D_WIDTH).astype(v.dtype)


def setup_inputs(seed: int = 0) -> dict:
    key = jax.random.key(seed)
    ks = jax.random.split(key, 24)
    f32 = jnp.float32
    L = DEPTH

    def nrm(k, shape, scale):
        return jax.random.normal(k, shape, f32) * scale

    def gain(k, shape):
        return 1.0 + 0.02 * jax.random.normal(k, shape, f32)

    f_bias = jnp.broadcast_to(jnp.linspace(3.0, 6.0, M_HEADS, dtype=f32), (L, M_HEADS))
    m_gate_b = jnp.concatenate([nrm(ks[10], (L, M_HEADS), 0.1), f_bias + nrm(ks[11], (L, M_HEADS), 0.1)], axis=-1)
    return {
        "x": jax.random.normal(ks[0], (BATCH, SEQ, D_MODEL), f32),
        "positions": jnp.broadcast_to(jnp.arange(SEQ, dtype=jnp.int32), (BATCH, SEQ)),
        "ffn1_norm": gain(ks[1], (L, D_MODEL)),
        "ffn1_w13": nrm(ks[2], (L, D_MODEL, 2 * D_FF), D_MODEL ** -0.5),
        "ffn1_w2": nrm(ks[3], (L, D_FF, D_MODEL), D_FF ** -0.5),
        "mix_norm": gain(ks[4], (L, D_MODEL)),
        "w_in": nrm(ks[5], (L, D_MODEL, N_IN), D_MODEL ** -0.5),
        "pool_w": nrm(ks[6], (L, len(POOL_WINDOWS), POOL_GROUP, POOL_GROUP), POOL_GROUP ** -0.5),
        "pool_scale": gain(ks[7], (L, POOL_WIDTH)),
        "m_conv_w": nrm(ks[8], (L, M_CONV, 2 * M_WIDTH), M_CONV ** -0.5),
        "m_conv_b": nrm(ks[9], (L, 2 * M_WIDTH), 0.01),
        "m_gate_b": m_gate_b,
        "m_norm": gain(ks[12], (L, M_WIDTH)),
        "d_lambda": nrm(ks[13], (L, 4, D_HEAD_DIM), 0.1),
        "d_norm": gain(ks[14], (L, D_V_DIM)),
        "p_a": nrm(ks[15], (L, POOL_WIDTH, D_MODEL), POOL_WIDTH ** -0.5),
        "p_b": nrm(ks[16], (L, M_WIDTH, D_MODEL), M_WIDTH ** -0.5),
        "p_c": nrm(ks[17], (L, D_WIDTH, D_MODEL), D_WIDTH ** -0.5),
        "w_out": nrm(ks[18], (L, D_MODEL, D_MODEL), D_MODEL ** -0.5),
        "ffn2_norm": gain(ks[19], (L, D_MODEL)),
        "ffn2_w13": nrm(ks[20], (L, D_MODEL, 2 * D_FF), D_MODEL ** -0.5),
        "ffn2_w2": nrm(ks[21], (L, D_FF, D_MODEL), D_FF ** -0.5),
        "final_norm": gain(ks[22], (D_MODEL,)),
    }


def reference(x, positions, ffn1_norm, ffn1_w13, ffn1_w2, mix_norm, w_in, pool_w, pool_scale, m_conv_w, m_conv_b, m_gate_b, m_norm, d_lambda, d_norm, p_a, p_b, p_c, w_out, ffn2_norm, ffn2_w13, ffn2_w2, final_norm):
    B, S, _ = x.shape
    cos, sin = rope_tables(positions, D_HEAD_DIM)
    split_at = np.cumsum(IN_SPLITS)[:-1].tolist()
    for l in range(DEPTH):
        lambda_init = 0.8 - 0.6 * math.exp(-0.3 * l)
        x = x + 0.5 * swiglu(rmsnorm(x, ffn1_norm[l]), ffn1_w13[l], ffn1_w2[l])
        h = rmsnorm(x, mix_norm[l])
        z = jnp.einsum('bsd,dn->bsn', h, w_in[l])
        (u_pool, m_q, m_k, m_v, m_o, m_if, d_q, d_k, d_v, gate_pre) = jnp.split(z, split_at, axis=-1)
        y_a = pool_mixer(u_pool, pool_w[l], pool_scale[l])
        y_b = mlstm_branch(m_q, m_k, m_v, m_o, m_if, m_conv_w[l], m_conv_b[l], m_gate_b[l], m_norm[l])
        dq = apply_rope(d_q.reshape(B, S, 2 * D_HEADS, D_HEAD_DIM), cos, sin)
        dk = apply_rope(d_k.reshape(B, S, 2 * D_HEADS, D_HEAD_DIM), cos, sin)
        y_c = diff_attention(dq, dk, d_v.reshape(B, S, D_HEADS, D_V_DIM), d_lambda[l], d_norm[l], lambda_init)
        g_a, g_b, g_c = jnp.split(jax.nn.sigmoid(gate_pre), N_BRANCH, axis=-1)
        merged = g_a * (y_a @ p_a[l]) + g_b * (y_b @ p_b[l]) + g_c * (y_c @ p_c[l])
        x = x + merged @ w_out[l]
        x = x + 0.5 * swiglu(rmsnorm(x, ffn2_norm[l]), ffn2_w13[l], ffn2_w2[l])
    return rmsnorm(x, final_norm)
```

```python
import math
import os
import numpy as np
import concourse.bass as bass
import concourse.mybir as mybir
from concourse.bass_utils import run_bass_kernel_spmd

F32 = mybir.dt.float32
BF16 = mybir.dt.bfloat16
I32 = mybir.dt.int32
ALU = mybir.AluOpType
AF = mybir.ActivationFunctionType

SEM_LIMIT = 12000
NDMASEM = 8


class Sched:
    def __init__(self):
        self.ops = []
        self.last_w = {}
        self.part_w = {}
        self.readers = {}

    def add(self, eng, fn, r=(), w=(), dma=False, pw=()):
        i = len(self.ops)
        deps = set()
        for k in r:
            deps.update(self.last_w.get(k, ()))
            deps.update(self.part_w.get(k, ()))
            if isinstance(k, tuple) and k and k[0] == 'ps':
                for j in self.readers.get(k, ()):
                    if self.ops[j]['eng'] != eng:
                        deps.add(j)
        for k in w:
            deps.update(self.last_w.get(k, ()))
            deps.update(self.part_w.get(k, ()))
            deps.update(self.readers.get(k, ()))
        for k in pw:
            deps.update(self.last_w.get(k, ()))
            deps.update(self.readers.get(k, ()))
        for k in r:
            self.readers.setdefault(k, []).append(i)
        for k in w:
            self.last_w[k] = [i]
            self.part_w[k] = []
            self.readers[k] = []
        for k in pw:
            self.part_w.setdefault(k, []).append(i)
        deps.discard(i)
        latest = {}
        keep = set()
        for j in deps:
            d = self.ops[j]
            if d['dma']:
                keep.add(j)
            elif j > latest.get(d['eng'], -1):
                latest[d['eng']] = j
        keep.update(latest.values())
        deps = keep
        self.ops.append(dict(eng=eng, fn=fn, deps=deps, dma=dma, sig=None, needed=False))
        return i

    def pe(self, fn, r=(), w=()):
        return self.add('pe', fn, r, w)

    def act(self, fn, r=(), w=()):
        return self.add('act', fn, r, w)

    def dve(self, fn, r=(), w=()):
        return self.add('dve', fn, r, w)

    def pool(self, fn, r=(), w=()):
        return self.add('pool', fn, r, w)

    def dma(self, fn, r=(), w=(), q='sp', pw=()):
        return self.add(q, fn, r, w, dma=True, pw=pw)

    def emit(self, nc, stack):
        ops = self.ops
        for op in ops:
            for j in op['deps']:
                d = ops[j]
                if d['eng'] == 'pe' and op['eng'] == 'pe' and not d['dma'] and not op['dma']:
                    continue
                d['needed'] = True
        sems = {}

        def getsem(name):
            if name not in sems:
                sems[name] = stack.enter_context(nc.semaphore(name))
            return sems[name]

        cnt = {}
        gen = {}
        dcnt = {}
        for op in ops:
            e = op['eng']
            if op['dma']:
                n = dcnt.get(e, 0)
                dcnt[e] = n + 1
                r = n % NDMASEM
                s = getsem(f"d_{e}_{r}")
                op['sig'] = (s, 16 * (n // NDMASEM + 1))
                op['pre'] = (s, 16 * (n // NDMASEM)) if n >= NDMASEM else None
            elif op['needed']:
                c = cnt.get(e, 0) + 1
                if c > SEM_LIMIT:
                    gen[e] = gen.get(e, 0) + 1
                    c = 1
                cnt[e] = c
                op['sig'] = (getsem(f"s_{e}_{gen.get(e, 0)}"), c)
        by_eng = {}
        for i, op in enumerate(ops):
            by_eng.setdefault(op['eng'], []).append(i)
        self.n_waits = 0

        def run_engine(ename, eng):
            seen = {}
            for i in by_eng.get(ename, ()):
                op = ops[i]
                waits = {}
                for j in op['deps']:
                    d = ops[j]
                    if d['sig'] is None:
                        continue
                    if d['eng'] == 'pe' and ename == 'pe' and not d['dma'] and not op['dma']:
                        continue
                    s, v = d['sig']
                    k = id(s)
                    if k not in waits or waits[k][1] < v:
                        waits[k] = (s, v)
                if op['dma'] and op.get('pre') is not None:
                    s, v = op['pre']
                    k = id(s)
                    if k not in waits or waits[k][1] < v:
                        waits[k] = (s, v)
                for k, (s, v) in waits.items():
                    if seen.get(k, 0) >= v:
                        continue
                    eng.wait_ge(s, v)
                    seen[k] = v
                    self.n_waits += 1
                ins = op['fn'](eng)
                if op['sig'] is not None:
                    s, v = op['sig']
                    ins.then_inc(s, 16 if op['dma'] else 1)

        with nc.Block() as block:
            @block.tensor
            def _(eng):
                run_engine('pe', eng)

            @block.scalar
            def _(eng):
                run_engine('act', eng)

            @block.vector
            def _(eng):
                run_engine('dve', eng)

            @block.gpsimd
            def _(eng):
                run_engine('pool', eng)

            @block.sync
            def _(eng):
                run_engine('sp', eng)
                for name, s in sems.items():
                    pass
                last = {}
                for i in by_eng.get('sp', ()):
                    op = ops[i]
                    if op['dma']:
                        s, v = op['sig']
                        last[id(s)] = (s, v)
                for s, v in last.values():
                    eng.wait_ge(s, v)


D = 1024
DFF = 2816
NFF = 22
NIN = 5896
NL = 2
EPS = 1e-6
AX = mybir.AxisListType

C_ID, C_MN, C_MP, C_INV, C_SEL, C_ICNT, CW_ = 0, 128, 256, 384, 416, 928, 992
POOL_WINDOWS = (2, 4, 8, 16)

WSHAPES = [
    ("ffn1_norm", [NL, D]), ("ffn1_w13", [NL, D, 2 * DFF]), ("ffn1_w2", [NL, DFF, D]),
    ("mix_norm", [NL, D]), ("w_in", [NL, D, NIN]), ("pool_w", [NL, 4, 64, 64]),
    ("pool_scale", [NL, 256]), ("m_conv_w", [NL, 4, 512]), ("m_conv_b", [NL, 512]),
    ("m_gate_b", [NL, 8]), ("m_norm", [NL, 256]), ("d_lambda", [NL, 4, 64]),
    ("d_norm", [NL, 128]), ("p_a", [NL, 256, D]), ("p_b", [NL, 256, D]), ("p_c", [NL, 512, D]),
    ("w_out", [NL, D, D]), ("ffn2_norm", [NL, D]), ("ffn2_w13", [NL, D, 2 * DFF]),
    ("ffn2_w2", [NL, DFF, D]), ("final_norm", [1, D]),
]


def make_consts():
    c = np.zeros((128, CW_), np.float32)
    c[:, C_ID:C_ID + 128] = np.eye(128, dtype=np.float32)
    s = np.arange(128)[:, None]
    t = np.arange(128)[None, :]
    c[:, C_MN:C_MN + 128] = np.where(s <= t, 0.0, -30000.0)
    c[:, C_MP:C_MP + 128] = np.where(s <= t, 0.0, 30000.0)
    inv = np.float32(1.0) / np.power(np.float32(10000.0), np.arange(0, 64, 2, dtype=np.float32) / np.float32(64))
    c[:, C_INV:C_INV + 32] = inv[None, :]
    for h in range(4):
        c[4 + h, C_SEL + h * 128:C_SEL + (h + 1) * 128] = 1.0
    for gi, w in enumerate(POOL_WINDOWS):
        c[:, C_ICNT + gi * 16:C_ICNT + (gi + 1) * 16] = 1.0 / np.minimum(np.arange(16) + 1, w)
    return c


def _dsize(dt):
    return 4 if dt in (F32, I32) else 2


class Arena:
    def __init__(self, t, nwords):
        self.t = t
        self.n = nwords
        self.off = 0

    def reset(self):
        self.off = 0

    def alloc(self, shape, dt, parts=None):
        nel = 1
        for s_ in shape[1:]:
            nel *= s_
        nw = (nel * _dsize(dt) + 3) // 4
        nw = (nw + 7) // 8 * 8
        assert self.off + nw <= self.n, ("arena overflow", self.off, nw, self.n)
        a = self.t[0:shape[0], self.off:self.off + nw]
        self.off += nw
        if dt != F32:
            a = a.bitcast(dt)
        a = a[:, 0:nel]
        return _view(a, shape[1:])


def _view(a, dims):
    if len(dims) == 1:
        return a
    if len(dims) == 2:
        return a.rearrange("p (a b) -> p a b", a=dims[0])
    if len(dims) == 3:
        return a.rearrange("p (a b c) -> p a b c", a=dims[0], b=dims[1])
    raise ValueError(dims)


def bc_mid(ap2, n):
    return ap2.unsqueeze(1).broadcast_to([ap2.shape[0], n, ap2.shape[1]])


def bc_last(ap2, n):
    return ap2.unsqueeze(2).broadcast_to([ap2.shape[0], ap2.shape[1], n])


class Builder:
    def __init__(self, S, debug=False, phases=None):
        self.S = S
        self.NT = S // 128
        self.NG = S // 512
        self.debug = debug
        self.phases = phases
        self.nc = bass.Bass("TRN2", target_bir_lowering=False)
        self.sc = Sched()
        _add = self.sc.add

        def add2(eng, fn, r=(), w=(), dma=False, pw=()):
            r = list(r)
            for k in list(r) + list(w) + list(pw):
                if isinstance(k, tuple) and k and (k[0] == 'A' or (isinstance(k[0], tuple) and k[0] and k[0][0] == 'A')):
                    r.append('arena')
                    break
            return _add(eng, fn, r, w, dma, pw)
        self.sc.add = add2
        self.castq = []
        self.cast_items_done = 0
        self.n_early = 2 * NFF + NFF // 2
        self.cast_recorded = 0
        self.pending_fin = None
        self.xg_ctr = 0

    def declare(self):
        nc, S, NT = self.nc, self.S, self.NT
        kind_s = "ExternalOutput" if self.debug else "Internal"

        def din(name, shape, dt=F32):
            return nc.dram_tensor(name, list(shape), dt, kind="ExternalInput")

        def dsc(name, shape, dt):
            return nc.dram_tensor(name, list(shape), dt, kind=kind_s)
        self.x_in = din("x", [S, D])
        self.pos_in = din("pos", [128, NT], I32)
        self.cst_in = din("cst", [128, CW_])
        self.W = {n: din(n, s) for n, s in WSHAPES}
        self.out_t = nc.dram_tensor("out", [S, D], F32, kind="ExternalOutput")
        self.xs = dsc("xs", [S, D], F32)
        self.w13b = [[dsc(f"w13b_{l}_{f}", [2 * NFF, 128, 1024], BF16) for f in range(2)] for l in range(NL)]
        self.w2b = [[dsc(f"w2b_{l}_{f}", [NFF, 128, 1024], BF16) for f in range(2)] for l in range(NL)]
        self.winF = [dsc(f"winF_{l}", [30, 128, 1024], BF16) for l in range(NL)]
        self.winIF = [dsc(f"winIF_{l}", [1, 128, 64], BF16) for l in range(NL)]
        self.winT = [dsc(f"winT_{l}", [4, 128, 4096], BF16) for l in range(NL)]
        self.pab = [dsc(f"pab_{l}", [8, 128, 256], BF16) for l in range(NL)]
        self.pbb = [dsc(f"pbb_{l}", [8, 128, 256], BF16) for l in range(NL)]
        self.pcb = [dsc(f"pcb_{l}", [8, 128, 512], BF16) for l in range(NL)]
        self.woutb = [dsc(f"woutb_{l}", [2, 128, 4096], BF16) for l in range(NL)]
        self.yaT = dsc("yaT", [2, 128, S], BF16)
        self.mqT = dsc("mqT", [2, 128, S], BF16)
        self.mkZ = dsc("mkZ", [4, 128, S], BF16)
        self.kTok = dsc("kTok", [NT, 128, 256], BF16)
        self.vv = dsc("vv", [NT, 128, 260], BF16)
        self.so = dsc("so", [NT, 128, 256], F32)
        self.cs8 = dsc("cs8", [8, S], F32)
        self.dqT = dsc("dqT", [4, 128, S], BF16)
        self.dkZ = dsc("dkZ", [8, 128, S], BF16)
        self.dvv = dsc("dvv", [NT, 128, 520], BF16)
        self.ybT = dsc("ybT", [2, 128, S], BF16)
        self.ycT = dsc("ycT", [4, 128, S], BF16)

        A = nc.alloc_sbuf_tensor
        self.CST = A("CST", [128, CW_], F32)
        self.IDB = A("IDB", [128, 128], BF16)
        self.MNB = A("MNB", [128, 128], BF16)
        self.COS = A("COS", [128, NT * 32], F32)
        self.SIN = A("SIN", [128, NT * 32], F32)
        self.XG = [A(f"XG{i}", [128, 4, 1024], F32) for i in range(2)]
        self.HN = [A(f"HN{i}", [128, 1024], BF16) for i in range(4)]
        self.HT = A("HT", [128, 8, 512], BF16)
        self.GBC = [A(f"GBC{i}", [128, 1024], F32) for i in range(2)]
        self.gbc_ctr = 0
        self.SSQ = A("SSQ", [128, 4], F32)
        self.RSTD = A("RSTD", [128, 4], F32)
        self.SSQ2 = A("SSQ2", [128, 4], F32)
        self.RSTD2 = A("RSTD2", [128, 4], F32)
        self.NST = 2
        self.STF = [A(f"STF{i}", [128, 2048], F32) for i in range(self.NST)]
        self.STB = [A(f"STB{i}", [128, 2048], BF16) for i in range(self.NST)]
        self.cast_ctr = 0
        self.NST_EARLY = 6
        self.PWF = A("PWF", [128, 2, 128], F32)
        self.PWB = A("PWB", [128, 2, 128], BF16)
        self.PSC = A("PSC", [128, 2], F32)
        self.CWc = A("CWc", [128, 4, 4], F32)
        self.CB = A("CB", [128, 4], F32)
        self.GB8 = A("GB8", [8, 1], F32)
        self.MNBC = A("MNBC", [128, 256], F32)
        self.DNBC = A("DNBC", [128, 128], F32)
        self.LAM = A("LAM", [128, 256], F32)
        self.DNCOL = A("DNCOL", [128, 1], F32)
        self.LAMC = A("LAMC", [128, 8], F32)
        self.COLA = A("COLA", [128, NT, 4], F32)
        self.NCE = A("NCE", [128, 4, NT], F32)
        self.WKc = A("WKc", [128, NT, 4], F32)
        self.AA = A("AA", [128, 2, NT], F32)
        self.BAR = A("BAR", [128, 8], F32)
        self.ONES8 = A("ONES8", [8, 128], F32)
        self.ARENA_WORDS = 27136
        self.ARENA_T = A("ARENA", [128, self.ARENA_WORDS], F32)
        self.arena = Arena(self.ARENA_T, self.ARENA_WORDS)
        for i_ in range(4):
            o_ = self.ARENA_WORDS - (i_ + 1) * 3072
            self.STF.append(self.ARENA_T[:, o_:o_ + 2048])
            self.STB.append(self.ARENA_T[:, o_ + 2048:o_ + 3072].bitcast(BF16))
        self.PS = [nc.alloc_psum_tensor(f"ps{b}", [128, 512], F32) for b in range(8)]

    def barrier(self):
        BAR = self.BAR
        self.sc.dve(lambda e: e.memset(BAR[0:1, 0:1], 0.0), w=['arena'])
        self.arena.reset()

    def init_consts(self):
        sc, NT = self.sc, self.NT
        CST, IDB, MNB, COS, SIN = self.CST, self.IDB, self.MNB, self.COS, self.SIN
        cst_in, pos_in = self.cst_in, self.pos_in
        sc.dma(lambda e: e.dma_start(out=CST[:], in_=cst_in[:, :]), w=['CST'])
        sc.dve(lambda e: e.tensor_copy(out=IDB[:], in_=CST[:, C_ID:C_ID + 128]), r=['CST'], w=['IDB'])
        sc.dve(lambda e: e.tensor_copy(out=MNB[:], in_=CST[:, C_MN:C_MN + 128]), r=['CST'], w=['MNB'])
        ONES8 = self.ONES8
        sc.dve(lambda e: e.memset(ONES8[:], 1.0), w=['ONES8'])
        PWF = self.PWF
        sc.dve(lambda e: e.memset(PWF[:], 0.0), w=['PWF'])
        ar = self.arena
        posi = ar.alloc([128, NT], I32)
        posf = ar.alloc([128, NT], F32)
        ang = ar.alloc([128, NT * 32], F32)
        a2 = ar.alloc([128, NT * 32], F32)
        u = ar.alloc([128, NT * 32], F32)
        ki = ar.alloc([128, NT * 32], I32)
        kf = ar.alloc([128, NT * 32], F32)
        m = ar.alloc([128, NT * 32], F32)
        kA = ('A', 'init')
        sc.dma(lambda e: e.dma_start(out=posi, in_=pos_in[:, :]), w=[kA])
        sc.dve(lambda e: e.tensor_copy(out=posf, in_=posi), r=[kA], w=[kA])
        for n in range(NT):
            sc.dve(lambda e, n=n: e.tensor_scalar(out=ang[:, n * 32:(n + 1) * 32], in0=CST[:, C_INV:C_INV + 32],
                                                  scalar1=posf[:, n:n + 1], scalar2=None, op0=ALU.mult),
                   r=[kA, 'CST'], w=[kA])
        TWO_PI = 2.0 * math.pi
        C1 = 6.28125
        C2 = TWO_PI - C1

        def reduce_sin(src, shift, dst, key):
            sc.dve(lambda e: e.tensor_scalar(out=a2, in0=src, scalar1=float(shift), scalar2=None, op0=ALU.add), r=[kA], w=[kA])
            sc.dve(lambda e: e.tensor_scalar(out=u, in0=a2, scalar1=float(1.0 / TWO_PI), scalar2=None, op0=ALU.mult), r=[kA], w=[kA])
            sc.dve(lambda e: e.tensor_copy(out=ki, in_=u), r=[kA], w=[kA])
            sc.dve(lambda e: e.tensor_copy(out=kf, in_=ki), r=[kA], w=[kA])
            sc.dve(lambda e: e.scalar_tensor_tensor(out=a2, in0=kf, scalar=-C1, in1=a2, op0=ALU.mult, op1=ALU.add), r=[kA], w=[kA])
            sc.dve(lambda e: e.scalar_tensor_tensor(out=a2, in0=kf, scalar=-C2, in1=a2, op0=ALU.mult, op1=ALU.add), r=[kA], w=[kA])
            sc.dve(lambda e: e.tensor_single_scalar(out=m, in_=a2, scalar=math.pi, op=ALU.is_gt), r=[kA], w=[kA])
            sc.dve(lambda e: e.scalar_tensor_tensor(out=a2, in0=m, scalar=-TWO_PI, in1=a2, op0=ALU.mult, op1=ALU.add), r=[kA], w=[kA])
            sc.dve(lambda e: e.tensor_single_scalar(out=m, in_=a2, scalar=-math.pi, op=ALU.is_lt), r=[kA], w=[kA])
            sc.dve(lambda e: e.scalar_tensor_tensor(out=a2, in0=m, scalar=TWO_PI, in1=a2, op0=ALU.mult, op1=ALU.add), r=[kA], w=[kA])
            sc.dve(lambda e: e.tensor_scalar(out=a2, in0=a2, scalar1=3.1415925, scalar2=-3.1415925, op0=ALU.min, op1=ALU.max), r=[kA], w=[kA])
            sc.act(lambda e: e.activation(out=dst[:], in_=a2, func=AF.Sin), r=[kA], w=[key])
        reduce_sin(ang, 0.0, SIN, 'SIN')
        reduce_sin(ang, math.pi / 2.0, COS, 'COS')
        self.barrier()

    def _cast_route(self):
        n = self.cast_items_done
        self.cast_items_done += 1
        if n < self.n_early:
            return ('pool', 'act', 'dve')[n % 3], 'sp'
        return 'pool', 'pool'

    def _cast_emit(self, in_ap, out_ap, src_ap, dst_ap, st_in, st_out, dkey, i):
        sc = self.sc
        eng, q = self._cast_route()
        kf_, kb_ = (('STF', i), ('STB', i)) if i < 2 else (('A', 'STF', i), ('A', 'STB', i))
        sc.dma(lambda e: e.dma_start(out=st_in, in_=src_ap), w=[kf_], q=q)

        def fin():
            if eng == 'act':
                sc.act(lambda e: e.copy(out=out_ap, in_=in_ap), r=[kf_], w=[kb_])
            elif eng == 'dve':
                sc.dve(lambda e: e.tensor_copy(out=out_ap, in_=in_ap), r=[kf_], w=[kb_])
            else:
                sc.pool(lambda e: e.tensor_copy(out=out_ap, in_=in_ap), r=[kf_], w=[kb_])
            sc.dma(lambda e: e.dma_start(out=dst_ap, in_=st_out), r=[kb_], pw=[dkey], q=q)
        if q == 'sp':
            fin()
        else:
            if self.pending_fin is not None:
                self.pending_fin()
            self.pending_fin = fin

    def cast_item(self, src_ap, n_in, in_view, out_view, dst_ap, dkey):
        sc = self.sc

        def item():
            i = self.cast_ctr % (self.NST_EARLY if self.cast_items_done < self.n_early else self.NST)
            self.cast_ctr += 1
            stf, stb = self.STF[i], self.STB[i]
            self._cast_emit(in_view(stf[:, 0:n_in]), out_view(stb[:, 0:n_in]), src_ap, dst_ap, in_view(stf[:, 0:n_in]),
                            out_view(stb[:, 0:n_in]), dkey, i)
        self.castq.append(item)

    def cast_cols(self, src2d, nk, col0, ncols, group, dst, u0, dkey):
        u = ncols // group
        src_ap = src2d.rearrange("(k p) n -> p k n", p=128)[:, :, col0:col0 + ncols]
        n_in = nk * ncols

        def in_view(a):
            return a.rearrange("p (k n) -> p k n", k=nk)

        def out_view(a):
            return a.rearrange("p (u k g) -> p k (u g)", u=u, k=nk) if u == 1 else \
                a.rearrange("p (u k g) -> p k u g", u=u, k=nk)

        def in_view2(a):
            v = a.rearrange("p (k n) -> p k n", k=nk)
            return v if u == 1 else a.rearrange("p (k u g) -> p k u g", k=nk, u=u)
        dst_ap = dst[u0:u0 + u].rearrange("u p f -> p u f")
        sc = self.sc

        def item():
            i = self.cast_ctr % (self.NST_EARLY if self.cast_items_done < self.n_early else self.NST)
            self.cast_ctr += 1
            stf, stb = self.STF[i], self.STB[i]
            self._cast_emit(in_view2(stf[:, 0:n_in]), out_view(stb[:, 0:n_in]), src_ap, dst_ap, in_view(stf[:, 0:n_in]),
                            stb[:, 0:n_in].rearrange("p (u f) -> p u f", u=u), dkey, i)
        self.castq.append(item)

    def queue_casts(self, l):
        W = self.W
        for f, (n13, n2) in enumerate((("ffn1_w13", "ffn1_w2"), ("ffn2_w13", "ffn2_w2"))):
            if f == 1:
                self.queue_mixer_casts(l)
            for c in range(NFF):
                key = ('w13b', l, f)
                self.cast_cols(W[n13][l], 8, c * 128, 128, 128, self.w13b[l][f], 2 * c, key)
                self.cast_cols(W[n13][l], 8, DFF + c * 128, 128, 128, self.w13b[l][f], 2 * c + 1, key)
            for c in range(0, NFF, 2):
                src = W[n2][l][c * 128:(c + 2) * 128, :].rearrange("(c p) n -> p c n", p=128)
                dst = self.w2b[l][f][c:c + 2].rearrange("c p n -> p c n")
                self.cast_item(src, 2048, lambda a: a.rearrange("p (c n) -> p c n", c=2),
                               lambda a: a.rearrange("p (c n) -> p c n", c=2), dst, ('w2b', l, f))

    def queue_mixer_casts(self, l):
        W = self.W
        win = W["w_in"][l]
        kF = ('winF', l)
        for i, c0 in enumerate((0, 128, 256, 384, 512, 640)):
            self.cast_cols(win, 8, c0, 128, 128, self.winF[l], i, kF)
        self.cast_cols(win, 8, 1280, 8, 8, self.winIF[l], 0, kF)
        for i, c0 in enumerate((768, 1288, 1800, 2312)):
            for hh in range(2):
                self.cast_T_half(win, c0 + hh * 256, self.winT[l], i, hh, ('winT', l))
        for i in range(24):
            self.cast_cols(win, 8, 2824 + i * 128, 128, 128, self.winF[l], 6 + i, kF)
        for name, nk, dst in (("p_a", 2, self.pab[l]), ("p_b", 2, self.pbb[l]), ("p_c", 4, self.pcb[l])):
            per = 2048 // (nk * 128)
            for c in range(0, 8, per):
                self.cast_cols(W[name][l], nk, c * 128, per * 128, 128, dst, c, ('pabc', l))
        for hh in range(2):
            for q in range(2):
                self.cast_T_half(W["w_out"][l], hh * 512 + q * 256, self.woutb[l], hh, q, ('woutb', l))

    def cast_T_half(self, src2d, col0, dst, blk, hh, dkey):
        src_ap = src2d.rearrange("(k p) n -> p k n", p=128)[:, :, col0:col0 + 256]
        dst_ap = dst[blk].rearrange("p (k n) -> p k n", k=8)[:, :, hh * 256:(hh + 1) * 256]
        v = lambda a: a.rearrange("p (k n) -> p k n", k=8)
        self.cast_item_3(src_ap, dst_ap, v, dkey)

    def cast_item_3(self, src_ap, dst_ap, v, dkey):
        sc = self.sc

        def item():
            i = self.cast_ctr % (self.NST_EARLY if self.cast_items_done < self.n_early else self.NST)
            self.cast_ctr += 1
            stf, stb = self.STF[i], self.STB[i]
            self._cast_emit(stf[:, 0:2048], stb[:, 0:2048], src_ap, dst_ap, v(stf[:, 0:2048]), v(stb[:, 0:2048]), dkey, i)
        self.castq.append(item)

    def drain(self, n=None):
        k = len(self.castq) if n is None else min(n, len(self.castq))
        for _ in range(k):
            self.castq.pop(0)()
            self.cast_recorded += 1

    def need_casts(self, total):
        if self.cast_recorded < total:
            self.drain(total - self.cast_recorded)
        if self.pending_fin is not None:
            self.pending_fin()
            self.pending_fin = None

    def load_gain(self, row_ap):
        i = self.gbc_ctr % 2
        self.gbc_ctr += 1
        G = self.GBC[i]
        self.sc.dma(lambda e: e.dma_start(out=G[:], in_=row_ap.partition_broadcast(128)), w=[('GBC', i)])
        return G, ('GBC', i)

    def xrows(self, t, g):
        return t.ap().rearrange("(n p) d -> p n d", p=128)[:, 4 * g:4 * g + 4, :]

    def frontend(self, src_t, src_key, g, G, gkey, ev=0):
        sc = self.sc
        slot = self.xg_ctr % 2
        self.xg_ctr += 1
        XG = self.XG[slot]
        kx = ('XG', slot)
        src = self.xrows(src_t, g)
        sc.dma(lambda e: e.dma_start(out=XG[:], in_=src), r=[(src_key, g)], w=[kx])
        self.norm_T(XG, kx, G, gkey)
        return XG, kx

    def norm_T(self, XG, kx, G, gkey):
        self.norm_a(XG, kx, G, gkey)
        return self.norm_b()

    def norm_a(self, XG, kx, G, gkey):
        sc = self.sc
        SSQ, RSTD = self.SSQ, self.RSTD
        for j in range(4):
            HN = self.HN[j]
            sc.act(lambda e, j=j, HN=HN: e.activation(out=HN[:], in_=XG[:, j, :], func=AF.Square, accum_out=SSQ[:, j:j + 1]),
                   r=[kx], w=[('SSQ', j), ('HN', j)])
        sc.act(lambda e: e.activation(out=RSTD[:], in_=SSQ[:], func=AF.Sqrt, scale=1.0 / D, bias=EPS),
               r=[('SSQ', j) for j in range(4)], w=['RSTD'])
        sc.dve(lambda e: e.reciprocal(out=RSTD[:], in_=RSTD[:]), r=['RSTD'], w=['RSTD'])
        for j in range(4):
            HN = self.HN[j]
            sc.dve(lambda e, j=j, HN=HN: e.scalar_tensor_tensor(out=HN[:], in0=XG[:, j, :], scalar=RSTD[:, j:j + 1], in1=G[:],
                                                               op0=ALU.mult, op1=ALU.mult), r=[kx, 'RSTD', gkey], w=[('HN', j)])

    def norm_b(self):
        sc = self.sc
        HT, IDB, PS = self.HT, self.IDB, self.PS
        for j in range(4):
            HN = self.HN[j]
            kh = ('HN', j)
            pT = PS[j][:, 0:512].bitcast(BF16).rearrange("p (k t) -> p k t", k=8)
            for k in range(8):
                sc.pe(lambda e, k=k, HN=HN, pT=pT: e.transpose(out=pT[:, k, :], in_=HN[:, k * 128:(k + 1) * 128], identity=IDB[:]),
                      r=[kh, 'IDB'], w=[('ps', j)])
            if j % 2 == 0:
                sc.act(lambda e, j=j, pT=pT: e.copy(out=HT[:, :, j * 128:(j + 1) * 128], in_=pT), r=[('ps', j)], w=[('HT', j)])
            else:
                sc.dve(lambda e, j=j, pT=pT: e.tensor_copy(out=HT[:, :, j * 128:(j + 1) * 128], in_=pT), r=[('ps', j)], w=[('HT', j)])
        return [('HT', j) for j in range(4)]

    def front_a(self, src_t, src_key, g, G, gkey):
        sc = self.sc
        slot = self.xg_ctr % 2
        self.xg_ctr += 1
        XG = self.XG[slot]
        kx = ('XG', slot)
        src = self.xrows(src_t, g)
        sc.dma(lambda e: e.dma_start(out=XG[:], in_=src), r=[(src_key, g)], w=[kx])
        self.norm_a(XG, kx, G, gkey)
        return XG, kx

    def ffn_phase(self, l, f, src_t, src_key, final=False, lazy_base=None):
        sc, NG, PS, HT = self.sc, self.NG, self.PS, self.HT
        ar = self.arena
        W = self.W
        nname = "ffn1_norm" if f == 0 else "ffn2_norm"
        G, gkey = self.load_gain(W[nname][l:l + 1, :])
        if final:
            FG, fgkey = self.load_gain(W["final_norm"][0:1, :])
        actT = ar.alloc([128, NFF, 512], BF16)
        NR13, NR2 = 6, 4
        w13r = [ar.alloc([128, 2, 1024], BF16) for _ in range(NR13)]
        sg = [ar.alloc([128, 512], F32) for _ in range(2)]
        w13b, w2b = self.w13b[l][f], self.w2b[l][f]
        resident_w2 = lazy_base is None
        if resident_w2:
            w2res = ar.alloc([128, NFF, 1024], BF16)
            for part in range(2):
                sc.dma(lambda e, part=part: e.dma_start(out=w2res[:, part * 11:(part + 1) * 11, :],
                                                        in_=w2b[part * 11:(part + 1) * 11].rearrange("c p n -> p c n")),
                       r=[('w2b', l, f)], w=[('A', 'w2res', part)])
        else:
            w2r = [ar.alloc([128, 2, 512], BF16) for _ in range(NR2)]
        hkeys = [('HT', j) for j in range(4)]
        c13 = 0
        c2 = 0
        items_per_group = min(7, (len(self.castq) + NG - 1) // NG) if self.castq else 0
        nxt = self.front_a(src_t, src_key, 0, G, gkey)
        self.norm_b()
        for g in range(NG):
            XG, kx = nxt
            for c in range(NFF):
                s = c13 % NR13
                c13 += 1
                wt = w13r[s]
                kw = ('A', 'w13', s)
                if lazy_base is not None and g == 0:
                    self.need_casts(lazy_base + 2 * (c + 1))
                sc.dma(lambda e, c=c, wt=wt: e.dma_start(out=wt, in_=w13b[2 * c:2 * c + 2].rearrange("u p f -> p u f")),
                       r=[('w13b', l, f)], w=[kw])
                bg, bu = (0, 1) if c % 2 == 0 else (2, 3)
                for half, bank in ((0, bg), (1, bu)):
                    for k in range(8):
                        sc.pe(lambda e, k=k, wt=wt, half=half, bank=bank: e.matmul(
                            PS[bank][:, :], lhsT=wt[:, half, k * 128:(k + 1) * 128], rhs=HT[:, k, :], start=(k == 0), stop=(k == 7)),
                            r=[kw] + hkeys, w=[('ps', bank)])
                sgb = sg[c % 2]
                ks = ('A', 'sg', c % 2)
                sc.act(lambda e, bg=bg, sgb=sgb: e.activation(out=sgb, in_=PS[bg][:, :], func=AF.Silu), r=[('ps', bg)], w=[ks])
                sc.dve(lambda e, c=c, bu=bu, sgb=sgb: e.tensor_tensor(out=actT[:, c, :], in0=PS[bu][:, :], in1=sgb, op=ALU.mult),
                       r=[('ps', bu), ks], w=[('A', 'actT', c)])
            if g + 1 < NG:
                nxt = self.front_a(src_t, src_key, g + 1, G, gkey)
            for half in range(2):
                for c in range(0, NFF, 2):
                    if resident_w2:
                        for cc in range(2):
                            for j in range(4):
                                sc.pe(lambda e, c=c, cc=cc, j=j, half=half: e.matmul(
                                    PS[4 + j][:, :], lhsT=actT[:, c + cc, j * 128:(j + 1) * 128],
                                    rhs=w2res[:, c + cc, half * 512:(half + 1) * 512],
                                    start=(c + cc == 0), stop=(c + cc == NFF - 1)),
                                    r=[('A', 'w2res', (c + cc) // 11), ('A', 'actT', c + cc)], w=[('ps', 4 + j)])
                        continue
                    s = c2 % NR2
                    c2 += 1
                    wt = w2r[s]
                    kw = ('A', 'w2', s)
                    if lazy_base is not None and g == 0:
                        self.need_casts(lazy_base + 2 * NFF + c // 2 + 1)
                    sc.dma(lambda e, c=c, wt=wt, half=half: e.dma_start(
                        out=wt, in_=w2b[c:c + 2].rearrange("c p n -> p c n")[:, :, half * 512:(half + 1) * 512]),
                        r=[('w2b', l, f)], w=[kw])
                    for cc in range(2):
                        for j in range(4):
                            sc.pe(lambda e, c=c, cc=cc, j=j, wt=wt: e.matmul(
                                PS[4 + j][:, :], lhsT=actT[:, c + cc, j * 128:(j + 1) * 128], rhs=wt[:, cc, :],
                                start=(c + cc == 0), stop=(c + cc == NFF - 1)),
                                r=[kw, ('A', 'actT', c + cc)], w=[('ps', 4 + j)])
                for j in range(4):
                    sc.dve(lambda e, j=j, half=half, XG=XG: e.scalar_tensor_tensor(
                        out=XG[:, j, half * 512:(half + 1) * 512], in0=PS[4 + j][:, :], scalar=0.5,
                        in1=XG[:, j, half * 512:(half + 1) * 512], op0=ALU.mult, op1=ALU.add),
                        r=[('ps', 4 + j), kx], w=[kx])
            if final:
                self.final_norm(XG, kx, FG, fgkey, g, sg)
            else:
                dst = self.xrows(self.xs, g)
                sc.dma(lambda e, XG=XG, dst=dst: e.dma_start(out=dst, in_=XG[:]), r=[kx], w=[('xs', g)])
            if g + 1 < NG:
                self.norm_b()
            self.drain(items_per_group)
        self.barrier()

    def final_norm(self, XG, kx, FG, fgkey, g, sg):
        sc = self.sc
        SSQ, RSTD = self.SSQ2, self.RSTD2
        for j in range(4):
            jk = sg[j % 2].bitcast(BF16)
            sc.act(lambda e, j=j, jk=jk: e.activation(out=jk, in_=XG[:, j, :], func=AF.Square, accum_out=SSQ[:, j:j + 1]),
                   r=[kx], w=[('SSQ2', j), ('A', 'sg', j % 2)])
        sc.act(lambda e: e.activation(out=RSTD[:], in_=SSQ[:], func=AF.Sqrt, scale=1.0 / D, bias=EPS),
               r=[('SSQ2', j) for j in range(4)], w=['RSTD2'])
        sc.dve(lambda e: e.reciprocal(out=RSTD[:], in_=RSTD[:]), r=['RSTD2'], w=['RSTD2'])
        for j in range(4):
            sc.dve(lambda e, j=j: e.scalar_tensor_tensor(out=XG[:, j, :], in0=XG[:, j, :], scalar=RSTD[:, j:j + 1], in1=FG[:],
                                                        op0=ALU.mult, op1=ALU.mult), r=[kx, 'RSTD2', fgkey], w=[kx])
        dst = self.xrows(self.out_t, g)
        sc.dma(lambda e, dst=dst: e.dma_start(out=dst, in_=XG[:]), r=[kx], w=[('out', g)])


    def layer_params(self, l):
        sc, W = self.sc, self.W
        PWF, PWB, PSC, CWc, CB, GB8, MNBC, DNBC, LAM, LAMC = (self.PWF, self.PWB, self.PSC, self.CWc, self.CB, self.GB8,
                                                              self.MNBC, self.DNBC, self.LAM, self.LAMC)
        lam_init = 0.8 - 0.6 * math.exp(-0.3 * l)
        self.lam_init = lam_init
        for i in range(2):
            sc.dma(lambda e, i=i: e.dma_start(out=PWF[0:64, i, 0:64], in_=W["pool_w"][l, 2 * i]), pw=['PWF'])
            sc.dma(lambda e, i=i: e.dma_start(out=PWF[64:128, i, 64:128], in_=W["pool_w"][l, 2 * i + 1]), pw=['PWF'])
        sc.dve(lambda e: e.tensor_copy(out=PWB[:], in_=PWF[:]), r=['PWF'], w=['PWB'])
        sc.dma(lambda e: e.dma_start(out=PSC[:], in_=W["pool_scale"][l].rearrange("(i p) -> p i", p=128),
                                     allow_slow_non_contiguous=True), w=['PSC'])
        for tap in range(4):
            sc.dma(lambda e, tap=tap: e.dma_start(out=CWc[:, :, tap], in_=W["m_conv_w"][l, tap].rearrange("(c p) -> p c", p=128),
                                                  allow_slow_non_contiguous=True), pw=['CWc'])
        sc.dma(lambda e: e.dma_start(out=CB[:], in_=W["m_conv_b"][l].rearrange("(c p) -> p c", p=128),
                                     allow_slow_non_contiguous=True), w=['CB'])
        sc.dma(lambda e: e.dma_start(out=GB8[:], in_=W["m_gate_b"][l].rearrange("(p o) -> p o", o=1)), w=['GB8'])
        sc.dma(lambda e: e.dma_start(out=MNBC[:], in_=W["m_norm"][l:l + 1, :].partition_broadcast(128)), w=['MNBC'])
        sc.dma(lambda e: e.dma_start(out=DNBC[:], in_=W["d_norm"][l:l + 1, :].partition_broadcast(128)), w=['DNBC'])
        sc.dve(lambda e: e.tensor_scalar(out=DNBC[:], in0=DNBC[:], scalar1=float(1.0 - lam_init), scalar2=None, op0=ALU.mult),
               r=['DNBC'], w=['DNBC'])
        DNCOL = self.DNCOL
        sc.dma(lambda e: e.dma_start(out=DNCOL[:], in_=W["d_norm"][l].rearrange("(p o) -> p o", o=1)), w=['DNCOL'])
        sc.dve(lambda e: e.tensor_scalar(out=DNCOL[:], in0=DNCOL[:], scalar1=float(1.0 - lam_init), scalar2=None, op0=ALU.mult),
               r=['DNCOL'], w=['DNCOL'])
        sc.dma(lambda e: e.dma_start(out=LAM[:], in_=W["d_lambda"][l:l + 1].rearrange("o a b -> o (a b)").partition_broadcast(128)),
               w=['LAM'])
        jf = self.HN[0][:, 0:128].bitcast(F32)
        sc.dve(lambda e: e.tensor_tensor(out=jf[:, 0:64], in0=LAM[:, 0:64], in1=LAM[:, 64:128], op=ALU.mult), r=['LAM'], w=['jf', ('HN', 0)])
        sc.dve(lambda e: e.tensor_reduce(out=LAMC[:, 0:1], in_=jf[:, 0:64], axis=AX.X, op=ALU.add), r=['jf'], w=['LAMC'])
        sc.dve(lambda e: e.tensor_tensor(out=jf[:, 0:64], in0=LAM[:, 128:192], in1=LAM[:, 192:256], op=ALU.mult), r=['LAM', 'LAMC'], w=['jf', ('HN', 0)])
        sc.dve(lambda e: e.tensor_reduce(out=LAMC[:, 1:2], in_=jf[:, 0:64], axis=AX.X, op=ALU.add), r=['jf'], w=['LAMC'])
        sc.act(lambda e: e.activation(out=LAMC[:, 4:6], in_=LAMC[:, 0:2], func=AF.Exp), r=['LAMC'], w=['LAMC'])
        sc.dve(lambda e: e.tensor_tensor(out=LAMC[:, 2:3], in0=LAMC[:, 4:5], in1=LAMC[:, 5:6], op=ALU.subtract), r=['LAMC'], w=['LAMC'])
        sc.dve(lambda e: e.tensor_scalar(out=LAMC[:, 2:3], in0=LAMC[:, 2:3], scalar1=float(lam_init), scalar2=None, op0=ALU.add),
               r=['LAMC'], w=['LAMC'])
        sc.dve(lambda e: e.tensor_scalar(out=LAMC[:, 3:4], in0=LAMC[:, 2:3], scalar1=-1.0, scalar2=None, op0=ALU.mult),
               r=['LAMC'], w=['LAMC'])

    def phaseB(self, l, src_t=None, src_key='xs'):
        src_t = self.xs if src_t is None else src_t
        sc, NG, NT, PS, HT, CST = self.sc, self.NG, self.NT, self.PS, self.HT, self.CST
        ar, W = self.arena, self.W
        COS, SIN, IDB = self.COS, self.SIN, self.IDB
        G, gkey = self.load_gain(W["mix_norm"][l:l + 1, :])
        wF = ar.alloc([128, 6, 1024], BF16)
        wIF = ar.alloc([128, 64], BF16)
        winF, winIF, winT = self.winF[l], self.winIF[l], self.winT[l]
        sc.dma(lambda e: e.dma_start(out=wF, in_=winF[0:6].rearrange("u p f -> p u f")), r=[('winF', l)], w=[('A', 'wF')])
        sc.dma(lambda e: e.dma_start(out=wIF, in_=winIF[0]), r=[('winF', l)], w=[('A', 'wIF')])
        wTr = [ar.alloc([128, 8, 512], BF16) for _ in range(2)]
        zF = [ar.alloc([128, 528], F32) for _ in range(6)]
        sA = ar.alloc([128, 528], F32)
        sB = ar.alloc([128, 528], F32)
        t16 = ar.alloc([128, 16], F32)
        pooled = ar.alloc([128, 512], BF16)
        cacc = [ar.alloc([128, 512], F32) for _ in range(2)]
        yaS = ar.alloc([128, 2, 512], BF16)
        qkS = ar.alloc([128, 4, 512], BF16)
        kTokS = ar.alloc([128, 4, 256], BF16)
        vvS = ar.alloc([128, 4, 260], BF16)
        soS = ar.alloc([128, 4, 256], F32)
        gz = ar.alloc([8, 512], F32)
        ee = ar.alloc([8, 512], F32)
        cs = ar.alloc([8, 512], F32)
        TT = ar.alloc([128, 32], F32)
        zq = [ar.alloc([128, 512], F32) for _ in range(2)]
        ta = ar.alloc([128, 256], F32)
        tb = ar.alloc([128, 256], F32)
        tc = ar.alloc([128, 256], F32)
        td = ar.alloc([128, 256], F32)
        rq = [ar.alloc([128, 512], BF16) for _ in range(2)]
        dqS = ar.alloc([128, 4, 512], BF16)
        dkZS = ar.alloc([128, 8, 512], BF16)
        mkZS = ar.alloc([128, 4, 512], BF16)
        dvS = ar.alloc([128, 4, 520], BF16)
        for i in range(6):
            sc.dve(lambda e, i=i: e.memset(zF[i][:, 0:16], 0.0), w=[('A', 'zF', i)])
        sc.dve(lambda e: e.memset(vvS, 1.0), w=[('A', 'vvS')])
        sc.dve(lambda e: e.memset(dvS, 1.0), w=[('A', 'dvS')])
        sc.dve(lambda e: e.memset(dkZS, 0.0), w=[('A', 'dkZS')])
        sc.dve(lambda e: e.memset(mkZS, 0.0), w=[('A', 'mkZS')])
        COLA, NCE, WKc, AA, GB8, ONES8 = self.COLA, self.NCE, self.WKc, self.AA, self.GB8, self.ONES8
        PWB, PSC, CWc, CB = self.PWB, self.PSC, self.CWc, self.CB
        hkeys = [('HT', j) for j in range(4)]
        ctrT = 0
        rctr = 0
        items_per_group = min(7, (len(self.castq) + NG - 1) // NG) if self.castq else 0
        nxt = self.front_a(src_t, src_key, 0, G, gkey)
        self.norm_b()
        for g in range(NG):
            XG, kx = nxt
            for ci in range(6):
                bank = 4 + (ci % 4)
                for k in range(8):
                    sc.pe(lambda e, ci=ci, k=k, bank=bank: e.matmul(PS[bank][:, :], lhsT=wF[:, ci, k * 128:(k + 1) * 128], rhs=HT[:, k, :],
                                                                      start=(k == 0), stop=(k == 7)),
                          r=[('A', 'wF')] + hkeys, w=[('ps', bank)])
                sc.act(lambda e, ci=ci, bank=bank: e.copy(out=zF[ci][:, 16:528], in_=PS[bank][:, :]), r=[('ps', bank)], w=[('A', 'zF', ci)])
            for k in range(8):
                sc.pe(lambda e, k=k: e.matmul(PS[6][0:8, :], lhsT=wIF[:, k * 8:(k + 1) * 8], rhs=HT[:, k, :], start=(k == 0), stop=(k == 7)),
                      r=[('A', 'wIF')] + hkeys, w=[('ps', 6)])
            sc.dve(lambda e: e.tensor_scalar(out=gz, in0=PS[6][0:8, :], scalar1=GB8[:, 0:1], scalar2=None, op0=ALU.add),
                   r=[('ps', 6), 'GB8'], w=[('A', 'gz')])
            sc.act(lambda e: e.activation(out=ee, in_=gz, func=AF.Exp, scale=-1.0), r=[('A', 'gz')], w=[('A', 'ee')])
            sc.act(lambda e: e.activation(out=ee, in_=ee, func=AF.Ln, bias=1.0), r=[('A', 'ee')], w=[('A', 'ee')])
            for j in range(4):
                sc.dve(lambda e, j=j: e.tensor_tensor_scan(out=cs[:, j * 128:(j + 1) * 128], data0=ONES8[:, :], data1=ee[:, j * 128:(j + 1) * 128],
                                                           initial=0.0, op0=ALU.mult, op1=ALU.add),
                       r=[('A', 'ee'), 'ONES8'], w=[('A', 'cs')])
            sc.dve(lambda e: e.tensor_copy(out=cs[0:4, :], in_=gz[0:4, :]), r=[('A', 'gz'), ('A', 'cs')], w=[('A', 'cs')])
            cs8 = self.cs8
            sc.dma(lambda e, g=g: e.dma_start(out=cs8[:, g * 512:(g + 1) * 512], in_=cs), r=[('A', 'cs')], w=[('cs8', g)])
            pend_tr = [None]
            for blk in range(4):
                s = ctrT % 2
                ctrT += 1
                wt = wTr[s]
                kw = ('A', 'wT', s)
                sc.dma(lambda e, blk=blk, wt=wt: e.dma_start(out=wt, in_=winT[blk].rearrange("p (k n) -> p k n", k=8)),
                       r=[('winT', l)], w=[kw])
                for j in range(4):
                    bank = j
                    tile = 4 * g + j
                    for k in range(8):
                        sc.pe(lambda e, k=k, j=j, wt=wt, bank=bank: e.matmul(PS[bank][:, :], lhsT=HT[:, k, j * 128:(j + 1) * 128], rhs=wt[:, k, :],
                                                                             start=(k == 0), stop=(k == 7)),
                              r=[kw, ('HT', j)], w=[('ps', bank)])
                    if blk == 0:
                        sc.act(lambda e, j=j, bank=bank: e.copy(out=vvS[:, j, :].rearrange("p (h d) -> p h d", h=4)[:, :, 0:64],
                                                                in_=PS[bank][:, 0:256].rearrange("p (h d) -> p h d", h=4)),
                               r=[('ps', bank)], w=[('A', 'vvS')])
                        sc.act(lambda e, j=j, bank=bank: e.activation(out=soS[:, j, :], in_=PS[bank][:, 256:512], func=AF.Sigmoid),
                               r=[('ps', bank)], w=[('A', 'soS')])
                    elif blk == 3:
                        sc.act(lambda e, j=j, bank=bank: e.copy(out=dvS[:, j, :].rearrange("p (h d) -> p h d", h=4)[:, :, 0:128],
                                                                in_=PS[bank][:, :].rearrange("p (h d) -> p h d", h=4)),
                               r=[('ps', bank)], w=[('A', 'dvS')])
                    else:
                        ri = rctr % 2
                        rctr += 1
                        zz = zq[ri]
                        kzq = ('A', 'zq', ri)
                        sc.act(lambda e, zz=zz, bank=bank: e.copy(out=zz, in_=PS[bank][:, :]), r=[('ps', bank)], w=[kzq])
                        zv = zz.rearrange("p (h a d) -> p h a d", h=8, a=2)
                        x1, x2 = zv[:, :, 0, :], zv[:, :, 1, :]
                        cosb = bc_mid(COS[:, tile * 32:(tile + 1) * 32], 8)
                        sinb = bc_mid(SIN[:, tile * 32:(tile + 1) * 32], 8)
                        r_ = rq[ri]
                        krq = ('A', 'rq', ri)
                        rv = r_.rearrange("p (h a d) -> p h a d", h=8, a=2)
                        v3 = lambda a: a.rearrange("p (h d) -> p h d", h=8)
                        sc.dve(lambda e, x1=x1, cosb=cosb: e.tensor_tensor(out=v3(ta), in0=x1, in1=cosb, op=ALU.mult), r=[kzq, 'COS'], w=[('A', 'ta')])
                        sc.dve(lambda e, x2=x2, sinb=sinb: e.tensor_tensor(out=v3(tb), in0=x2, in1=sinb, op=ALU.mult), r=[kzq, 'SIN'], w=[('A', 'tb')])
                        sc.dve(lambda e, rv=rv: e.tensor_tensor(out=rv[:, :, 0, :], in0=v3(ta), in1=v3(tb), op=ALU.subtract),
                               r=[('A', 'ta'), ('A', 'tb')], w=[(krq[0], krq[1], krq[2], 0)])
                        sc.pool(lambda e, x2=x2, cosb=cosb: e.tensor_tensor(out=v3(tc), in0=x2, in1=cosb, op=ALU.mult), r=[kzq, 'COS'], w=[('A', 'tc')])
                        sc.pool(lambda e, x1=x1, sinb=sinb: e.tensor_tensor(out=v3(td), in0=x1, in1=sinb, op=ALU.mult), r=[kzq, 'SIN'], w=[('A', 'td')])
                        sc.pool(lambda e, rv=rv: e.tensor_tensor(out=rv[:, :, 1, :], in0=v3(tc), in1=v3(td), op=ALU.add),
                                r=[('A', 'tc'), ('A', 'td')], w=[(krq[0], krq[1], krq[2], 1)])
                        tb_ = 4 + (rctr % 4)

                        def tr_(tb_=tb_, r_=r_, krq=krq, blk=blk, j=j):
                            pq = PS[tb_][:, 0:256].bitcast(BF16).rearrange("p (c n) -> p c n", c=4)
                            for pr in range(4):
                                sc.pe(lambda e, pr=pr: e.transpose(out=pq[:, pr, :], in_=r_[:, pr * 128:(pr + 1) * 128], identity=IDB[:]),
                                      r=[(krq[0], krq[1], krq[2], 0), (krq[0], krq[1], krq[2], 1), 'IDB'], w=[('ps', tb_)])
                            if blk == 1:
                                sc.dve(lambda e: e.tensor_copy(out=dqS[:, :, j * 128:(j + 1) * 128], in_=pq), r=[('ps', tb_)], w=[('A', 'dqS')])
                            else:
                                sc.dve(lambda e: e.tensor_copy(out=dkZS[0:64, 0::2, j * 128:(j + 1) * 128], in_=pq[0:64, :, :]),
                                       r=[('ps', tb_)], w=[('A', 'dkZS')])
                                sc.dve(lambda e: e.tensor_copy(out=dkZS[64:128, 1::2, j * 128:(j + 1) * 128], in_=pq[64:128, :, :]),
                                       r=[('ps', tb_), ('A', 'dkZS')], w=[('A', 'dkZS')])
                        if pend_tr[0] is not None:
                            pend_tr[0]()
                        pend_tr[0] = tr_
            if pend_tr[0] is not None:
                pend_tr[0]()
                pend_tr[0] = None
            vv, so, dqT, dkZ, dvv = self.vv, self.so, self.dqT, self.dkZ, self.dvv
            sc.dma(lambda e, g=g: e.dma_start(out=vv[4 * g:4 * g + 4].rearrange("n p f -> p n f"), in_=vvS), r=[('A', 'vvS')], w=[('vv', g)])
            sc.dma(lambda e, g=g: e.dma_start(out=so[4 * g:4 * g + 4].rearrange("n p f -> p n f"), in_=soS), r=[('A', 'soS')], w=[('so', g)])
            sc.dma(lambda e, g=g: e.dma_start(out=dqT.ap().rearrange("c p s -> p c s")[:, :, g * 512:(g + 1) * 512], in_=dqS),
                   r=[('A', 'dqS')], w=[('dqT', g)])
            sc.dma(lambda e, g=g: e.dma_start(out=dkZ.ap().rearrange("c p s -> p c s")[:, :, g * 512:(g + 1) * 512], in_=dkZS),
                   r=[('A', 'dkZS')], w=[('dkZ', g)])
            sc.dma(lambda e, g=g: e.dma_start(out=dvv[4 * g:4 * g + 4].rearrange("n p f -> p n f"), in_=dvS), r=[('A', 'dvS')], w=[('dvv', g)])
            for j in range(4):
                sc.pe(lambda e, j=j: e.transpose(out=PS[7][:, j * 8:(j + 1) * 8], in_=cs[0:8, j * 128:(j + 1) * 128],
                                                 identity=CST[0:8, C_ID:C_ID + 8]), r=[('A', 'cs'), 'CST'], w=[('ps', 7)])
            for h in range(4):
                sc.pe(lambda e, h=h: e.matmul(PS[7][:, 64 + h * 4:64 + h * 4 + 4], lhsT=CST[0:8, C_SEL + h * 128:C_SEL + (h + 1) * 128],
                                              rhs=cs[0:8, 127::128], start=True, stop=True), r=[('A', 'cs'), 'CST'], w=[('ps', 7)])
            sc.act(lambda e: e.copy(out=TT, in_=PS[7][:, 0:32]), r=[('ps', 7)], w=[('A', 'TT')])
            sc.dve(lambda e, g=g: e.tensor_scalar(out=NCE[:, :, 4 * g:4 * g + 4], in0=PS[7][:, 64:80].rearrange("p (h j) -> p h j", h=4),
                                                  scalar1=-1.0, scalar2=None, op0=ALU.mult), r=[('ps', 7)], w=[('NCE', g)])
            TTv = TT.rearrange("p (j c) -> p j c", j=4)
            sc.dve(lambda e, g=g: e.tensor_tensor(out=COLA[:, 4 * g:4 * g + 4, :], in0=TTv[:, :, 0:4], in1=TTv[:, :, 4:8], op=ALU.add),
                   r=[('A', 'TT')], w=[('COLA', g)])
            sc.dve(lambda e, g=g: e.tensor_tensor(out=WKc[:, 4 * g:4 * g + 4, :], in0=COLA[:, 4 * g:4 * g + 4, :],
                                                  in1=NCE[:, :, 4 * g:4 * g + 4].rearrange("p h j -> p j h"), op=ALU.add),
                   r=[('COLA', g), ('NCE', g)], w=[('WKc', g)])
            sc.act(lambda e, g=g: e.activation(out=WKc[:, 4 * g:4 * g + 4, :], in_=WKc[:, 4 * g:4 * g + 4, :], func=AF.Exp, bias=math.log(0.125)),
                   r=[('WKc', g)], w=[('WKc', g)])
            sc.act(lambda e, g=g: e.activation(out=AA[0:64, :, 4 * g:4 * g + 4], in_=NCE[0:64, 0::2, 4 * g:4 * g + 4], func=AF.Exp),
                   r=[('NCE', g)], w=[('AA', g, 0)])
            sc.act(lambda e, g=g: e.activation(out=AA[64:128, :, 4 * g:4 * g + 4], in_=NCE[64:128, 1::2, 4 * g:4 * g + 4], func=AF.Exp),
                   r=[('NCE', g)], w=[('AA', g, 1)])
            if g + 1 < NG:
                nxt = self.front_a(src_t, src_key, g + 1, G, gkey)
                self.norm_b()
            for ch in range(2):
                z = zF[ch]
                kz = ('A', 'zF', ch)
                kA, kB = ('A', 'sA'), ('A', 'sB')
                sc.pool(lambda e, z=z: e.tensor_tensor(out=sA[:, 1:528], in0=z[:, 1:528], in1=z[:, 0:527], op=ALU.add), r=[kz], w=[kA])
                if ch == 0:
                    sc.pool(lambda e: e.tensor_tensor(out=sB[64:128, 3:528], in0=sA[64:128, 3:528], in1=sA[64:128, 1:526], op=ALU.add),
                            r=[kA], w=[kB])
                else:
                    sc.pool(lambda e: e.tensor_tensor(out=sB[:, 3:528], in0=sA[:, 3:528], in1=sA[:, 1:526], op=ALU.add), r=[kA], w=[kB])
                    sc.pool(lambda e: e.tensor_tensor(out=sA[:, 7:528], in0=sB[:, 7:528], in1=sB[:, 3:524], op=ALU.add), r=[kB, kA], w=[kA])
                    sc.pool(lambda e: e.tensor_tensor(out=sB[64:128, 15:528], in0=sA[64:128, 15:528], in1=sA[64:128, 7:520], op=ALU.add),
                            r=[kA, kB], w=[kB])
                for half, srcb, ksrc in ((0, sA, kA), (1, sB, kB)):
                    gi = 2 * ch + half
                    w = POOL_WINDOWS[gi]
                    p0, p1 = half * 64, half * 64 + 64
                    c0 = 0
                    if g == 0:
                        sc.dve(lambda e, srcb=srcb, p0=p0, p1=p1, gi=gi: e.tensor_tensor(
                            out=t16[p0:p1, :], in0=srcb[p0:p1, 16:32], in1=CST[p0:p1, C_ICNT + gi * 16:C_ICNT + (gi + 1) * 16], op=ALU.mult),
                            r=[ksrc, 'CST'], w=[('A', 't16', half)])
                        sc.dve(lambda e, z=z, p0=p0, p1=p1: e.tensor_tensor(out=pooled[p0:p1, 0:16], in0=t16[p0:p1, :], in1=z[p0:p1, 16:32],
                                                                            op=ALU.subtract),
                               r=[('A', 't16', half), kz], w=[('A', 'pooled', half)])
                        c0 = 16
                    sc.dve(lambda e, srcb=srcb, z=z, p0=p0, p1=p1, c0=c0, w=w: e.scalar_tensor_tensor(
                        out=pooled[p0:p1, c0:512], in0=srcb[p0:p1, 16 + c0:528], scalar=float(1.0 / w), in1=z[p0:p1, 16 + c0:528],
                        op0=ALU.mult, op1=ALU.subtract), r=[ksrc, kz], w=[('A', 'pooled', half)])
                bank = 4 + ch
                sc.pe(lambda e, ch=ch, bank=bank: e.matmul(PS[bank][:, :], lhsT=PWB[:, ch, :], rhs=pooled, start=True, stop=True),
                      r=['PWB', ('A', 'pooled', 0), ('A', 'pooled', 1)], w=[('ps', bank)])
                sc.dve(lambda e, ch=ch, bank=bank: e.tensor_scalar(out=yaS[:, ch, :], in0=PS[bank][:, :], scalar1=PSC[:, ch:ch + 1], scalar2=None,
                                                                   op0=ALU.mult), r=[('ps', bank), 'PSC'], w=[('A', 'yaS', ch)])
                sc.pool(lambda e, z=z: e.tensor_copy(out=z[:, 0:16], in_=z[:, 512:528]), r=[kz], w=[kz])
            yaT = self.yaT
            sc.dma(lambda e, g=g: e.dma_start(out=yaT.ap().rearrange("c p s -> p c s")[:, :, g * 512:(g + 1) * 512], in_=yaS),
                   r=[('A', 'yaS', 0), ('A', 'yaS', 1)], w=[('yaT', g)])
            for cc in range(4):
                ci = 2 + cc
                z = zF[ci]
                kz = ('A', 'zF', ci)
                acc = cacc[cc % 2]
                ka = ('A', 'cacc', cc % 2)
                sc.dve(lambda e, z=z, acc=acc, cc=cc: e.tensor_scalar(out=acc, in0=z[:, 13:525], scalar1=CWc[:, cc, 0:1], scalar2=None, op0=ALU.mult),
                       r=[kz, 'CWc'], w=[ka])
                for tap in range(1, 4):
                    sc.dve(lambda e, z=z, acc=acc, cc=cc, tap=tap: e.scalar_tensor_tensor(
                        out=acc, in0=z[:, 13 + tap:525 + tap], scalar=CWc[:, cc, tap:tap + 1], in1=acc, op0=ALU.mult, op1=ALU.add),
                        r=[kz, 'CWc', ka], w=[ka])
                sc.act(lambda e, acc=acc, cc=cc: e.activation(out=qkS[:, cc, :], in_=acc, func=AF.Silu, bias=CB[:, cc:cc + 1]),
                       r=[ka, 'CB'], w=[('A', 'qkS', cc)])
                sc.pool(lambda e, z=z: e.tensor_copy(out=z[:, 0:16], in_=z[:, 512:528]), r=[kz], w=[kz])
            pk = PS[6][:, :].bitcast(BF16).rearrange("p (c j n) -> p c j n", c=2, j=4)
            for cc in range(2):
                for j in range(4):
                    sc.pe(lambda e, cc=cc, j=j: e.transpose(out=pk[:, cc, j, :], in_=qkS[:, 2 + cc, j * 128:(j + 1) * 128], identity=IDB[:]),
                          r=[('A', 'qkS', 2 + cc), 'IDB'], w=[('ps', 6)])
            sc.dve(lambda e: e.tensor_copy(out=kTokS.rearrange("p j (c n) -> p c j n", c=2), in_=pk), r=[('ps', 6)], w=[('A', 'kTokS')])
            mqT, mkZ, kTok = self.mqT, self.mkZ, self.kTok
            sc.act(lambda e: e.copy(out=mkZS[0:64, 0::2, :], in_=qkS[0:64, 2:4, :]), r=[('A', 'qkS', 2), ('A', 'qkS', 3)], w=[('A', 'mkZS')])
            sc.act(lambda e: e.copy(out=mkZS[64:128, 1::2, :], in_=qkS[64:128, 2:4, :]), r=[('A', 'qkS', 2), ('A', 'qkS', 3), ('A', 'mkZS')], w=[('A', 'mkZS')])
            sc.dma(lambda e, g=g: e.dma_start(out=mqT.ap().rearrange("c p s -> p c s")[:, :, g * 512:(g + 1) * 512], in_=qkS[:, 0:2, :]),
                   r=[('A', 'qkS', 0), ('A', 'qkS', 1)], w=[('mqT', g)])
            sc.dma(lambda e, g=g: e.dma_start(out=mkZ.ap().rearrange("c p s -> p c s")[:, :, g * 512:(g + 1) * 512], in_=mkZS),
                   r=[('A', 'mkZS')], w=[('mkZ', g)])
            sc.dma(lambda e, g=g: e.dma_start(out=kTok[4 * g:4 * g + 4].rearrange("n p f -> p n f"), in_=kTokS), r=[('A', 'kTokS')], w=[('kTok', g)])
            self.drain(items_per_group)
        self.barrier()


    def mlstm_phase(self, l):
        sc, NG, NT, PS, CST, IDB = self.sc, self.NG, self.NT, self.PS, self.CST, self.IDB
        MSKIP = os.environ.get("MSKIP", "")
        sc0 = sc

        class _SC:
            def __getattr__(s_, n):
                return getattr(sc0, n)

            def pe(s_, fn, r=(), w=(), tag=''):
                if tag and tag in MSKIP:
                    return None
                return sc0.pe(fn, r, w)
        sc = _SC()
        ar = self.arena
        COLA, WKc, AA, MNBC = self.COLA, self.WKc, self.AA, self.MNBC
        ld = []
        for i in range(2):
            ld.append(dict(qT=ar.alloc([128, 2, 512], BF16), kT=ar.alloc([128, 4, 512], BF16), kTok=ar.alloc([128, 4, 256], BF16),
                           vv=ar.alloc([128, 4, 260], BF16), so=ar.alloc([128, 4, 256], F32), cs=ar.alloc([8, 512], F32)))
        Dm = [ar.alloc([128, 4, 128], F32) for _ in range(2)]
        EB = [ar.alloc([128, 2, 128], F32) for _ in range(2)]
        PT = [ar.alloc([128, 4, 128], BF16) for _ in range(2)]
        qs = [ar.alloc([128, 4, 128], BF16) for _ in range(2)]
        kw = [ar.alloc([128, 4, 64], BF16) for _ in range(2)]
        Sf = ar.alloc([128, 2, 65], F32)
        Sb = ar.alloc([128, 2, 65], BF16)
        gso = ar.alloc([128, 256], F32)
        den = ar.alloc([128, 4], F32)
        rec = ar.alloc([128, 4], F32)
        hN = ar.alloc([128, 4, 64], F32)
        msum = ar.alloc([128, 4], F32)
        cen = ar.alloc([128, 4, 64], F32)
        sq = ar.alloc([128, 4, 64], F32)
        vs = ar.alloc([128, 4], F32)
        rstd = ar.alloc([128, 4], F32)
        ybf = ar.alloc([128, 256], F32)
        ybb = [ar.alloc([128, 256], BF16) for _ in range(2)]
        ybS = [ar.alloc([128, 2, 512], BF16) for _ in range(2)]
        sc.dve(lambda e: e.memset(Sf, 0.0), w=[('A', 'Sf')])
        for i_ in range(2):
            sc.dve(lambda e, i_=i_: e.memset(qs[i_], 0.0), w=[('A', 'qs', i_)])
        mqT, mkZ, kTok, vv, so, cs8, ybT = self.mqT, self.mkZ, self.kTok, self.vv, self.so, self.cs8, self.ybT

        def load_group(g):
            L = ld[g % 2]
            k = ('A', 'ld', g % 2)
            sl = slice(g * 512, (g + 1) * 512)
            sc.dma(lambda e: e.dma_start(out=L['qT'], in_=mqT.ap().rearrange("c p s -> p c s")[:, :, sl]), r=[('mqT', g)], w=[(k, 'qT')])
            sc.dma(lambda e: e.dma_start(out=L['kT'], in_=mkZ.ap().rearrange("c p s -> p c s")[:, :, sl]), r=[('mkZ', g)], w=[(k, 'kT')])
            sc.dma(lambda e: e.dma_start(out=L['kTok'], in_=kTok[4 * g:4 * g + 4].rearrange("n p f -> p n f")), r=[('kTok', g)], w=[(k, 'kTok')])
            sc.dma(lambda e: e.dma_start(out=L['vv'], in_=vv[4 * g:4 * g + 4].rearrange("n p f -> p n f")), r=[('vv', g)], w=[(k, 'vv')])
            sc.dma(lambda e: e.dma_start(out=L['so'], in_=so[4 * g:4 * g + 4].rearrange("n p f -> p n f")), r=[('so', g)], w=[(k, 'so')])
            sc.dma(lambda e: e.dma_start(out=L['cs'], in_=cs8[:, sl]), r=[('cs8', g)], w=[(k, 'cs')])

        def K(g, n):
            return (('A', 'ld', g % 2), n)

        def stage1(c):
            g, j, p = c // 4, c % 4, c % 2
            if j == 0:
                load_group(g)
            L = ld[g % 2]
            cols = slice(j * 128, (j + 1) * 128)
            for h in range(4):
                pair, hb = h // 2, (h % 2) * 64
                sc.pe(lambda e, h=h, pair=pair, hb=hb: e.matmul(PS[p][:, h * 128:(h + 1) * 128], lhsT=L['kT'][:, h, cols],
                                                               rhs=L['qT'][:, pair, cols], start=True, stop=True),
                      r=[K(g, 'kT'), K(g, 'qT')], w=[('ps', p)], tag='S')
            for h in range(4):
                sc.pe(lambda e, h=h: e.matmul(PS[2 + p][:, h * 128:(h + 1) * 128], lhsT=CST[0:8, C_SEL + h * 128:C_SEL + (h + 1) * 128],
                                              rhs=L['cs'][0:8, cols], start=True, stop=False), r=[K(g, 'cs'), 'CST'], w=[('ps', 2 + p)], tag='C')
                sc.pe(lambda e, h=h: e.matmul(PS[2 + p][:, h * 128:(h + 1) * 128], lhsT=CST[:, C_ID:C_ID + 128],
                                              rhs=CST[:, C_MP:C_MP + 128], start=False, stop=True), r=['CST'], w=[('ps', 2 + p)], tag='K')
            for h in range(4):
                sc.pe(lambda e, h=h: e.matmul(PS[4][:, h * 128:(h + 1) * 128], lhsT=CST[0:8, C_SEL + h * 128:C_SEL + (h + 1) * 128],
                                              rhs=L['cs'][0:8, cols], start=True, stop=True),
                      r=[K(g, 'cs'), 'CST'], w=[('ps', 4)], tag='E')
            for h in range(4):
                sc.act(lambda e, h=h: e.activation(out=Dm[p][:, h, :], in_=PS[2 + p][:, h * 128:(h + 1) * 128], func=AF.Exp,
                                                   scale=-1.0, bias=COLA[:, c, h:h + 1]),
                       r=[('ps', 2 + p), ('COLA', g)], w=[('A', 'Dm', p, h)])
            for half in range(2):
                sc.act(lambda e, half=half: e.activation(
                    out=EB[p][half * 64:(half + 1) * 64, :, :],
                    in_=PS[4][half * 64:(half + 1) * 64, :].rearrange("p (h n) -> p h n", h=4)[:, half::2, :], func=AF.Exp, scale=-1.0),
                    r=[('ps', 4)], w=[('A', 'EB', p, half)])
            sc.dve(lambda e: e.scalar_tensor_tensor(out=PT[p].rearrange("p h n -> p (h n)"), in0=PS[p][:, :], scalar=0.125,
                                                    in1=Dm[p].rearrange("p h n -> p (h n)"), op0=ALU.mult, op1=ALU.mult),
                   r=[('ps', p)] + [('A', 'Dm', p, h) for h in range(4)], w=[('A', 'PT', p)])
            for half in range(2):
                ps_, pe_ = half * 64, half * 64 + 64
                sc.dve(lambda e, half=half, ps_=ps_, pe_=pe_: e.tensor_tensor(out=qs[p][ps_:pe_, half::2, :], in0=L['qT'][ps_:pe_, :, cols],
                                                                              in1=EB[p][ps_:pe_, :, :], op=ALU.mult),
                       r=[K(g, 'qT'), ('A', 'EB', p, half), ('A', 'qs', p)], w=[('A', 'qs', p)])
            sc.dve(lambda e: e.tensor_tensor(out=kw[p], in0=L['kTok'][:, j, :].rearrange("p (h d) -> p h d", h=4),
                                             in1=bc_last(WKc[:, c, :], 64), op=ALU.mult),
                   r=[K(g, 'kTok'), ('WKc', g)], w=[('A', 'kw', p)])

        def stage2(c):
            g, j, p = c // 4, c % 4, c % 2
            L = ld[g % 2]
            for h in range(4):
                pair, hb = h // 2, (h % 2) * 64
                sc.pe(lambda e, h=h: e.matmul(PS[6 + p][:, h * 65:(h + 1) * 65], lhsT=PT[p][:, h, :], rhs=L['vv'][:, j, h * 65:(h + 1) * 65],
                                              start=True, stop=(c == 0)), r=[('A', 'PT', p), K(g, 'vv')], w=[('ps', 6 + p)], tag='O')
                if c > 0:
                    sc.pe(lambda e, h=h, pair=pair, hb=hb: e.matmul(PS[6 + p][:, h * 65:(h + 1) * 65], lhsT=qs[p][:, h, :],
                                                                   rhs=Sb[:, pair, :], start=False, stop=True),
                          r=[('A', 'qs', p), ('A', 'Sb')], w=[('ps', 6 + p)], tag='I')
            if c < NT - 1:
                for h in range(4):
                    pair, hb = h // 2, (h % 2) * 64
                    sc.pe(lambda e, h=h, pair=pair, hb=hb: e.matmul(PS[5][hb:hb + 64, p * 130 + pair * 65:p * 130 + (pair + 1) * 65],
                                                                   lhsT=kw[p][:, h, :], rhs=L['vv'][:, j, h * 65:(h + 1) * 65],
                                                                   start=True, stop=True),
                          r=[('A', 'kw', p), K(g, 'vv')], w=[('ps', 5)], tag='U')
                for pair in range(2):
                    sc.dve(lambda e, pair=pair: e.scalar_tensor_tensor(out=Sf[:, pair, :], in0=Sf[:, pair, :], scalar=AA[:, pair, c:c + 1],
                                                                       in1=PS[5][:, p * 130 + pair * 65:p * 130 + (pair + 1) * 65],
                                                                       op0=ALU.mult, op1=ALU.add),
                           r=[('ps', 5), ('AA', g, 0), ('AA', g, 1), ('A', 'Sf')], w=[('A', 'Sf')])
                sc.act(lambda e: e.copy(out=Sb, in_=Sf), r=[('A', 'Sf')], w=[('A', 'Sb')])
            psO = PS[6 + p][:, 0:260].rearrange("p (h d) -> p h d", h=4)
            kO = ('ps', 6 + p)
            sc.dve(lambda e: e.tensor_scalar(out=den, in0=psO[:, :, 64], scalar1=-1.0, scalar2=None, op0=ALU.mult), r=[kO], w=[('A', 'den')])
            sc.dve(lambda e: e.tensor_tensor(out=den, in0=den, in1=psO[:, :, 64], op=ALU.max), r=[kO, ('A', 'den')], w=[('A', 'den')])
            sc.dve(lambda e: e.tensor_scalar(out=den, in0=den, scalar1=1.0, scalar2=None, op0=ALU.max), r=[('A', 'den')], w=[('A', 'den')])
            sc.dve(lambda e: e.reciprocal(out=rec, in_=den), r=[('A', 'den')], w=[('A', 'rec')])
            sc.dve(lambda e: e.tensor_tensor(out=hN, in0=psO[:, :, 0:64], in1=bc_last(rec, 64), op=ALU.mult), r=[kO, ('A', 'rec')], w=[('A', 'hN')])
            sc.dve(lambda e: e.tensor_reduce(out=msum, in_=hN, axis=AX.X, op=ALU.add), r=[('A', 'hN')], w=[('A', 'msum')])
            sc.dve(lambda e: e.tensor_scalar(out=msum, in0=msum, scalar1=-1.0 / 64, scalar2=None, op0=ALU.mult), r=[('A', 'msum')], w=[('A', 'msum')])
            sc.dve(lambda e: e.tensor_tensor(out=cen, in0=hN, in1=bc_last(msum, 64), op=ALU.add), r=[('A', 'hN'), ('A', 'msum')], w=[('A', 'cen')])
            sc.dve(lambda e: e.tensor_tensor(out=sq, in0=cen, in1=cen, op=ALU.mult), r=[('A', 'cen')], w=[('A', 'sq')])
            sc.dve(lambda e: e.tensor_reduce(out=vs, in_=sq, axis=AX.X, op=ALU.add), r=[('A', 'sq')], w=[('A', 'vs')])
            sc.act(lambda e: e.activation(out=rstd, in_=vs, func=AF.Ln, scale=1.0 / 64, bias=EPS), r=[('A', 'vs')], w=[('A', 'rstd')])
            sc.act(lambda e: e.activation(out=rstd, in_=rstd, func=AF.Exp, scale=-0.5), r=[('A', 'rstd')], w=[('A', 'rstd')])
            sc.pool(lambda e: e.tensor_tensor(out=gso, in0=L['so'][:, j, :], in1=MNBC[:, :], op=ALU.mult), r=[K(g, 'so'), 'MNBC'], w=[('A', 'gso')])
            sc.dve(lambda e: e.tensor_tensor(out=ybf.rearrange("p (h d) -> p h d", h=4), in0=cen, in1=bc_last(rstd, 64), op=ALU.mult),
                   r=[('A', 'cen'), ('A', 'rstd')], w=[('A', 'ybf')])
            yb = ybb[p]
            sc.dve(lambda e: e.tensor_tensor(out=yb, in0=ybf, in1=gso, op=ALU.mult), r=[('A', 'ybf'), ('A', 'gso')], w=[('A', 'yb', p)])

        def stage2_tr(c):
            g, j, p = c // 4, c % 4, c % 2
            yb = ybb[p]
            pT = PS[5][:, 384:512].bitcast(BF16).rearrange("p (c n) -> p c n", c=2)
            for ch in range(2):
                sc.pe(lambda e, ch=ch: e.transpose(out=pT[:, ch, :], in_=yb[:, ch * 128:(ch + 1) * 128], identity=IDB[:]),
                      r=[('A', 'yb', p), 'IDB'], w=[('ps', 5)], tag='T')
            sc.act(lambda e: e.copy(out=ybS[g % 2][:, :, j * 128:(j + 1) * 128], in_=pT), r=[('ps', 5)], w=[('A', 'ybS', g % 2)])
            if j == 3:
                sc.dma(lambda e: e.dma_start(out=ybT.ap().rearrange("c p s -> p c s")[:, :, g * 512:(g + 1) * 512], in_=ybS[g % 2]),
                       r=[('A', 'ybS', g % 2)], w=[('ybT', g)])

        stage1(0)
        for c in range(NT):
            if c + 1 < NT:
                stage1(c + 1)
            if c > 0:
                stage2_tr(c - 1)
            stage2(c)
        stage2_tr(NT - 1)
        if self.debug:
            lp = (NT - 1) % 2
            dbg = self.nc.dram_tensor("dbgM", [128, 2048], F32, kind="ExternalOutput")
            dbb = self.nc.dram_tensor("dbgMb", [128, 2048], BF16, kind="ExternalOutput")
            items = [(Dm[lp].rearrange("p h n -> p (h n)"), 0, 512, [('A', 'Dm', lp, h) for h in range(4)]),
                     (EB[lp].rearrange("p a n -> p (a n)"), 512, 256, [('A', 'EB', lp, 0), ('A', 'EB', lp, 1)]),
                     (hN.rearrange("p h n -> p (h n)"), 768, 256, [('A', 'hN')]),
                     (cen.rearrange("p h n -> p (h n)"), 1024, 256, [('A', 'cen')]),
                     (ybf, 1280, 256, [('A', 'ybf')]),
                     (Sf.rearrange("p a n -> p (a n)"), 1536, 130, [('A', 'Sf')]),
                     (den, 1700, 4, [('A', 'den')]), (rec, 1704, 4, [('A', 'rec')]), (msum, 1708, 4, [('A', 'msum')]),
                     (vs, 1712, 4, [('A', 'vs')]), (rstd, 1716, 4, [('A', 'rstd')]), (gso, 1792, 256, [('A', 'gso')])]
            for ap_, off, n, keys in items:
                sc.dma(lambda e, ap_=ap_, off=off, n=n: e.dma_start(out=dbg[:, off:off + n], in_=ap_), r=keys, pw=['dbgM'])
            itb = [(PT[lp].rearrange("p h n -> p (h n)"), 0, 512, [('A', 'PT', lp)]),
                   (qs[lp].rearrange("p h n -> p (h n)"), 512, 512, [('A', 'qs', lp)]),
                   (kw[lp].rearrange("p h n -> p (h n)"), 1024, 256, [('A', 'kw', lp)]),
                   (Sb.rearrange("p a n -> p (a n)"), 1280, 130, [('A', 'Sb')]),
                   (ybb[lp], 1536, 256, [('A', 'yb', lp)])]
            for ap_, off, n, keys in itb:
                sc.dma(lambda e, ap_=ap_, off=off, n=n: e.dma_start(out=dbb[:, off:off + n], in_=ap_), r=keys, pw=['dbgMb'])
        self.barrier()

    def attn_phase(self, l):
        sc, NG, NT, PS, IDB, MNB, S = self.sc, self.NG, self.NT, self.PS, self.IDB, self.MNB, self.S
        ar = self.arena
        LAMC, DNBC = self.LAMC, self.DNBC
        hd = [dict(qT=ar.alloc([128, S], BF16), kT=ar.alloc([128, 2, S], BF16), vv=ar.alloc([128, NT, 130], BF16)) for _ in range(2)]
        PTb = [ar.alloc([128, 2, 256], BF16) for _ in range(3)]
        ycS = [ar.alloc([128, S], BF16) for _ in range(2)]
        accS = [ar.alloc([128, 2, 132], F32) for _ in range(2)]
        rec2 = [ar.alloc([128, 2], F32) for _ in range(2)]
        l2 = [ar.alloc([128, 1], F32) for _ in range(2)]
        o1 = [ar.alloc([128, 128], F32) for _ in range(2)]
        oo = [ar.alloc([128, 128], F32) for _ in range(2)]
        sqj = [ar.alloc([128, 128], F32) for _ in range(2)]
        ssq = [ar.alloc([128, 1], F32) for _ in range(2)]
        rstd = [ar.alloc([128, 1], F32) for _ in range(2)]
        yc = [ar.alloc([128, 128], BF16) for _ in range(2)]
        deferred = []
        dqT, dkZ, dvv, ycT = self.dqT, self.dkZ, self.dvv, self.ycT
        gk = lambda n: [(n, g) for g in range(NG)]
        uctr = [0]
        for h in range(4):
            H = hd[h % 2]
            kh = ('A', 'hd', h % 2)
            sc.dma(lambda e, h=h, H=H: e.dma_start(out=H['qT'], in_=dqT[h]), r=gk('dqT'), w=[(kh, 'q')])
            sc.dma(lambda e, h=h, H=H: e.dma_start(out=H['kT'], in_=dkZ[2 * h:2 * h + 2].rearrange("c p s -> p c s")), r=gk('dkZ'), w=[(kh, 'k')])
            sc.dma(lambda e, h=h, H=H: e.dma_start(out=H['vv'], in_=dvv.ap().rearrange("n p (h f) -> p n h f", h=4)[:, :, h, :]),
                   r=gk('dvv'), w=[(kh, 'v')])
            units = [(sb, j) for sb in range(NT // 2) for j in range(2 * sb + 2)]
            ycs = ycS[h % 2]
            kyc = ('A', 'ycS', h % 2)

            def score(sb, j, r, H=H, kh=kh):
                bank = r % 2
                psS = PS[bank][:, :].rearrange("p (m n) -> p m n", m=2)
                t0, t1 = 2 * sb, 2 * sb + 1
                pt = PTb[r % 3]
                for m in range(2):
                    kTm = H['kT'][:, m, j * 128:(j + 1) * 128]
                    if j <= t0:
                        sc.pe(lambda e, m=m, kTm=kTm: e.matmul(psS[:, m, :], lhsT=kTm, rhs=H['qT'][:, sb * 256:(sb + 1) * 256],
                                                               start=True, stop=(j < t0)), r=[(kh, 'q'), (kh, 'k')], w=[('ps', bank)])
                        if j == t0:
                            sc.pe(lambda e, m=m: e.matmul(psS[:, m, 0:128], lhsT=IDB[:], rhs=MNB[:], start=False, stop=True),
                                  r=['IDB', 'MNB'], w=[('ps', bank)])
                    else:
                        sc.pe(lambda e, m=m, kTm=kTm: e.matmul(psS[:, m, 128:256], lhsT=kTm, rhs=H['qT'][:, t1 * 128:(t1 + 1) * 128],
                                                               start=True, stop=False), r=[(kh, 'q'), (kh, 'k')], w=[('ps', bank)])
                        sc.pe(lambda e, m=m: e.matmul(psS[:, m, 128:256], lhsT=IDB[:], rhs=MNB[:], start=False, stop=True),
                              r=['IDB', 'MNB'], w=[('ps', bank)])
                if j <= t0:
                    sc.act(lambda e: e.activation(out=pt, in_=psS, func=AF.Exp, scale=0.125), r=[('ps', bank)], w=[('A', 'PTb', r % 3)])
                else:
                    sc.act(lambda e: e.activation(out=pt[:, :, 128:256], in_=psS[:, :, 128:256], func=AF.Exp, scale=0.125),
                           r=[('ps', bank)], w=[('A', 'PTb', r % 3)])

            def pv(sb, j, r, H=H, kh=kh):
                t0, t1 = 2 * sb, 2 * sb + 1
                pt = PTb[r % 3]
                for m in range(2):
                    for ti in range(2):
                        if ti == 0 and j > t0:
                            continue
                        bank = 2 + ti * 2 + m
                        last = t0 if ti == 0 else t1
                        sc.pe(lambda e, m=m, ti=ti, bank=bank, last=last: e.matmul(
                            PS[bank][:, 0:129], lhsT=pt[:, m, ti * 128:(ti + 1) * 128], rhs=H['vv'][:, j, 0:129], start=(j == 0), stop=(j == last)),
                            r=[('A', 'PTb', r % 3), (kh, 'v')], w=[('ps', bank)])
                for ti in range(2):
                    if j == (t0 if ti == 0 else t1):
                        epilogue(sb, ti)
                for d in deferred:
                    d[0] -= 1
                while deferred and deferred[0][0] <= 0:
                    deferred.pop(0)[1]()

            def epilogue(sb, ti, ycs=ycs, kyc=kyc):
                tile = 2 * sb + ti
                q = tile % 2
                b0, b1 = 2 + ti * 2, 3 + ti * 2
                kacc = ('A', 'accS', q)
                sc.dve(lambda e: e.tensor_copy(out=accS[q][:, 0, 0:129], in_=PS[b0][:, 0:129]), r=[('ps', b0)], w=[(kacc, 0)])
                sc.dve(lambda e: e.tensor_copy(out=accS[q][:, 1, 0:129], in_=PS[b1][:, 0:129]), r=[('ps', b1)], w=[(kacc, 1)])
                sc.dve(lambda e: e.reciprocal(out=rec2[q], in_=accS[q][:, :, 128]), r=[(kacc, 0), (kacc, 1)], w=[('A', 'rec2', q)])
                sc.dve(lambda e: e.tensor_tensor(out=l2[q], in0=rec2[q][:, 1:2], in1=LAMC[:, 3:4], op=ALU.mult), r=[('A', 'rec2', q), 'LAMC'],
                       w=[('A', 'l2', q)])
                sc.dve(lambda e: e.tensor_scalar(out=o1[q], in0=accS[q][:, 0, 0:128], scalar1=rec2[q][:, 0:1], scalar2=None, op0=ALU.mult),
                       r=[(kacc, 0), ('A', 'rec2', q)], w=[('A', 'o1', q)])
                sc.dve(lambda e: e.scalar_tensor_tensor(out=oo[q], in0=accS[q][:, 1, 0:128], scalar=l2[q][:, 0:1], in1=o1[q], op0=ALU.mult, op1=ALU.add),
                       r=[(kacc, 1), ('A', 'l2', q), ('A', 'o1', q)], w=[('A', 'oo', q)])
                sc.dve(lambda e: e.tensor_tensor(out=sqj[q], in0=oo[q], in1=oo[q], op=ALU.mult), r=[('A', 'oo', q)], w=[('A', 'sqj', q)])
                sc.dve(lambda e: e.tensor_reduce(out=ssq[q], in_=sqj[q], axis=AX.X, op=ALU.add), r=[('A', 'sqj', q)], w=[('A', 'ssq', q)])

                def partB():
                    sc.act(lambda e: e.activation(out=rstd[q], in_=ssq[q], func=AF.Ln, scale=1.0 / 128, bias=EPS), r=[('A', 'ssq', q)],
                           w=[('A', 'rstd', q)])
                    sc.act(lambda e: e.activation(out=rstd[q], in_=rstd[q], func=AF.Exp, scale=-0.5), r=[('A', 'rstd', q)], w=[('A', 'rstd', q)])
                    y = yc[q]
                    ky = ('A', 'yc', q)
                    sc.dve(lambda e: e.scalar_tensor_tensor(out=y, in0=oo[q], scalar=rstd[q][:, 0:1], in1=DNBC[:, :], op0=ALU.mult, op1=ALU.mult),
                           r=[('A', 'oo', q), ('A', 'rstd', q), 'DNBC'], w=[ky])
                    tb = 6 + q
                    pT = PS[tb][:, 0:64].bitcast(BF16)
                    sc.pe(lambda e: e.transpose(out=pT, in_=y, identity=IDB[:]), r=[ky, 'IDB'], w=[('ps', tb)])
                    sc.dve(lambda e: e.tensor_copy(out=ycs[:, tile * 128:(tile + 1) * 128], in_=pT), r=[('ps', tb)], w=[kyc])
                deferred.append([3, partB])

            r0 = uctr[0]
            score(units[0][0], units[0][1], r0)
            for i, (sb, j) in enumerate(units):
                if i + 1 < len(units):
                    score(units[i + 1][0], units[i + 1][1], r0 + i + 1)
                pv(sb, j, r0 + i)
            uctr[0] = r0 + len(units)
            while deferred:
                deferred.pop(0)[1]()
            sc.dma(lambda e, h=h, ycs=ycs: e.dma_start(out=ycT[h], in_=ycs), r=[kyc], w=[('ycT', h)])
        self.barrier()

    def attn_phase2(self, l):
        sc, NG, NT, PS, IDB, MNB, S = self.sc, self.NG, self.NT, self.PS, self.IDB, self.MNB, self.S
        ar = self.arena
        LAMC, DNCOL = self.LAMC, self.DNCOL
        hd = [dict(qT=ar.alloc([128, S], BF16), kT=ar.alloc([128, 2, S], BF16), vv=ar.alloc([128, NT, 130], BF16)) for _ in range(2)]
        PTb = [ar.alloc([128, 2, 256], BF16) for _ in range(3)]
        ycS = [ar.alloc([128, S], BF16) for _ in range(2)]
        accD = [ar.alloc([128, 2, 256], F32) for _ in range(2)]
        accS = [ar.alloc([128, 2, 256], F32) for _ in range(2)]
        rec = ar.alloc([128, 2, 256], F32)
        o1 = ar.alloc([128, 256], F32)
        oo = ar.alloc([128, 256], F32)
        sqh = ar.alloc([128, 256], BF16)
        sql = ar.alloc([128, 256], BF16)
        rsb = ar.alloc([128, 256], F32)
        ONESB = ar.alloc([128, 128], BF16)
        sc.dve(lambda e: e.memset(ONESB, 1.0), w=[('A', 'ONESB')])
        deferred = []
        dqT, dkZ, dvv, ycT = self.dqT, self.dkZ, self.dvv, self.ycT
        gk = lambda n: [(n, g) for g in range(NG)]
        uctr = [0]
        for h in range(4):
            H = hd[h % 2]
            kh = ('A', 'hd', h % 2)
            sc.dma(lambda e, h=h, H=H: e.dma_start(out=H['qT'], in_=dqT[h]), r=gk('dqT'), w=[(kh, 'q')])
            sc.dma(lambda e, h=h, H=H: e.dma_start(out=H['kT'], in_=dkZ[2 * h:2 * h + 2].rearrange("c p s -> p c s")), r=gk('dkZ'), w=[(kh, 'k')])
            sc.dma(lambda e, h=h, H=H: e.dma_start(out=H['vv'], in_=dvv.ap().rearrange("n p (h f) -> p n h f", h=4)[:, :, h, :]),
                   r=gk('dvv'), w=[(kh, 'v')])
            units = [(sb, j) for sb in range(NT // 2) for j in range(2 * sb + 2)]
            ycs = ycS[h % 2]
            kyc = ('A', 'ycS', h % 2)

            def score(sb, j, r, H=H, kh=kh):
                bank = r % 2
                psS = PS[bank][:, :].rearrange("p (m n) -> p m n", m=2)
                t0, t1 = 2 * sb, 2 * sb + 1
                pt = PTb[r % 3]
                for m in range(2):
                    kTm = H['kT'][:, m, j * 128:(j + 1) * 128]
                    if j <= t0:
                        sc.pe(lambda e, m=m, kTm=kTm: e.matmul(psS[:, m, :], lhsT=kTm, rhs=H['qT'][:, sb * 256:(sb + 1) * 256],
                                                               start=True, stop=(j < t0)), r=[(kh, 'q'), (kh, 'k')], w=[('ps', bank)])
                        if j == t0:
                            sc.pe(lambda e, m=m: e.matmul(psS[:, m, 0:128], lhsT=IDB[:], rhs=MNB[:], start=False, stop=True),
                                  r=['IDB', 'MNB'], w=[('ps', bank)])
                    else:
                        sc.pe(lambda e, m=m, kTm=kTm: e.matmul(psS[:, m, 128:256], lhsT=kTm, rhs=H['qT'][:, t1 * 128:(t1 + 1) * 128],
                                                               start=True, stop=False), r=[(kh, 'q'), (kh, 'k')], w=[('ps', bank)])
                        sc.pe(lambda e, m=m: e.matmul(psS[:, m, 128:256], lhsT=IDB[:], rhs=MNB[:], start=False, stop=True),
                              r=['IDB', 'MNB'], w=[('ps', bank)])
                if j <= t0:
                    sc.act(lambda e: e.activation(out=pt, in_=psS, func=AF.Exp, scale=0.125), r=[('ps', bank)], w=[('A', 'PTb', r % 3)])
                else:
                    sc.act(lambda e: e.activation(out=pt[:, :, 128:256], in_=psS[:, :, 128:256], func=AF.Exp, scale=0.125),
                           r=[('ps', bank)], w=[('A', 'PTb', r % 3)])

            def pv(sb, j, r, H=H, kh=kh):
                t0, t1 = 2 * sb, 2 * sb + 1
                q = sb % 2
                pt = PTb[r % 3]
                kpt = ('A', 'PTb', r % 3)
                cols = slice(0, 256) if j <= t0 else slice(128, 256)
                for m in range(2):
                    sc.pe(lambda e, m=m: e.matmul(PS[2 + m][:, cols], lhsT=H['vv'][:, j, 0:128], rhs=pt[:, m, cols], start=(j == 0), stop=(j == t1)),
                          r=[kpt, (kh, 'v')], w=[('ps', 2 + m)])
                for m in range(2):
                    sc.pe(lambda e, m=m: e.matmul(PS[4 + m][:, cols], lhsT=ONESB, rhs=pt[:, m, cols], start=(j == 0), stop=(j == t1)),
                          r=[kpt, ('A', 'ONESB')], w=[('ps', 4 + m)])
                if j == t1:
                    epilogue(sb)
                for d in deferred:
                    d[0] -= 1
                while deferred and deferred[0][0] <= 0:
                    deferred.pop(0)[1]()

            def epilogue(sb, ycs=ycs, kyc=kyc):
                q = sb % 2
                kas = ('A', 'accS', q)
                for m in range(2):
                    sc.dve(lambda e, m=m: e.tensor_copy(out=accS[q][:, m, :], in_=PS[2 + m][:, 0:256]), r=[('ps', 2 + m)], w=[(kas, m)])
                for m in range(2):
                    sc.act(lambda e, m=m: e.copy(out=accD[q][:, m, :], in_=PS[4 + m][:, 0:256]), r=[('ps', 4 + m)], w=[('A', 'accD', q, m)])

                def partB():
                    sc.dve(lambda e: e.reciprocal(out=rec.rearrange("p m n -> p (m n)"), in_=accD[q].rearrange("p m n -> p (m n)")),
                           r=[('A', 'accD', q, 0), ('A', 'accD', q, 1)], w=[('A', 'rec', 0), ('A', 'rec', 1)])
                    sc.dve(lambda e: e.tensor_tensor(out=o1, in0=accS[q][:, 0, :], in1=rec[:, 0, :], op=ALU.mult),
                           r=[(kas, 0), ('A', 'rec', 0)], w=[('A', 'o1')])
                    sc.dve(lambda e: e.tensor_tensor(out=rec[:, 1, :], in0=accS[q][:, 1, :], in1=rec[:, 1, :], op=ALU.mult),
                           r=[(kas, 1), ('A', 'rec', 1)], w=[('A', 'rec', 1)])
                    sc.dve(lambda e: e.scalar_tensor_tensor(out=oo, in0=rec[:, 1, :], scalar=LAMC[:, 3:4], in1=o1, op0=ALU.mult, op1=ALU.add),
                           r=[('A', 'rec', 1), ('A', 'o1'), 'LAMC'], w=[('A', 'oo')])
                    sc.dve(lambda e: e.tensor_tensor(out=o1, in0=oo, in1=oo, op=ALU.mult), r=[('A', 'oo'), ('A', 'o1')], w=[('A', 'o1')])
                    sc.pool(lambda e: e.tensor_copy(out=sqh, in_=o1), r=[('A', 'o1')], w=[('A', 'sqh')])
                    sc.pool(lambda e: e.tensor_tensor(out=sql, in0=o1, in1=sqh, op=ALU.subtract), r=[('A', 'o1'), ('A', 'sqh')], w=[('A', 'sql')])
                    sc.pe(lambda e: e.matmul(PS[7][:, 0:256], lhsT=ONESB, rhs=sqh, start=True, stop=False), r=[('A', 'ONESB'), ('A', 'sqh')],
                          w=[('ps', 7)])
                    sc.pe(lambda e: e.matmul(PS[7][:, 0:256], lhsT=ONESB, rhs=sql, start=False, stop=True), r=[('A', 'ONESB'), ('A', 'sql')],
                          w=[('ps', 7)])
                    sc.act(lambda e: e.activation(out=rsb, in_=PS[7][:, 0:256], func=AF.Ln, scale=1.0 / 128, bias=EPS), r=[('ps', 7)],
                           w=[('A', 'rsb')])
                    sc.act(lambda e: e.activation(out=rsb, in_=rsb, func=AF.Exp, scale=-0.5), r=[('A', 'rsb')], w=[('A', 'rsb')])
                    sc.dve(lambda e: e.scalar_tensor_tensor(out=ycs[:, sb * 256:(sb + 1) * 256], in0=oo, scalar=DNCOL[:, 0:1], in1=rsb,
                                                            op0=ALU.mult, op1=ALU.mult), r=[('A', 'oo'), ('A', 'rsb'), 'DNCOL'], w=[kyc])
                deferred.append([2, partB])

            r0 = uctr[0]
            score(units[0][0], units[0][1], r0)
            for i, (sb, j) in enumerate(units):
                if i + 1 < len(units):
                    score(units[i + 1][0], units[i + 1][1], r0 + i + 1)
                pv(sb, j, r0 + i)
            uctr[0] = r0 + len(units)
            while deferred:
                deferred.pop(0)[1]()
            sc.dma(lambda e, h=h, ycs=ycs: e.dma_start(out=ycT[h], in_=ycs), r=[kyc], w=[('ycT', h)])
        self.barrier()

    def phaseD(self, l):
        sc, NG, NT, PS, HT = self.sc, self.NG, self.NT, self.PS, self.HT
        ar, W = self.arena, self.W
        G, gkey = self.load_gain(W["mix_norm"][l:l + 1, :])
        NRG = 6
        wG = [ar.alloc([128, 1024], BF16) for _ in range(NRG)]
        pA = ar.alloc([128, 8, 256], BF16)
        pB = ar.alloc([128, 8, 256], BF16)
        pC = ar.alloc([128, 8, 512], BF16)
        wO = ar.alloc([128, 2, 4096], BF16)
        yl = [dict(a=ar.alloc([128, 2, 512], BF16), b=ar.alloc([128, 2, 512], BF16), c=ar.alloc([128, 4, 512], BF16)) for _ in range(2)]
        sig = [ar.alloc([128, 512], F32) for _ in range(3)]
        u = [ar.alloc([128, 512], F32) for _ in range(3)]
        mT = ar.alloc([128, 8, 512], BF16)
        winF, pab, pbb, pcb, woutb = self.winF[l], self.pab[l], self.pbb[l], self.pcb[l], self.woutb[l]
        sc.dma(lambda e: e.dma_start(out=pA, in_=pab.ap().rearrange("u p f -> p u f")), r=[('pabc', l)], w=[('A', 'pA')])
        sc.dma(lambda e: e.dma_start(out=pB, in_=pbb.ap().rearrange("u p f -> p u f")), r=[('pabc', l)], w=[('A', 'pB')])
        sc.dma(lambda e: e.dma_start(out=pC, in_=pcb.ap().rearrange("u p f -> p u f")), r=[('pabc', l)], w=[('A', 'pC')])
        sc.dma(lambda e: e.dma_start(out=wO, in_=woutb.ap().rearrange("h p f -> p h f")), r=[('woutb', l)], w=[('A', 'wO')])
        yaT, ybT, ycT = self.yaT, self.ybT, self.ycT
        hkeys = [('HT', j) for j in range(4)]
        gctr = 0
        items_per_group = min(7, (len(self.castq) + NG - 1) // NG) if self.castq else 0
        nxt = self.front_a(self.xs, 'xs', 0, G, gkey)
        self.norm_b()
        for g in range(NG):
            XG, kx = nxt
            Y = yl[g % 2]
            ky = ('A', 'yl', g % 2)
            sl = slice(g * 512, (g + 1) * 512)
            sc.dma(lambda e, Y=Y, sl=sl: e.dma_start(out=Y['a'], in_=yaT.ap().rearrange("c p s -> p c s")[:, :, sl]), r=[('yaT', g)], w=[(ky, 'a')])
            sc.dma(lambda e, Y=Y, sl=sl: e.dma_start(out=Y['b'], in_=ybT.ap().rearrange("c p s -> p c s")[:, :, sl]), r=[('ybT', g)], w=[(ky, 'b')])
            sc.dma(lambda e, Y=Y, sl=sl: e.dma_start(out=Y['c'], in_=ycT.ap().rearrange("c p s -> p c s")[:, :, sl]),
                   r=[('ycT', h) for h in range(4)], w=[(ky, 'c')])
            for fo in range(8):
                bset = (4, 5, 6) if fo % 2 == 0 else (7, 3, 2)
                for br, (pw_, yk, nk) in enumerate(((pA, 'a', 2), (pB, 'b', 2), (pC, 'c', 4))):
                    bank = bset[br]
                    for k in range(nk):
                        sc.pe(lambda e, pw_=pw_, yk=yk, k=k, nk=nk, fo=fo, bank=bank, Y=Y: e.matmul(
                            PS[bank][:, :], lhsT=pw_[:, fo, k * 128:(k + 1) * 128], rhs=Y[yk][:, k, :], start=(k == 0), stop=(k == nk - 1)),
                            r=[('A', 'pA'), ('A', 'pB'), ('A', 'pC'), (ky, yk)], w=[('ps', bank)])
                for br in range(3):
                    s = gctr % NRG
                    gctr += 1
                    wt = wG[s]
                    kw = ('A', 'wG', s)
                    sc.dma(lambda e, wt=wt, br=br, fo=fo: e.dma_start(out=wt, in_=winF[6 + br * 8 + fo]), r=[('winF', l)], w=[kw])
                    bank = gctr % 2
                    for k in range(8):
                        sc.pe(lambda e, wt=wt, k=k, bank=bank: e.matmul(PS[bank][:, :], lhsT=wt[:, k * 128:(k + 1) * 128], rhs=HT[:, k, :],
                                                                        start=(k == 0), stop=(k == 7)), r=[kw] + hkeys, w=[('ps', bank)])
                    sc.act(lambda e, br=br, bank=bank: e.activation(out=sig[br], in_=PS[bank][:, :], func=AF.Sigmoid),
                           r=[('ps', bank)], w=[('A', 'sig', br)])
                    sc.dve(lambda e, br=br, bb=bset[br]: e.tensor_tensor(out=u[br], in0=PS[bb][:, :], in1=sig[br], op=ALU.mult),
                           r=[('ps', bset[br]), ('A', 'sig', br)], w=[('A', 'u', br)])
                sc.pool(lambda e: e.tensor_tensor(out=u[0], in0=u[0], in1=u[1], op=ALU.add), r=[('A', 'u', 0), ('A', 'u', 1)], w=[('A', 'u', 0)])
                sc.pool(lambda e, fo=fo: e.tensor_tensor(out=mT[:, fo, :], in0=u[0], in1=u[2], op=ALU.add),
                        r=[('A', 'u', 0), ('A', 'u', 2)], w=[('A', 'mT', fo)])
            if g + 1 < NG:
                nxt = self.front_a(self.xs, 'xs', g + 1, G, gkey)
            for j in range(4):
                for half in range(2):
                    bank = (j * 2 + half) % 2
                    for fo in range(8):
                        sc.pe(lambda e, j=j, half=half, fo=fo, bank=bank: e.matmul(
                            PS[bank][:, :], lhsT=mT[:, fo, j * 128:(j + 1) * 128], rhs=wO[:, half, fo * 512:(fo + 1) * 512],
                            start=(fo == 0), stop=(fo == 7)), r=[('A', 'mT', fo), ('A', 'wO')], w=[('ps', bank)])
                    sc.dve(lambda e, j=j, half=half, bank=bank, XG=XG: e.tensor_tensor(
                        out=XG[:, j, half * 512:(half + 1) * 512], in0=PS[bank][:, :], in1=XG[:, j, half * 512:(half + 1) * 512], op=ALU.add),
                        r=[('ps', bank), kx], w=[kx])
            dst = self.xrows(self.xs, g)
            sc.dma(lambda e, XG=XG, dst=dst: e.dma_start(out=dst, in_=XG[:]), r=[kx], w=[('xs', g)])
            if g + 1 < NG:
                self.norm_b()
            self.drain(items_per_group)
        self.barrier()

    def program(self):
        for l in range(NL):
            self.queue_casts(l)
        nf = 2 * NFF + NFF // 2
        per_layer = 2 * nf + 47
        for l in range(NL):
            base = l * per_layer
            if l == 0:
                self.ffn_phase(l, 0, self.x_in, 'x_in', lazy_base=0)
            else:
                self.need_casts(base + nf)
                self.ffn_phase(l, 0, self.xs, 'xs')
            self.layer_params(l)
            self.need_casts(base + nf + 47)
            self.phaseB(l)
            self.mlstm_phase(l)
            self.attn_phase(l)
            self.phaseD(l)
            self.need_casts(base + per_layer)
            self.ffn_phase(l, 1, self.xs, 'xs', final=(l == NL - 1))


def build(S, debug=False, phases=None):
    from contextlib import ExitStack
    b = Builder(S, debug=debug, phases=phases)
    b.declare()
    ph = phases
    with ExitStack() as st:
        b.init_consts()
        if ph is None:
            b.program()
        else:
            ph(b)
        b.sc.emit(b.nc, st)
    return b


_BUILD_CACHE = {}
SEQ = 4096
NCORES = 8


def kernel(**inputs):
    if SEQ not in _BUILD_CACHE:
        _BUILD_CACHE[SEQ] = build(SEQ)
    b = _BUILD_CACHE[SEQ]
    x = np.asarray(inputs["x"], dtype=np.float32)
    pos = np.asarray(inputs["positions"], dtype=np.int32)
    cst = make_consts()
    shared = {"cst": cst}
    for n, s in WSHAPES:
        shared[n] = np.ascontiguousarray(np.asarray(inputs[n], dtype=np.float32).reshape(s))
    in_maps = []
    for c in range(NCORES):
        m = dict(shared)
        m["x"] = np.ascontiguousarray(x[c])
        m["pos"] = np.ascontiguousarray(pos[c].reshape(SEQ // 128, 128).T)
        in_maps.append(m)
    res = run_bass_kernel_spmd(b.nc, in_maps, core_ids=list(range(NCORES)))
    out = np.stack([np.asarray(r["out"], dtype=np.float32) for r in res.results], axis=0)
    return out
```

```python
import math
import os
import numpy as np
import concourse.bass as bass
import concourse.mybir as mybir
from concourse.bass_utils import run_bass_kernel_spmd

F32 = mybir.dt.float32
BF16 = mybir.dt.bfloat16
I32 = mybir.dt.int32
ALU = mybir.AluOpType
AF = mybir.ActivationFunctionType

SEM_LIMIT = 12000
NDMASEM = 8


class Sched:
    def __init__(self):
        self.ops = []
        self.last_w = {}
        self.part_w = {}
        self.readers = {}

    def add(self, eng, fn, r=(), w=(), dma=False, pw=()):
        i = len(self.ops)
        deps = set()
        for k in r:
            deps.update(self.last_w.get(k, ()))
            deps.update(self.part_w.get(k, ()))
            if isinstance(k, tuple) and k and k[0] == 'ps':
                for j in self.readers.get(k, ()):
                    if self.ops[j]['eng'] != eng:
                        deps.add(j)
        for k in w:
            deps.update(self.last_w.get(k, ()))
            deps.update(self.part_w.get(k, ()))
            deps.update(self.readers.get(k, ()))
        for k in pw:
            deps.update(self.last_w.get(k, ()))
            deps.update(self.readers.get(k, ()))
        for k in r:
            self.readers.setdefault(k, []).append(i)
        for k in w:
            self.last_w[k] = [i]
            self.part_w[k] = []
            self.readers[k] = []
        for k in pw:
            self.part_w.setdefault(k, []).append(i)
        deps.discard(i)
        latest = {}
        keep = set()
        for j in deps:
            d = self.ops[j]
            if d['dma']:
                keep.add(j)
            elif j > latest.get(d['eng'], -1):
                latest[d['eng']] = j
        keep.update(latest.values())
        deps = keep
        self.ops.append(dict(eng=eng, fn=fn, deps=deps, dma=dma, sig=None, needed=False))
        return i

    def pe(self, fn, r=(), w=()):
        return self.add('pe', fn, r, w)

    def act(self, fn, r=(), w=()):
        return self.add('act', fn, r, w)

    def dve(self, fn, r=(), w=()):
        return self.add('dve', fn, r, w)

    def pool(self, fn, r=(), w=()):
        return self.add('pool', fn, r, w)

    def dma(self, fn, r=(), w=(), q='sp', pw=()):
        return self.add(q, fn, r, w, dma=True, pw=pw)

    def emit(self, nc, stack):
        ops = self.ops
        for op in ops:
            for j in op['deps']:
                d = ops[j]
                if d['eng'] == 'pe' and op['eng'] == 'pe' and not d['dma'] and not op['dma']:
                    continue
                d['needed'] = True
        sems = {}

        def getsem(name):
            if name not in sems:
                sems[name] = stack.enter_context(nc.semaphore(name))
            return sems[name]

        cnt = {}
        gen = {}
        dcnt = {}
        for op in ops:
            e = op['eng']
            if op['dma']:
                n = dcnt.get(e, 0)
                dcnt[e] = n + 1
                r = n % NDMASEM
                s = getsem(f"d_{e}_{r}")
                op['sig'] = (s, 16 * (n // NDMASEM + 1))
                op['pre'] = (s, 16 * (n // NDMASEM)) if n >= NDMASEM else None
            elif op['needed']:
                c = cnt.get(e, 0) + 1
                if c > SEM_LIMIT:
                    gen[e] = gen.get(e, 0) + 1
                    c = 1
                cnt[e] = c
                op['sig'] = (getsem(f"s_{e}_{gen.get(e, 0)}"), c)
        by_eng = {}
        for i, op in enumerate(ops):
            by_eng.setdefault(op['eng'], []).append(i)
        self.n_waits = 0

        def run_engine(ename, eng):
            seen = {}
            for i in by_eng.get(ename, ()):
                op = ops[i]
                waits = {}
                for j in op['deps']:
                    d = ops[j]
                    if d['sig'] is None:
                        continue
                    if d['eng'] == 'pe' and ename == 'pe' and not d['dma'] and not op['dma']:
                        continue
                    s, v = d['sig']
                    k = id(s)
                    if k not in waits or waits[k][1] < v:
                        waits[k] = (s, v)
                if op['dma'] and op.get('pre') is not None:
                    s, v = op['pre']
                    k = id(s)
                    if k not in waits or waits[k][1] < v:
                        waits[k] = (s, v)
                for k, (s, v) in waits.items():
                    if seen.get(k, 0) >= v:
                        continue
                    eng.wait_ge(s, v)
                    seen[k] = v
                    self.n_waits += 1
                ins = op['fn'](eng)
                if op['sig'] is not None:
                    s, v = op['sig']
                    ins.then_inc(s, 16 if op['dma'] else 1)

        with nc.Block() as block:
            @block.tensor
            def _(eng):
                run_engine('pe', eng)

            @block.scalar
            def _(eng):
                run_engine('act', eng)

            @block.vector
            def _(eng):
                run_engine('dve', eng)

            @block.gpsimd
            def _(eng):
                run_engine('pool', eng)

            @block.sync
            def _(eng):
                run_engine('sp', eng)
                for name, s in sems.items():
                    pass
                last = {}
                for i in by_eng.get('sp', ()):
                    op = ops[i]
                    if op['dma']:
                        s, v = op['sig']
                        last[id(s)] = (s, v)
                for s, v in last.values():
                    eng.wait_ge(s, v)


D = 1024
DFF = 2816
NFF = 22
NIN = 5896
NL = 2
EPS = 1e-6
AX = mybir.AxisListType

C_ID, C_MN, C_MP, C_INV, C_SEL, C_ICNT, CW_ = 0, 128, 256, 384, 416, 928, 992
POOL_WINDOWS = (2, 4, 8, 16)

WSHAPES = [
    ("ffn1_norm", [NL, D]), ("ffn1_w13", [NL, D, 2 * DFF]), ("ffn1_w2", [NL, DFF, D]),
    ("mix_norm", [NL, D]), ("w_in", [NL, D, NIN]), ("pool_w", [NL, 4, 64, 64]),
    ("pool_scale", [NL, 256]), ("m_conv_w", [NL, 4, 512]), ("m_conv_b", [NL, 512]),
    ("m_gate_b", [NL, 8]), ("m_norm", [NL, 256]), ("d_lambda", [NL, 4, 64]),
    ("d_norm", [NL, 128]), ("p_a", [NL, 256, D]), ("p_b", [NL, 256, D]), ("p_c", [NL, 512, D]),
    ("w_out", [NL, D, D]), ("ffn2_norm", [NL, D]), ("ffn2_w13", [NL, D, 2 * DFF]),
    ("ffn2_w2", [NL, DFF, D]), ("final_norm", [1, D]),
]


def make_consts():
    c = np.zeros((128, CW_), np.float32)
    c[:, C_ID:C_ID + 128] = np.eye(128, dtype=np.float32)
    s = np.arange(128)[:, None]
    t = np.arange(128)[None, :]
    c[:, C_MN:C_MN + 128] = np.where(s <= t, 0.0, -30000.0)
    c[:, C_MP:C_MP + 128] = np.where(s <= t, 0.0, 30000.0)
    inv = np.float32(1.0) / np.power(np.float32(10000.0), np.arange(0, 64, 2, dtype=np.float32) / np.float32(64))
    c[:, C_INV:C_INV + 32] = inv[None, :]
    for h in range(4):
        c[4 + h, C_SEL + h * 128:C_SEL + (h + 1) * 128] = 1.0
    for gi, w in enumerate(POOL_WINDOWS):
        c[:, C_ICNT + gi * 16:C_ICNT + (gi + 1) * 16] = 1.0 / np.minimum(np.arange(16) + 1, w)
    return c


def _dsize(dt):
    return 4 if dt in (F32, I32) else 2


class Arena:
    def __init__(self, t, nwords):
        self.t = t
        self.n = nwords
        self.off = 0

    def reset(self):
        self.off = 0

    def alloc(self, shape, dt, parts=None):
        nel = 1
        for s_ in shape[1:]:
            nel *= s_
        nw = (nel * _dsize(dt) + 3) // 4
        nw = (nw + 7) // 8 * 8
        assert self.off + nw <= self.n, ("arena overflow", self.off, nw, self.n)
        a = self.t[0:shape[0], self.off:self.off + nw]
        self.off += nw
        if dt != F32:
            a = a.bitcast(dt)
        a = a[:, 0:nel]
        return _view(a, shape[1:])


def _view(a, dims):
    if len(dims) == 1:
        return a
    if len(dims) == 2:
        return a.rearrange("p (a b) -> p a b", a=dims[0])
    if len(dims) == 3:
        return a.rearrange("p (a b c) -> p a b c", a=dims[0], b=dims[1])
    raise ValueError(dims)


def bc_mid(ap2, n):
    return ap2.unsqueeze(1).broadcast_to([ap2.shape[0], n, ap2.shape[1]])


def bc_last(ap2, n):
    return ap2.unsqueeze(2).broadcast_to([ap2.shape[0], ap2.shape[1], n])


class Builder:
    def __init__(self, S, debug=False, phases=None):
        self.S = S
        self.NT = S // 128
        self.NG = S // 512
        self.debug = debug
        self.phases = phases
        self.nc = bass.Bass("TRN2", target_bir_lowering=False)
        self.sc = Sched()
        _add = self.sc.add

        def add2(eng, fn, r=(), w=(), dma=False, pw=()):
            r = list(r)
            for k in list(r) + list(w) + list(pw):
                if isinstance(k, tuple) and k and (k[0] == 'A' or (isinstance(k[0], tuple) and k[0] and k[0][0] == 'A')):
                    r.append('arena')
                    break
            return _add(eng, fn, r, w, dma, pw)
        self.sc.add = add2
        self.castq = []
        self.cast_items_done = 0
        self.n_early = 2 * NFF + NFF // 2
        self.cast_recorded = 0
        self.pending_fin = None
        self.xg_ctr = 0

    def declare(self):
        nc, S, NT = self.nc, self.S, self.NT
        kind_s = "ExternalOutput" if self.debug else "Internal"

        def din(name, shape, dt=F32):
            return nc.dram_tensor(name, list(shape), dt, kind="ExternalInput")

        def dsc(name, shape, dt):
            return nc.dram_tensor(name, list(shape), dt, kind=kind_s)
        self.x_in = din("x", [S, D])
        self.pos_in = din("pos", [128, NT], I32)
        self.cst_in = din("cst", [128, CW_])
        self.W = {n: din(n, s) for n, s in WSHAPES}
        self.out_t = nc.dram_tensor("out", [S, D], F32, kind="ExternalOutput")
        self.xs = dsc("xs", [S, D], F32)
        self.w13b = [[dsc(f"w13b_{l}_{f}", [2 * NFF, 128, 1024], BF16) for f in range(2)] for l in range(NL)]
        self.w2b = [[dsc(f"w2b_{l}_{f}", [NFF, 128, 1024], BF16) for f in range(2)] for l in range(NL)]
        self.winF = [dsc(f"winF_{l}", [30, 128, 1024], BF16) for l in range(NL)]
        self.winIF = [dsc(f"winIF_{l}", [1, 128, 64], BF16) for l in range(NL)]
        self.winT = [dsc(f"winT_{l}", [4, 128, 4096], BF16) for l in range(NL)]
        self.pab = [dsc(f"pab_{l}", [8, 128, 256], BF16) for l in range(NL)]
        self.pbb = [dsc(f"pbb_{l}", [8, 128, 256], BF16) for l in range(NL)]
        self.pcb = [dsc(f"pcb_{l}", [8, 128, 512], BF16) for l in range(NL)]
        self.woutb = [dsc(f"woutb_{l}", [2, 128, 4096], BF16) for l in range(NL)]
        self.yaT = dsc("yaT", [2, 128, S], BF16)
        self.mqT = dsc("mqT", [2, 128, S], BF16)
        self.mkZ = dsc("mkZ", [4, 128, S], BF16)
        self.kTok = dsc("kTok", [NT, 128, 256], BF16)
        self.vv = dsc("vv", [NT, 128, 260], BF16)
        self.so = dsc("so", [NT, 128, 256], F32)
        self.cs8 = dsc("cs8", [8, S], F32)
        self.dqT = dsc("dqT", [4, 128, S], BF16)
        self.dkZ = dsc("dkZ", [8, 128, S], BF16)
        self.dvv = dsc("dvv", [NT, 128, 520], BF16)
        self.ybT = dsc("ybT", [2, 128, S], BF16)
        self.ycT = dsc("ycT", [4, 128, S], BF16)

        A = nc.alloc_sbuf_tensor
        self.CST = A("CST", [128, CW_], F32)
        self.IDB = A("IDB", [128, 128], BF16)
        self.MNB = A("MNB", [128, 128], BF16)
        self.COS = A("COS", [128, NT * 32], F32)
        self.SIN = A("SIN", [128, NT * 32], F32)
        self.XG = [A(f"XG{i}", [128, 4, 1024], F32) for i in range(2)]
        self.HN = [A(f"HN{i}", [128, 1024], BF16) for i in range(4)]
        self.HT = A("HT", [128, 8, 512], BF16)
        self.GBC = [A(f"GBC{i}", [128, 1024], F32) for i in range(2)]
        self.gbc_ctr = 0
        self.SSQ = A("SSQ", [128, 4], F32)
        self.RSTD = A("RSTD", [128, 4], F32)
        self.SSQ2 = A("SSQ2", [128, 4], F32)
        self.RSTD2 = A("RSTD2", [128, 4], F32)
        self.NST = 2
        self.STF = [A(f"STF{i}", [128, 2048], F32) for i in range(self.NST)]
        self.STB = [A(f"STB{i}", [128, 2048], BF16) for i in range(self.NST)]
        self.cast_ctr = 0
        self.NST_EARLY = 6
        self.PWF = A("PWF", [128, 2, 128], F32)
        self.PWB = A("PWB", [128, 2, 128], BF16)
        self.PSC = A("PSC", [128, 2], F32)
        self.CWc = A("CWc", [128, 4, 4], F32)
        self.CB = A("CB", [128, 4], F32)
        self.GB8 = A("GB8", [8, 1], F32)
        self.MNBC = A("MNBC", [128, 256], F32)
        self.DNBC = A("DNBC", [128, 128], F32)
        self.LAM = A("LAM", [128, 256], F32)
        self.DNCOL = A("DNCOL", [128, 1], F32)
        self.LAMC = A("LAMC", [128, 8], F32)
        self.COLA = A("COLA", [128, NT, 4], F32)
        self.NCE = A("NCE", [128, 4, NT], F32)
        self.WKc = A("WKc", [128, NT, 4], F32)
        self.AA = A("AA", [128, 2, NT], F32)
        self.BAR = A("BAR", [128, 8], F32)
        self.ONES8 = A("ONES8", [8, 128], F32)
        self.ARENA_WORDS = 27136
        self.ARENA_T = A("ARENA", [128, self.ARENA_WORDS], F32)
        self.arena = Arena(self.ARENA_T, self.ARENA_WORDS)
        for i_ in range(4):
            o_ = self.ARENA_WORDS - (i_ + 1) * 3072
            self.STF.append(self.ARENA_T[:, o_:o_ + 2048])
            self.STB.append(self.ARENA_T[:, o_ + 2048:o_ + 3072].bitcast(BF16))
        self.PS = [nc.alloc_psum_tensor(f"ps{b}", [128, 512], F32) for b in range(8)]

    def barrier(self):
        BAR = self.BAR
        self.sc.dve(lambda e: e.memset(BAR[0:1, 0:1], 0.0), w=['arena'])
        self.arena.reset()

    def init_consts(self):
        sc, NT = self.sc, self.NT
        CST, IDB, MNB, COS, SIN = self.CST, self.IDB, self.MNB, self.COS, self.SIN
        cst_in, pos_in = self.cst_in, self.pos_in
        sc.dma(lambda e: e.dma_start(out=CST[:], in_=cst_in[:, :]), w=['CST'])
        sc.dve(lambda e: e.tensor_copy(out=IDB[:], in_=CST[:, C_ID:C_ID + 128]), r=['CST'], w=['IDB'])
        sc.dve(lambda e: e.tensor_copy(out=MNB[:], in_=CST[:, C_MN:C_MN + 128]), r=['CST'], w=['MNB'])
        ONES8 = self.ONES8
        sc.dve(lambda e: e.memset(ONES8[:], 1.0), w=['ONES8'])
        PWF = self.PWF
        sc.dve(lambda e: e.memset(PWF[:], 0.0), w=['PWF'])
        ar = self.arena
        posi = ar.alloc([128, NT], I32)
        posf = ar.alloc([128, NT], F32)
        ang = ar.alloc([128, NT * 32], F32)
        a2 = ar.alloc([128, NT * 32], F32)
        u = ar.alloc([128, NT * 32], F32)
        ki = ar.alloc([128, NT * 32], I32)
        kf = ar.alloc([128, NT * 32], F32)
        m = ar.alloc([128, NT * 32], F32)
        kA = ('A', 'init')
        sc.dma(lambda e: e.dma_start(out=posi, in_=pos_in[:, :]), w=[kA])
        sc.dve(lambda e: e.tensor_copy(out=posf, in_=posi), r=[kA], w=[kA])
        for n in range(NT):
            sc.dve(lambda e, n=n: e.tensor_scalar(out=ang[:, n * 32:(n + 1) * 32], in0=CST[:, C_INV:C_INV + 32],
                                                  scalar1=posf[:, n:n + 1], scalar2=None, op0=ALU.mult),
                   r=[kA, 'CST'], w=[kA])
        TWO_PI = 2.0 * math.pi
        C1 = 6.28125
        C2 = TWO_PI - C1

        def reduce_sin(src, shift, dst, key):
            sc.dve(lambda e: e.tensor_scalar(out=a2, in0=src, scalar1=float(shift), scalar2=None, op0=ALU.add), r=[kA], w=[kA])
            sc.dve(lambda e: e.tensor_scalar(out=u, in0=a2, scalar1=float(1.0 / TWO_PI), scalar2=None, op0=ALU.mult), r=[kA], w=[kA])
            sc.dve(lambda e: e.tensor_copy(out=ki, in_=u), r=[kA], w=[kA])
            sc.dve(lambda e: e.tensor_copy(out=kf, in_=ki), r=[kA], w=[kA])
            sc.dve(lambda e: e.scalar_tensor_tensor(out=a2, in0=kf, scalar=-C1, in1=a2, op0=ALU.mult, op1=ALU.add), r=[kA], w=[kA])
            sc.dve(lambda e: e.scalar_tensor_tensor(out=a2, in0=kf, scalar=-C2, in1=a2, op0=ALU.mult, op1=ALU.add), r=[kA], w=[kA])
            sc.dve(lambda e: e.tensor_single_scalar(out=m, in_=a2, scalar=math.pi, op=ALU.is_gt), r=[kA], w=[kA])
            sc.dve(lambda e: e.scalar_tensor_tensor(out=a2, in0=m, scalar=-TWO_PI, in1=a2, op0=ALU.mult, op1=ALU.add), r=[kA], w=[kA])
            sc.dve(lambda e: e.tensor_single_scalar(out=m, in_=a2, scalar=-math.pi, op=ALU.is_lt), r=[kA], w=[kA])
            sc.dve(lambda e: e.scalar_tensor_tensor(out=a2, in0=m, scalar=TWO_PI, in1=a2, op0=ALU.mult, op1=ALU.add), r=[kA], w=[kA])
            sc.dve(lambda e: e.tensor_scalar(out=a2, in0=a2, scalar1=3.1415925, scalar2=-3.1415925, op0=ALU.min, op1=ALU.max), r=[kA], w=[kA])
            sc.act(lambda e: e.activation(out=dst[:], in_=a2, func=AF.Sin), r=[kA], w=[key])
        reduce_sin(ang, 0.0, SIN, 'SIN')
        reduce_sin(ang, math.pi / 2.0, COS, 'COS')
        self.barrier()

    def _cast_route(self):
        n = self.cast_items_done
        self.cast_items_done += 1
        if n < self.n_early:
            return ('pool', 'act', 'dve')[n % 3], 'sp'
        return 'pool', 'pool'

    def _cast_emit(self, in_ap, out_ap, src_ap, dst_ap, st_in, st_out, dkey, i):
        sc = self.sc
        eng, q = self._cast_route()
        kf_, kb_ = (('STF', i), ('STB', i)) if i < 2 else (('A', 'STF', i), ('A', 'STB', i))
        sc.dma(lambda e: e.dma_start(out=st_in, in_=src_ap), w=[kf_], q=q)

        def fin():
            if eng == 'act':
                sc.act(lambda e: e.copy(out=out_ap, in_=in_ap), r=[kf_], w=[kb_])
            elif eng == 'dve':
                sc.dve(lambda e: e.tensor_copy(out=out_ap, in_=in_ap), r=[kf_], w=[kb_])
            else:
                sc.pool(lambda e: e.tensor_copy(out=out_ap, in_=in_ap), r=[kf_], w=[kb_])
            sc.dma(lambda e: e.dma_start(out=dst_ap, in_=st_out), r=[kb_], pw=[dkey], q=q)
        if q == 'sp':
            fin()
        else:
            if self.pending_fin is not None:
                self.pending_fin()
            self.pending_fin = fin

    def cast_item(self, src_ap, n_in, in_view, out_view, dst_ap, dkey):
        sc = self.sc

        def item():
            i = self.cast_ctr % (self.NST_EARLY if self.cast_items_done < self.n_early else self.NST)
            self.cast_ctr += 1
            stf, stb = self.STF[i], self.STB[i]
            self._cast_emit(in_view(stf[:, 0:n_in]), out_view(stb[:, 0:n_in]), src_ap, dst_ap, in_view(stf[:, 0:n_in]),
                            out_view(stb[:, 0:n_in]), dkey, i)
        self.castq.append(item)

    def cast_cols(self, src2d, nk, col0, ncols, group, dst, u0, dkey):
        u = ncols // group
        src_ap = src2d.rearrange("(k p) n -> p k n", p=128)[:, :, col0:col0 + ncols]
        n_in = nk * ncols

        def in_view(a):
            return a.rearrange("p (k n) -> p k n", k=nk)

        def out_view(a):
            return a.rearrange("p (u k g) -> p k (u g)", u=u, k=nk) if u == 1 else \
                a.rearrange("p (u k g) -> p k u g", u=u, k=nk)

        def in_view2(a):
            v = a.rearrange("p (k n) -> p k n", k=nk)
            return v if u == 1 else a.rearrange("p (k u g) -> p k u g", k=nk, u=u)
        dst_ap = dst[u0:u0 + u].rearrange("u p f -> p u f")
        sc = self.sc

        def item():
            i = self.cast_ctr % (self.NST_EARLY if self.cast_items_done < self.n_early else self.NST)
            self.cast_ctr += 1
            stf, stb = self.STF[i], self.STB[i]
            self._cast_emit(in_view2(stf[:, 0:n_in]), out_view(stb[:, 0:n_in]), src_ap, dst_ap, in_view(stf[:, 0:n_in]),
                            stb[:, 0:n_in].rearrange("p (u f) -> p u f", u=u), dkey, i)
        self.castq.append(item)

    def queue_casts(self, l):
        W = self.W
        for f, (n13, n2) in enumerate((("ffn1_w13", "ffn1_w2"), ("ffn2_w13", "ffn2_w2"))):
            if f == 1:
                self.queue_mixer_casts(l)
            for c in range(NFF):
                key = ('w13b', l, f)
                self.cast_cols(W[n13][l], 8, c * 128, 128, 128, self.w13b[l][f], 2 * c, key)
                self.cast_cols(W[n13][l], 8, DFF + c * 128, 128, 128, self.w13b[l][f], 2 * c + 1, key)
            for c in range(0, NFF, 2):
                src = W[n2][l][c * 128:(c + 2) * 128, :].rearrange("(c p) n -> p c n", p=128)
                dst = self.w2b[l][f][c:c + 2].rearrange("c p n -> p c n")
                self.cast_item(src, 2048, lambda a: a.rearrange("p (c n) -> p c n", c=2),
                               lambda a: a.rearrange("p (c n) -> p c n", c=2), dst, ('w2b', l, f))

    def queue_mixer_casts(self, l):
        W = self.W
        win = W["w_in"][l]
        kF = ('winF', l)
        for i, c0 in enumerate((0, 128, 256, 384, 512, 640)):
            self.cast_cols(win, 8, c0, 128, 128, self.winF[l], i, kF)
        self.cast_cols(win, 8, 1280, 8, 8, self.winIF[l], 0, kF)
        for i, c0 in enumerate((768, 1288, 1800, 2312)):
            for hh in range(2):
                self.cast_T_half(win, c0 + hh * 256, self.winT[l], i, hh, ('winT', l))
        for i in range(24):
            self.cast_cols(win, 8, 2824 + i * 128, 128, 128, self.winF[l], 6 + i, kF)
        for name, nk, dst in (("p_a", 2, self.pab[l]), ("p_b", 2, self.pbb[l]), ("p_c", 4, self.pcb[l])):
            per = 2048 // (nk * 128)
            for c in range(0, 8, per):
                self.cast_cols(W[name][l], nk, c * 128, per * 128, 128, dst, c, ('pabc', l))
        for hh in range(2):
            for q in range(2):
                self.cast_T_half(W["w_out"][l], hh * 512 + q * 256, self.woutb[l], hh, q, ('woutb', l))

    def cast_T_half(self, src2d, col0, dst, blk, hh, dkey):
        src_ap = src2d.rearrange("(k p) n -> p k n", p=128)[:, :, col0:col0 + 256]
        dst_ap = dst[blk].rearrange("p (k n) -> p k n", k=8)[:, :, hh * 256:(hh + 1) * 256]
        v = lambda a: a.rearrange("p (k n) -> p k n", k=8)
        self.cast_item_3(src_ap, dst_ap, v, dkey)

    def cast_item_3(self, src_ap, dst_ap, v, dkey):
        sc = self.sc

        def item():
            i = self.cast_ctr % (self.NST_EARLY if self.cast_items_done < self.n_early else self.NST)
            self.cast_ctr += 1
            stf, stb = self.STF[i], self.STB[i]
            self._cast_emit(stf[:, 0:2048], stb[:, 0:2048], src_ap, dst_ap, v(stf[:, 0:2048]), v(stb[:, 0:2048]), dkey, i)
        self.castq.append(item)

    def drain(self, n=None):
        k = len(self.castq) if n is None else min(n, len(self.castq))
        for _ in range(k):
            self.castq.pop(0)()
            self.cast_recorded += 1

    def need_casts(self, total):
        if self.cast_recorded < total:
            self.drain(total - self.cast_recorded)
        if self.pending_fin is not None:
            self.pending_fin()
            self.pending_fin = None

    def load_gain(self, row_ap):
        i = self.gbc_ctr % 2
        self.gbc_ctr += 1
        G = self.GBC[i]
        self.sc.dma(lambda e: e.dma_start(out=G[:], in_=row_ap.partition_broadcast(128)), w=[('GBC', i)])
        return G, ('GBC', i)

    def xrows(self, t, g):
        return t.ap().rearrange("(n p) d -> p n d", p=128)[:, 4 * g:4 * g + 4, :]

    def frontend(self, src_t, src_key, g, G, gkey, ev=0):
        sc = self.sc
        slot = self.xg_ctr % 2
        self.xg_ctr += 1
        XG = self.XG[slot]
        kx = ('XG', slot)
        src = self.xrows(src_t, g)
        sc.dma(lambda e: e.dma_start(out=XG[:], in_=src), r=[(src_key, g)], w=[kx])
        self.norm_T(XG, kx, G, gkey)
        return XG, kx

    def norm_T(self, XG, kx, G, gkey):
        self.norm_a(XG, kx, G, gkey)
        return self.norm_b()

    def norm_a(self, XG, kx, G, gkey):
        sc = self.sc
        SSQ, RSTD = self.SSQ, self.RSTD
        for j in range(4):
            HN = self.HN[j]
            sc.act(lambda e, j=j, HN=HN: e.activation(out=HN[:], in_=XG[:, j, :], func=AF.Square, accum_out=SSQ[:, j:j + 1]),
                   r=[kx], w=[('SSQ', j), ('HN', j)])
        sc.act(lambda e: e.activation(out=RSTD[:], in_=SSQ[:], func=AF.Sqrt, scale=1.0 / D, bias=EPS),
               r=[('SSQ', j) for j in range(4)], w=['RSTD'])
        sc.dve(lambda e: e.reciprocal(out=RSTD[:], in_=RSTD[:]), r=['RSTD'], w=['RSTD'])
        for j in range(4):
            HN = self.HN[j]
            sc.dve(lambda e, j=j, HN=HN: e.scalar_tensor_tensor(out=HN[:], in0=XG[:, j, :], scalar=RSTD[:, j:j + 1], in1=G[:],
                                                               op0=ALU.mult, op1=ALU.mult), r=[kx, 'RSTD', gkey], w=[('HN', j)])

    def norm_b(self):
        sc = self.sc
        HT, IDB, PS = self.HT, self.IDB, self.PS
        for j in range(4):
            HN = self.HN[j]
            kh = ('HN', j)
            pT = PS[j][:, 0:512].bitcast(BF16).rearrange("p (k t) -> p k t", k=8)
            for k in range(8):
                sc.pe(lambda e, k=k, HN=HN, pT=pT: e.transpose(out=pT[:, k, :], in_=HN[:, k * 128:(k + 1) * 128], identity=IDB[:]),
                      r=[kh, 'IDB'], w=[('ps', j)])
            if j % 2 == 0:
                sc.act(lambda e, j=j, pT=pT: e.copy(out=HT[:, :, j * 128:(j + 1) * 128], in_=pT), r=[('ps', j)], w=[('HT', j)])
            else:
                sc.dve(lambda e, j=j, pT=pT: e.tensor_copy(out=HT[:, :, j * 128:(j + 1) * 128], in_=pT), r=[('ps', j)], w=[('HT', j)])
        return [('HT', j) for j in range(4)]

    def front_a(self, src_t, src_key, g, G, gkey):
        sc = self.sc
        slot = self.xg_ctr % 2
        self.xg_ctr += 1
        XG = self.XG[slot]
        kx = ('XG', slot)
        src = self.xrows(src_t, g)
        sc.dma(lambda e: e.dma_start(out=XG[:], in_=src), r=[(src_key, g)], w=[kx])
        self.norm_a(XG, kx, G, gkey)
        return XG, kx

    def ffn_phase(self, l, f, src_t, src_key, final=False, lazy_base=None):
        sc, NG, PS, HT = self.sc, self.NG, self.PS, self.HT
        ar = self.arena
        W = self.W
        nname = "ffn1_norm" if f == 0 else "ffn2_norm"
        G, gkey = self.load_gain(W[nname][l:l + 1, :])
        if final:
            FG, fgkey = self.load_gain(W["final_norm"][0:1, :])
        actT = ar.alloc([128, NFF, 512], BF16)
        NR13, NR2 = 6, 4
        w13r = [ar.alloc([128, 2, 1024], BF16) for _ in range(NR13)]
        sg = [ar.alloc([128, 512], F32) for _ in range(2)]
        w13b, w2b = self.w13b[l][f], self.w2b[l][f]
        resident_w2 = lazy_base is None
        if resident_w2:
            w2res = ar.alloc([128, NFF, 1024], BF16)
            for part in range(2):
                sc.dma(lambda e, part=part: e.dma_start(out=w2res[:, part * 11:(part + 1) * 11, :],
                                                        in_=w2b[part * 11:(part + 1) * 11].rearrange("c p n -> p c n")),
                       r=[('w2b', l, f)], w=[('A', 'w2res', part)])
        else:
            w2r = [ar.alloc([128, 2, 512], BF16) for _ in range(NR2)]
        hkeys = [('HT', j) for j in range(4)]
        c13 = 0
        c2 = 0
        items_per_group = min(7, (len(self.castq) + NG - 1) // NG) if self.castq else 0
        nxt = self.front_a(src_t, src_key, 0, G, gkey)
        self.norm_b()
        for g in range(NG):
            XG, kx = nxt
            for c in range(NFF):
                s = c13 % NR13
                c13 += 1
                wt = w13r[s]
                kw = ('A', 'w13', s)
                if lazy_base is not None and g == 0:
                    self.need_casts(lazy_base + 2 * (c + 1))
                sc.dma(lambda e, c=c, wt=wt: e.dma_start(out=wt, in_=w13b[2 * c:2 * c + 2].rearrange("u p f -> p u f")),
                       r=[('w13b', l, f)], w=[kw])
                bg, bu = (0, 1) if c % 2 == 0 else (2, 3)
                for half, bank in ((0, bg), (1, bu)):
                    for k in range(8):
                        sc.pe(lambda e, k=k, wt=wt, half=half, bank=bank: e.matmul(
                            PS[bank][:, :], lhsT=wt[:, half, k * 128:(k + 1) * 128], rhs=HT[:, k, :], start=(k == 0), stop=(k == 7)),
                            r=[kw] + hkeys, w=[('ps', bank)])
                sgb = sg[c % 2]
                ks = ('A', 'sg', c % 2)
                sc.act(lambda e, bg=bg, sgb=sgb: e.activation(out=sgb, in_=PS[bg][:, :], func=AF.Silu), r=[('ps', bg)], w=[ks])
                sc.dve(lambda e, c=c, bu=bu, sgb=sgb: e.tensor_tensor(out=actT[:, c, :], in0=PS[bu][:, :], in1=sgb, op=ALU.mult),
                       r=[('ps', bu), ks], w=[('A', 'actT', c)])
            if g + 1 < NG:
                nxt = self.front_a(src_t, src_key, g + 1, G, gkey)
            for half in range(2):
                for c in range(0, NFF, 2):
                    if resident_w2:
                        for cc in range(2):
                            for j in range(4):
                                sc.pe(lambda e, c=c, cc=cc, j=j, half=half: e.matmul(
                                    PS[4 + j][:, :], lhsT=actT[:, c + cc, j * 128:(j + 1) * 128],
                                    rhs=w2res[:, c + cc, half * 512:(half + 1) * 512],
                                    start=(c + cc == 0), stop=(c + cc == NFF - 1)),
                                    r=[('A', 'w2res', (c + cc) // 11), ('A', 'actT', c + cc)], w=[('ps', 4 + j)])
                        continue
                    s = c2 % NR2
                    c2 += 1
                    wt = w2r[s]
                    kw = ('A', 'w2', s)
                    if lazy_base is not None and g == 0:
                        self.need_casts(lazy_base + 2 * NFF + c // 2 + 1)
                    sc.dma(lambda e, c=c, wt=wt, half=half: e.dma_start(
                        out=wt, in_=w2b[c:c + 2].rearrange("c p n -> p c n")[:, :, half * 512:(half + 1) * 512]),
                        r=[('w2b', l, f)], w=[kw])
                    for cc in range(2):
                        for j in range(4):
                            sc.pe(lambda e, c=c, cc=cc, j=j, wt=wt: e.matmul(
                                PS[4 + j][:, :], lhsT=actT[:, c + cc, j * 128:(j + 1) * 128], rhs=wt[:, cc, :],
                                start=(c + cc == 0), stop=(c + cc == NFF - 1)),
                                r=[kw, ('A', 'actT', c + cc)], w=[('ps', 4 + j)])
                for j in range(4):
                    sc.dve(lambda e, j=j, half=half, XG=XG: e.scalar_tensor_tensor(
                        out=XG[:, j, half * 512:(half + 1) * 512], in0=PS[4 + j][:, :], scalar=0.5,
                        in1=XG[:, j, half * 512:(half + 1) * 512], op0=ALU.mult, op1=ALU.add),
                        r=[('ps', 4 + j), kx], w=[kx])
            if final:
                self.final_norm(XG, kx, FG, fgkey, g, sg)
            else:
                dst = self.xrows(self.xs, g)
                sc.dma(lambda e, XG=XG, dst=dst: e.dma_start(out=dst, in_=XG[:]), r=[kx], w=[('xs', g)])
            if g + 1 < NG:
                self.norm_b()
            self.drain(items_per_group)
        self.barrier()

    def final_norm(self, XG, kx, FG, fgkey, g, sg):
        sc = self.sc
        SSQ, RSTD = self.SSQ2, self.RSTD2
        for j in range(4):
            jk = sg[j % 2].bitcast(BF16)
            sc.act(lambda e, j=j, jk=jk: e.activation(out=jk, in_=XG[:, j, :], func=AF.Square, accum_out=SSQ[:, j:j + 1]),
                   r=[kx], w=[('SSQ2', j), ('A', 'sg', j % 2)])
        sc.act(lambda e: e.activation(out=RSTD[:], in_=SSQ[:], func=AF.Sqrt, scale=1.0 / D, bias=EPS),
               r=[('SSQ2', j) for j in range(4)], w=['RSTD2'])
        sc.dve(lambda e: e.reciprocal(out=RSTD[:], in_=RSTD[:]), r=['RSTD2'], w=['RSTD2'])
        for j in range(4):
            sc.dve(lambda e, j=j: e.scalar_tensor_tensor(out=XG[:, j, :], in0=XG[:, j, :], scalar=RSTD[:, j:j + 1], in1=FG[:],
                                                        op0=ALU.mult, op1=ALU.mult), r=[kx, 'RSTD2', fgkey], w=[kx])
        dst = self.xrows(self.out_t, g)
        sc.dma(lambda e, dst=dst: e.dma_start(out=dst, in_=XG[:]), r=[kx], w=[('out', g)])


    def layer_params(self, l):
        sc, W = self.sc, self.W
        PWF, PWB, PSC, CWc, CB, GB8, MNBC, DNBC, LAM, LAMC = (self.PWF, self.PWB, self.PSC, self.CWc, self.CB, self.GB8,
                                                              self.MNBC, self.DNBC, self.LAM, self.LAMC)
        lam_init = 0.8 - 0.6 * math.exp(-0.3 * l)
        self.lam_init = lam_init
        for i in range(2):
            sc.dma(lambda e, i=i: e.dma_start(out=PWF[0:64, i, 0:64], in_=W["pool_w"][l, 2 * i]), pw=['PWF'])
            sc.dma(lambda e, i=i: e.dma_start(out=PWF[64:128, i, 64:128], in_=W["pool_w"][l, 2 * i + 1]), pw=['PWF'])
        sc.dve(lambda e: e.tensor_copy(out=PWB[:], in_=PWF[:]), r=['PWF'], w=['PWB'])
        sc.dma(lambda e: e.dma_start(out=PSC[:], in_=W["pool_scale"][l].rearrange("(i p) -> p i", p=128),
                                     allow_slow_non_contiguous=True), w=['PSC'])
        for tap in range(4):
            sc.dma(lambda e, tap=tap: e.dma_start(out=CWc[:, :, tap], in_=W["m_conv_w"][l, tap].rearrange("(c p) -> p c", p=128),
                                                  allow_slow_non_contiguous=True), pw=['CWc'])
        sc.dma(lambda e: e.dma_start(out=CB[:], in_=W["m_conv_b"][l].rearrange("(c p) -> p c", p=128),
                                     allow_slow_non_contiguous=True), w=['CB'])
        sc.dma(lambda e: e.dma_start(out=GB8[:], in_=W["m_gate_b"][l].rearrange("(p o) -> p o", o=1)), w=['GB8'])
        sc.dma(lambda e: e.dma_start(out=MNBC[:], in_=W["m_norm"][l:l + 1, :].partition_broadcast(128)), w=['MNBC'])
        sc.dma(lambda e: e.dma_start(out=DNBC[:], in_=W["d_norm"][l:l + 1, :].partition_broadcast(128)), w=['DNBC'])
        sc.dve(lambda e: e.tensor_scalar(out=DNBC[:], in0=DNBC[:], scalar1=float(1.0 - lam_init), scalar2=None, op0=ALU.mult),
               r=['DNBC'], w=['DNBC'])
        DNCOL = self.DNCOL
        sc.dma(lambda e: e.dma_start(out=DNCOL[:], in_=W["d_norm"][l].rearrange("(p o) -> p o", o=1)), w=['DNCOL'])
        sc.dve(lambda e: e.tensor_scalar(out=DNCOL[:], in0=DNCOL[:], scalar1=float(1.0 - lam_init), scalar2=None, op0=ALU.mult),
               r=['DNCOL'], w=['DNCOL'])
        sc.dma(lambda e: e.dma_start(out=LAM[:], in_=W["d_lambda"][l:l + 1].rearrange("o a b -> o (a b)").partition_broadcast(128)),
               w=['LAM'])
        jf = self.HN[0][:, 0:128].bitcast(F32)
        sc.dve(lambda e: e.tensor_tensor(out=jf[:, 0:64], in0=LAM[:, 0:64], in1=LAM[:, 64:128], op=ALU.mult), r=['LAM'], w=['jf', ('HN', 0)])
        sc.dve(lambda e: e.tensor_reduce(out=LAMC[:, 0:1], in_=jf[:, 0:64], axis=AX.X, op=ALU.add), r=['jf'], w=['LAMC'])
        sc.dve(lambda e: e.tensor_tensor(out=jf[:, 0:64], in0=LAM[:, 128:192], in1=LAM[:, 192:256], op=ALU.mult), r=['LAM', 'LAMC'], w=['jf', ('HN', 0)])
        sc.dve(lambda e: e.tensor_reduce(out=LAMC[:, 1:2], in_=jf[:, 0:64], axis=AX.X, op=ALU.add), r=['jf'], w=['LAMC'])
        sc.act(lambda e: e.activation(out=LAMC[:, 4:6], in_=LAMC[:, 0:2], func=AF.Exp), r=['LAMC'], w=['LAMC'])
        sc.dve(lambda e: e.tensor_tensor(out=LAMC[:, 2:3], in0=LAMC[:, 4:5], in1=LAMC[:, 5:6], op=ALU.subtract), r=['LAMC'], w=['LAMC'])
        sc.dve(lambda e: e.tensor_scalar(out=LAMC[:, 2:3], in0=LAMC[:, 2:3], scalar1=float(lam_init), scalar2=None, op0=ALU.add),
               r=['LAMC'], w=['LAMC'])
        sc.dve(lambda e: e.tensor_scalar(out=LAMC[:, 3:4], in0=LAMC[:, 2:3], scalar1=-1.0, scalar2=None, op0=ALU.mult),
               r=['LAMC'], w=['LAMC'])

    def phaseB(self, l, src_t=None, src_key='xs'):
        src_t = self.xs if src_t is None else src_t
        sc, NG, NT, PS, HT, CST = self.sc, self.NG, self.NT, self.PS, self.HT, self.CST
        ar, W = self.arena, self.W
        COS, SIN, IDB = self.COS, self.SIN, self.IDB
        G, gkey = self.load_gain(W["mix_norm"][l:l + 1, :])
        wF = ar.alloc([128, 6, 1024], BF16)
        wIF = ar.alloc([128, 64], BF16)
        winF, winIF, winT = self.winF[l], self.winIF[l], self.winT[l]
        sc.dma(lambda e: e.dma_start(out=wF, in_=winF[0:6].rearrange("u p f -> p u f")), r=[('winF', l)], w=[('A', 'wF')])
        sc.dma(lambda e: e.dma_start(out=wIF, in_=winIF[0]), r=[('winF', l)], w=[('A', 'wIF')])
        wTr = [ar.alloc([128, 8, 512], BF16) for _ in range(2)]
        zF = [ar.alloc([128, 528], F32) for _ in range(6)]
        sA = ar.alloc([128, 528], F32)
        sB = ar.alloc([128, 528], F32)
        t16 = ar.alloc([128, 16], F32)
        pooled = ar.alloc([128, 512], BF16)
        cacc = [ar.alloc([128, 512], F32) for _ in range(2)]
        yaS = ar.alloc([128, 2, 512], BF16)
        qkS = ar.alloc([128, 4, 512], BF16)
        kTokS = ar.alloc([128, 4, 256], BF16)
        vvS = ar.alloc([128, 4, 260], BF16)
        soS = ar.alloc([128, 4, 256], F32)
        gz = ar.alloc([8, 512], F32)
        ee = ar.alloc([8, 512], F32)
        cs = ar.alloc([8, 512], F32)
        TT = ar.alloc([128, 32], F32)
        zq = [ar.alloc([128, 512], F32) for _ in range(2)]
        ta = ar.alloc([128, 256], F32)
        tb = ar.alloc([128, 256], F32)
        tc = ar.alloc([128, 256], F32)
        td = ar.alloc([128, 256], F32)
        rq = [ar.alloc([128, 512], BF16) for _ in range(2)]
        dqS = ar.alloc([128, 4, 512], BF16)
        dkZS = ar.alloc([128, 8, 512], BF16)
        mkZS = ar.alloc([128, 4, 512], BF16)
        dvS = ar.alloc([128, 4, 520], BF16)
        for i in range(6):
            sc.dve(lambda e, i=i: e.memset(zF[i][:, 0:16], 0.0), w=[('A', 'zF', i)])
        sc.dve(lambda e: e.memset(vvS, 1.0), w=[('A', 'vvS')])
        sc.dve(lambda e: e.memset(dvS, 1.0), w=[('A', 'dvS')])
        sc.dve(lambda e: e.memset(dkZS, 0.0), w=[('A', 'dkZS')])
        sc.dve(lambda e: e.memset(mkZS, 0.0), w=[('A', 'mkZS')])
        COLA, NCE, WKc, AA, GB8, ONES8 = self.COLA, self.NCE, self.WKc, self.AA, self.GB8, self.ONES8
        PWB, PSC, CWc, CB = self.PWB, self.PSC, self.CWc, self.CB
        hkeys = [('HT', j) for j in range(4)]
        ctrT = 0
        rctr = 0
        items_per_group = min(7, (len(self.castq) + NG - 1) // NG) if self.castq else 0
        nxt = self.front_a(src_t, src_key, 0, G, gkey)
        self.norm_b()
        for g in range(NG):
            XG, kx = nxt
            for ci in range(6):
                bank = 4 + (ci % 4)
                for k in range(8):
                    sc.pe(lambda e, ci=ci, k=k, bank=bank: e.matmul(PS[bank][:, :], lhsT=wF[:, ci, k * 128:(k + 1) * 128], rhs=HT[:, k, :],
                                                                      start=(k == 0), stop=(k == 7)),
                          r=[('A', 'wF')] + hkeys, w=[('ps', bank)])
                sc.act(lambda e, ci=ci, bank=bank: e.copy(out=zF[ci][:, 16:528], in_=PS[bank][:, :]), r=[('ps', bank)], w=[('A', 'zF', ci)])
            for k in range(8):
                sc.pe(lambda e, k=k: e.matmul(PS[6][0:8, :], lhsT=wIF[:, k * 8:(k + 1) * 8], rhs=HT[:, k, :], start=(k == 0), stop=(k == 7)),
                      r=[('A', 'wIF')] + hkeys, w=[('ps', 6)])
            sc.dve(lambda e: e.tensor_scalar(out=gz, in0=PS[6][0:8, :], scalar1=GB8[:, 0:1], scalar2=None, op0=ALU.add),
                   r=[('ps', 6), 'GB8'], w=[('A', 'gz')])
            sc.act(lambda e: e.activation(out=ee, in_=gz, func=AF.Exp, scale=-1.0), r=[('A', 'gz')], w=[('A', 'ee')])
            sc.act(lambda e: e.activation(out=ee, in_=ee, func=AF.Ln, bias=1.0), r=[('A', 'ee')], w=[('A', 'ee')])
            for j in range(4):
                sc.dve(lambda e, j=j: e.tensor_tensor_scan(out=cs[:, j * 128:(j + 1) * 128], data0=ONES8[:, :], data1=ee[:, j * 128:(j + 1) * 128],
                                                           initial=0.0, op0=ALU.mult, op1=ALU.add),
                       r=[('A', 'ee'), 'ONES8'], w=[('A', 'cs')])
            sc.dve(lambda e: e.tensor_copy(out=cs[0:4, :], in_=gz[0:4, :]), r=[('A', 'gz'), ('A', 'cs')], w=[('A', 'cs')])
            cs8 = self.cs8
            sc.dma(lambda e, g=g: e.dma_start(out=cs8[:, g * 512:(g + 1) * 512], in_=cs), r=[('A', 'cs')], w=[('cs8', g)])
            pend_tr = [None]
            for blk in range(4):
                s = ctrT % 2
                ctrT += 1
                wt = wTr[s]
                kw = ('A', 'wT', s)
                sc.dma(lambda e, blk=blk, wt=wt: e.dma_start(out=wt, in_=winT[blk].rearrange("p (k n) -> p k n", k=8)),
                       r=[('winT', l)], w=[kw])
                for j in range(4):
                    bank = j
                    tile = 4 * g + j
                    for k in range(8):
                        sc.pe(lambda e, k=k, j=j, wt=wt, bank=bank: e.matmul(PS[bank][:, :], lhsT=HT[:, k, j * 128:(j + 1) * 128], rhs=wt[:, k, :],
                                                                             start=(k == 0), stop=(k == 7)),
                              r=[kw, ('HT', j)], w=[('ps', bank)])
                    if blk == 0:
                        sc.act(lambda e, j=j, bank=bank: e.copy(out=vvS[:, j, :].rearrange("p (h d) -> p h d", h=4)[:, :, 0:64],
                                                                in_=PS[bank][:, 0:256].rearrange("p (h d) -> p h d", h=4)),
                               r=[('ps', bank)], w=[('A', 'vvS')])
                        sc.act(lambda e, j=j, bank=bank: e.activation(out=soS[:, j, :], in_=PS[bank][:, 256:512], func=AF.Sigmoid),
                               r=[('ps', bank)], w=[('A', 'soS')])
                    elif blk == 3:
                        sc.act(lambda e, j=j, bank=bank: e.copy(out=dvS[:, j, :].rearrange("p (h d) -> p h d", h=4)[:, :, 0:128],
                                                                in_=PS[bank][:, :].rearrange("p (h d) -> p h d", h=4)),
                               r=[('ps', bank)], w=[('A', 'dvS')])
                    else:
                        ri = rctr % 2
                        rctr += 1
                        zz = zq[ri]
                        kzq = ('A', 'zq', ri)
                        sc.act(lambda e, zz=zz, bank=bank: e.copy(out=zz, in_=PS[bank][:, :]), r=[('ps', bank)], w=[kzq])
                        zv = zz.rearrange("p (h a d) -> p h a d", h=8, a=2)
                        x1, x2 = zv[:, :, 0, :], zv[:, :, 1, :]
                        cosb = bc_mid(COS[:, tile * 32:(tile + 1) * 32], 8)
                        sinb = bc_mid(SIN[:, tile * 32:(tile + 1) * 32], 8)
                        r_ = rq[ri]
                        krq = ('A', 'rq', ri)
                        rv = r_.rearrange("p (h a d) -> p h a d", h=8, a=2)
                        v3 = lambda a: a.rearrange("p (h d) -> p h d", h=8)
                        sc.dve(lambda e, x1=x1, cosb=cosb: e.tensor_tensor(out=v3(ta), in0=x1, in1=cosb, op=ALU.mult), r=[kzq, 'COS'], w=[('A', 'ta')])
                        sc.dve(lambda e, x2=x2, sinb=sinb: e.tensor_tensor(out=v3(tb), in0=x2, in1=sinb, op=ALU.mult), r=[kzq, 'SIN'], w=[('A', 'tb')])
                        sc.dve(lambda e, rv=rv: e.tensor_tensor(out=rv[:, :, 0, :], in0=v3(ta), in1=v3(tb), op=ALU.subtract),
                               r=[('A', 'ta'), ('A', 'tb')], w=[(krq[0], krq[1], krq[2], 0)])
                        sc.pool(lambda e, x2=x2, cosb=cosb: e.tensor_tensor(out=v3(tc), in0=x2, in1=cosb, op=ALU.mult), r=[kzq, 'COS'], w=[('A', 'tc')])
                        sc.pool(lambda e, x1=x1, sinb=sinb: e.tensor_tensor(out=v3(td), in0=x1, in1=sinb, op=ALU.mult), r=[kzq, 'SIN'], w=[('A', 'td')])
                        sc.pool(lambda e, rv=rv: e.tensor_tensor(out=rv[:, :, 1, :], in0=v3(tc), in1=v3(td), op=ALU.add),
                                r=[('A', 'tc'), ('A', 'td')], w=[(krq[0], krq[1], krq[2], 1)])
                        tb_ = 4 + (rctr % 4)

                        def tr_(tb_=tb_, r_=r_, krq=krq, blk=blk, j=j):
                            pq = PS[tb_][:, 0:256].bitcast(BF16).rearrange("p (c n) -> p c n", c=4)
                            for pr in range(4):
                                sc.pe(lambda e, pr=pr: e.transpose(out=pq[:, pr, :], in_=r_[:, pr * 128:(pr + 1) * 128], identity=IDB[:]),
                                      r=[(krq[0], krq[1], krq[2], 0), (krq[0], krq[1], krq[2], 1), 'IDB'], w=[('ps', tb_)])
                            if blk == 1:
                                sc.dve(lambda e: e.tensor_copy(out=dqS[:, :, j * 128:(j + 1) * 128], in_=pq), r=[('ps', tb_)], w=[('A', 'dqS')])
                            else:
                                sc.dve(lambda e: e.tensor_copy(out=dkZS[0:64, 0::2, j * 128:(j + 1) * 128], in_=pq[0:64, :, :]),
                                       r=[('ps', tb_)], w=[('A', 'dkZS')])
                                sc.dve(lambda e: e.tensor_copy(out=dkZS[64:128, 1::2, j * 128:(j + 1) * 128], in_=pq[64:128, :, :]),
                                       r=[('ps', tb_), ('A', 'dkZS')], w=[('A', 'dkZS')])
                        if pend_tr[0] is not None:
                            pend_tr[0]()
                        pend_tr[0] = tr_
            if pend_tr[0] is not None:
                pend_tr[0]()
                pend_tr[0] = None
            vv, so, dqT, dkZ, dvv = self.vv, self.so, self.dqT, self.dkZ, self.dvv
            sc.dma(lambda e, g=g: e.dma_start(out=vv[4 * g:4 * g + 4].rearrange("n p f -> p n f"), in_=vvS), r=[('A', 'vvS')], w=[('vv', g)])
            sc.dma(lambda e, g=g: e.dma_start(out=so[4 * g:4 * g + 4].rearrange("n p f -> p n f"), in_=soS), r=[('A', 'soS')], w=[('so', g)])
            sc.dma(lambda e, g=g: e.dma_start(out=dqT.ap().rearrange("c p s -> p c s")[:, :, g * 512:(g + 1) * 512], in_=dqS),
                   r=[('A', 'dqS')], w=[('dqT', g)])
            sc.dma(lambda e, g=g: e.dma_start(out=dkZ.ap().rearrange("c p s -> p c s")[:, :, g * 512:(g + 1) * 512], in_=dkZS),
                   r=[('A', 'dkZS')], w=[('dkZ', g)])
            sc.dma(lambda e, g=g: e.dma_start(out=dvv[4 * g:4 * g + 4].rearrange("n p f -> p n f"), in_=dvS), r=[('A', 'dvS')], w=[('dvv', g)])
            for j in range(4):
                sc.pe(lambda e, j=j: e.transpose(out=PS[7][:, j * 8:(j + 1) * 8], in_=cs[0:8, j * 128:(j + 1) * 128],
                                                 identity=CST[0:8, C_ID:C_ID + 8]), r=[('A', 'cs'), 'CST'], w=[('ps', 7)])
            for h in range(4):
                sc.pe(lambda e, h=h: e.matmul(PS[7][:, 64 + h * 4:64 + h * 4 + 4], lhsT=CST[0:8, C_SEL + h * 128:C_SEL + (h + 1) * 128],
                                              rhs=cs[0:8, 127::128], start=True, stop=True), r=[('A', 'cs'), 'CST'], w=[('ps', 7)])
            sc.act(lambda e: e.copy(out=TT, in_=PS[7][:, 0:32]), r=[('ps', 7)], w=[('A', 'TT')])
            sc.dve(lambda e, g=g: e.tensor_scalar(out=NCE[:, :, 4 * g:4 * g + 4], in0=PS[7][:, 64:80].rearrange("p (h j) -> p h j", h=4),
                                                  scalar1=-1.0, scalar2=None, op0=ALU.mult), r=[('ps', 7)], w=[('NCE', g)])
            TTv = TT.rearrange("p (j c) -> p j c", j=4)
            sc.dve(lambda e, g=g: e.tensor_tensor(out=COLA[:, 4 * g:4 * g + 4, :], in0=TTv[:, :, 0:4], in1=TTv[:, :, 4:8], op=ALU.add),
                   r=[('A', 'TT')], w=[('COLA', g)])
            sc.dve(lambda e, g=g: e.tensor_tensor(out=WKc[:, 4 * g:4 * g + 4, :], in0=COLA[:, 4 * g:4 * g + 4, :],
                                                  in1=NCE[:, :, 4 * g:4 * g + 4].rearrange("p h j -> p j h"), op=ALU.add),
                   r=[('COLA', g), ('NCE', g)], w=[('WKc', g)])
            sc.act(lambda e, g=g: e.activation(out=WKc[:, 4 * g:4 * g + 4, :], in_=WKc[:, 4 * g:4 * g + 4, :], func=AF.Exp, bias=math.log(0.125)),
                   r=[('WKc', g)], w=[('WKc', g)])
            sc.act(lambda e, g=g: e.activation(out=AA[0:64, :, 4 * g:4 * g + 4], in_=NCE[0:64, 0::2, 4 * g:4 * g + 4], func=AF.Exp),
                   r=[('NCE', g)], w=[('AA', g, 0)])
            sc.act(lambda e, g=g: e.activation(out=AA[64:128, :, 4 * g:4 * g + 4], in_=NCE[64:128, 1::2, 4 * g:4 * g + 4], func=AF.Exp),
                   r=[('NCE', g)], w=[('AA', g, 1)])
            if g + 1 < NG:
                nxt = self.front_a(src_t, src_key, g + 1, G, gkey)
                self.norm_b()
            for ch in range(2):
                z = zF[ch]
                kz = ('A', 'zF', ch)
                kA, kB = ('A', 'sA'), ('A', 'sB')
                sc.pool(lambda e, z=z: e.tensor_tensor(out=sA[:, 1:528], in0=z[:, 1:528], in1=z[:, 0:527], op=ALU.add), r=[kz], w=[kA])
                if ch == 0:
                    sc.pool(lambda e: e.tensor_tensor(out=sB[64:128, 3:528], in0=sA[64:128, 3:528], in1=sA[64:128, 1:526], op=ALU.add),
                            r=[kA], w=[kB])
                else:
                    sc.pool(lambda e: e.tensor_tensor(out=sB[:, 3:528], in0=sA[:, 3:528], in1=sA[:, 1:526], op=ALU.add), r=[kA], w=[kB])
                    sc.pool(lambda e: e.tensor_tensor(out=sA[:, 7:528], in0=sB[:, 7:528], in1=sB[:, 3:524], op=ALU.add), r=[kB, kA], w=[kA])
                    sc.pool(lambda e: e.tensor_tensor(out=sB[64:128, 15:528], in0=sA[64:128, 15:528], in1=sA[64:128, 7:520], op=ALU.add),
                            r=[kA, kB], w=[kB])
                for half, srcb, ksrc in ((0, sA, kA), (1, sB, kB)):
                    gi = 2 * ch + half
                    w = POOL_WINDOWS[gi]
                    p0, p1 = half * 64, half * 64 + 64
                    c0 = 0
                    if g == 0:
                        sc.dve(lambda e, srcb=srcb, p0=p0, p1=p1, gi=gi: e.tensor_tensor(
                            out=t16[p0:p1, :], in0=srcb[p0:p1, 16:32], in1=CST[p0:p1, C_ICNT + gi * 16:C_ICNT + (gi + 1) * 16], op=ALU.mult),
                            r=[ksrc, 'CST'], w=[('A', 't16', half)])
                        sc.dve(lambda e, z=z, p0=p0, p1=p1: e.tensor_tensor(out=pooled[p0:p1, 0:16], in0=t16[p0:p1, :], in1=z[p0:p1, 16:32],
                                                                            op=ALU.subtract),
                               r=[('A', 't16', half), kz], w=[('A', 'pooled', half)])
                        c0 = 16
                    sc.dve(lambda e, srcb=srcb, z=z, p0=p0, p1=p1, c0=c0, w=w: e.scalar_tensor_tensor(
                        out=pooled[p0:p1, c0:512], in0=srcb[p0:p1, 16 + c0:528], scalar=float(1.0 / w), in1=z[p0:p1, 16 + c0:528],
                        op0=ALU.mult, op1=ALU.subtract), r=[ksrc, kz], w=[('A', 'pooled', half)])
                bank = 4 + ch
                sc.pe(lambda e, ch=ch, bank=bank: e.matmul(PS[bank][:, :], lhsT=PWB[:, ch, :], rhs=pooled, start=True, stop=True),
                      r=['PWB', ('A', 'pooled', 0), ('A', 'pooled', 1)], w=[('ps', bank)])
                sc.dve(lambda e, ch=ch, bank=bank: e.tensor_scalar(out=yaS[:, ch, :], in0=PS[bank][:, :], scalar1=PSC[:, ch:ch + 1], scalar2=None,
                                                                   op0=ALU.mult), r=[('ps', bank), 'PSC'], w=[('A', 'yaS', ch)])
                sc.pool(lambda e, z=z: e.tensor_copy(out=z[:, 0:16], in_=z[:, 512:528]), r=[kz], w=[kz])
            yaT = self.yaT
            sc.dma(lambda e, g=g: e.dma_start(out=yaT.ap().rearrange("c p s -> p c s")[:, :, g * 512:(g + 1) * 512], in_=yaS),
                   r=[('A', 'yaS', 0), ('A', 'yaS', 1)], w=[('yaT', g)])
            for cc in range(4):
                ci = 2 + cc
                z = zF[ci]
                kz = ('A', 'zF', ci)
                acc = cacc[cc % 2]
                ka = ('A', 'cacc', cc % 2)
                sc.dve(lambda e, z=z, acc=acc, cc=cc: e.tensor_scalar(out=acc, in0=z[:, 13:525], scalar1=CWc[:, cc, 0:1], scalar2=None, op0=ALU.mult),
                       r=[kz, 'CWc'], w=[ka])
                for tap in range(1, 4):
                    sc.dve(lambda e, z=z, acc=acc, cc=cc, tap=tap: e.scalar_tensor_tensor(
                        out=acc, in0=z[:, 13 + tap:525 + tap], scalar=CWc[:, cc, tap:tap + 1], in1=acc, op0=ALU.mult, op1=ALU.add),
                        r=[kz, 'CWc', ka], w=[ka])
                sc.act(lambda e, acc=acc, cc=cc: e.activation(out=qkS[:, cc, :], in_=acc, func=AF.Silu, bias=CB[:, cc:cc + 1]),
                       r=[ka, 'CB'], w=[('A', 'qkS', cc)])
                sc.pool(lambda e, z=z: e.tensor_copy(out=z[:, 0:16], in_=z[:, 512:528]), r=[kz], w=[kz])
            pk = PS[6][:, :].bitcast(BF16).rearrange("p (c j n) -> p c j n", c=2, j=4)
            for cc in range(2):
                for j in range(4):
                    sc.pe(lambda e, cc=cc, j=j: e.transpose(out=pk[:, cc, j, :], in_=qkS[:, 2 + cc, j * 128:(j + 1) * 128], identity=IDB[:]),
                          r=[('A', 'qkS', 2 + cc), 'IDB'], w=[('ps', 6)])
            sc.dve(lambda e: e.tensor_copy(out=kTokS.rearrange("p j (c n) -> p c j n", c=2), in_=pk), r=[('ps', 6)], w=[('A', 'kTokS')])
            mqT, mkZ, kTok = self.mqT, self.mkZ, self.kTok
            sc.act(lambda e: e.copy(out=mkZS[0:64, 0::2, :], in_=qkS[0:64, 2:4, :]), r=[('A', 'qkS', 2), ('A', 'qkS', 3)], w=[('A', 'mkZS')])
            sc.act(lambda e: e.copy(out=mkZS[64:128, 1::2, :], in_=qkS[64:128, 2:4, :]), r=[('A', 'qkS', 2), ('A', 'qkS', 3), ('A', 'mkZS')], w=[('A', 'mkZS')])
            sc.dma(lambda e, g=g: e.dma_start(out=mqT.ap().rearrange("c p s -> p c s")[:, :, g * 512:(g + 1) * 512], in_=qkS[:, 0:2, :]),
                   r=[('A', 'qkS', 0), ('A', 'qkS', 1)], w=[('mqT', g)])
            sc.dma(lambda e, g=g: e.dma_start(out=mkZ.ap().rearrange("c p s -> p c s")[:, :, g * 512:(g + 1) * 512], in_=mkZS),
                   r=[('A', 'mkZS')], w=[('mkZ', g)])
            sc.dma(lambda e, g=g: e.dma_start(out=kTok[4 * g:4 * g + 4].rearrange("n p f -> p n f"), in_=kTokS), r=[('A', 'kTokS')], w=[('kTok', g)])
            self.drain(items_per_group)
        self.barrier()


    def mlstm_phase(self, l):
        sc, NG, NT, PS, CST, IDB = self.sc, self.NG, self.NT, self.PS, self.CST, self.IDB
        MSKIP = os.environ.get("MSKIP", "")
        sc0 = sc

        class _SC:
            def __getattr__(s_, n):
                return getattr(sc0, n)

            def pe(s_, fn, r=(), w=(), tag=''):
                if tag and tag in MSKIP:
                    return None
                return sc0.pe(fn, r, w)
        sc = _SC()
        ar = self.arena
        COLA, WKc, AA, MNBC = self.COLA, self.WKc, self.AA, self.MNBC
        ld = []
        for i in range(2):
            ld.append(dict(qT=ar.alloc([128, 2, 512], BF16), kT=ar.alloc([128, 4, 512], BF16), kTok=ar.alloc([128, 4, 256], BF16),
                           vv=ar.alloc([128, 4, 260], BF16), so=ar.alloc([128, 4, 256], F32), cs=ar.alloc([8, 512], F32)))
        Dm = [ar.alloc([128, 4, 128], F32) for _ in range(2)]
        EB = [ar.alloc([128, 2, 128], F32) for _ in range(2)]
        PT = [ar.alloc([128, 4, 128], BF16) for _ in range(2)]
        qs = [ar.alloc([128, 4, 128], BF16) for _ in range(2)]
        kw = [ar.alloc([128, 4, 64], BF16) for _ in range(2)]
        Sf = ar.alloc([128, 2, 65], F32)
        Sb = ar.alloc([128, 2, 65], BF16)
        gso = ar.alloc([128, 256], F32)
        den = ar.alloc([128, 4], F32)
        rec = ar.alloc([128, 4], F32)
        hN = ar.alloc([128, 4, 64], F32)
        msum = ar.alloc([128, 4], F32)
        cen = ar.alloc([128, 4, 64], F32)
        sq = ar.alloc([128, 4, 64], F32)
        vs = ar.alloc([128, 4], F32)
        rstd = ar.alloc([128, 4], F32)
        ybf = ar.alloc([128, 256], F32)
        ybb = [ar.alloc([128, 256], BF16) for _ in range(2)]
        ybS = [ar.alloc([128, 2, 512], BF16) for _ in range(2)]
        sc.dve(lambda e: e.memset(Sf, 0.0), w=[('A', 'Sf')])
        for i_ in range(2):
            sc.dve(lambda e, i_=i_: e.memset(qs[i_], 0.0), w=[('A', 'qs', i_)])
        mqT, mkZ, kTok, vv, so, cs8, ybT = self.mqT, self.mkZ, self.kTok, self.vv, self.so, self.cs8, self.ybT

        def load_group(g):
            L = ld[g % 2]
            k = ('A', 'ld', g % 2)
            sl = slice(g * 512, (g + 1) * 512)
            sc.dma(lambda e: e.dma_start(out=L['qT'], in_=mqT.ap().rearrange("c p s -> p c s")[:, :, sl]), r=[('mqT', g)], w=[(k, 'qT')])
            sc.dma(lambda e: e.dma_start(out=L['kT'], in_=mkZ.ap().rearrange("c p s -> p c s")[:, :, sl]), r=[('mkZ', g)], w=[(k, 'kT')])
            sc.dma(lambda e: e.dma_start(out=L['kTok'], in_=kTok[4 * g:4 * g + 4].rearrange("n p f -> p n f")), r=[('kTok', g)], w=[(k, 'kTok')])
            sc.dma(lambda e: e.dma_start(out=L['vv'], in_=vv[4 * g:4 * g + 4].rearrange("n p f -> p n f")), r=[('vv', g)], w=[(k, 'vv')])
            sc.dma(lambda e: e.dma_start(out=L['so'], in_=so[4 * g:4 * g + 4].rearrange("n p f -> p n f")), r=[('so', g)], w=[(k, 'so')])
            sc.dma(lambda e: e.dma_start(out=L['cs'], in_=cs8[:, sl]), r=[('cs8', g)], w=[(k, 'cs')])

        def K(g, n):
            return (('A', 'ld', g % 2), n)

        def stage1(c):
            g, j, p = c // 4, c % 4, c % 2
            if j == 0:
                load_group(g)
            L = ld[g % 2]
            cols = slice(j * 128, (j + 1) * 128)
            for h in range(4):
                pair, hb = h // 2, (h % 2) * 64
                sc.pe(lambda e, h=h, pair=pair, hb=hb: e.matmul(PS[p][:, h * 128:(h + 1) * 128], lhsT=L['kT'][:, h, cols],
                                                               rhs=L['qT'][:, pair, cols], start=True, stop=True),
                      r=[K(g, 'kT'), K(g, 'qT')], w=[('ps', p)], tag='S')
            for h in range(4):
                sc.pe(lambda e, h=h: e.matmul(PS[2 + p][:, h * 128:(h + 1) * 128], lhsT=CST[0:8, C_SEL + h * 128:C_SEL + (h + 1) * 128],
                                              rhs=L['cs'][0:8, cols], start=True, stop=False), r=[K(g, 'cs'), 'CST'], w=[('ps', 2 + p)], tag='C')
                sc.pe(lambda e, h=h: e.matmul(PS[2 + p][:, h * 128:(h + 1) * 128], lhsT=CST[:, C_ID:C_ID + 128],
                                              rhs=CST[:, C_MP:C_MP + 128], start=False, stop=True), r=['CST'], w=[('ps', 2 + p)], tag='K')
            for h in range(4):
                sc.pe(lambda e, h=h: e.matmul(PS[4][:, h * 128:(h + 1) * 128], lhsT=CST[0:8, C_SEL + h * 128:C_SEL + (h + 1) * 128],
                                              rhs=L['cs'][0:8, cols], start=True, stop=True),
                      r=[K(g, 'cs'), 'CST'], w=[('ps', 4)], tag='E')
            for h in range(4):
                sc.act(lambda e, h=h: e.activation(out=Dm[p][:, h, :], in_=PS[2 + p][:, h * 128:(h + 1) * 128], func=AF.Exp,
                                                   scale=-1.0, bias=COLA[:, c, h:h + 1]),
                       r=[('ps', 2 + p), ('COLA', g)], w=[('A', 'Dm', p, h)])
            for half in range(2):
                sc.act(lambda e, half=half: e.activation(
                    out=EB[p][half * 64:(half + 1) * 64, :, :],
                    in_=PS[4][half * 64:(half + 1) * 64, :].rearrange("p (h n) -> p h n", h=4)[:, half::2, :], func=AF.Exp, scale=-1.0),
                    r=[('ps', 4)], w=[('A', 'EB', p, half)])
            sc.dve(lambda e: e.scalar_tensor_tensor(out=PT[p].rearrange("p h n -> p (h n)"), in0=PS[p][:, :], scalar=0.125,
                                                    in1=Dm[p].rearrange("p h n -> p (h n)"), op0=ALU.mult, op1=ALU.mult),
                   r=[('ps', p)] + [('A', 'Dm', p, h) for h in range(4)], w=[('A', 'PT', p)])
            for half in range(2):
                ps_, pe_ = half * 64, half * 64 + 64
                sc.dve(lambda e, half=half, ps_=ps_, pe_=pe_: e.tensor_tensor(out=qs[p][ps_:pe_, half::2, :], in0=L['qT'][ps_:pe_, :, cols],
                                                                              in1=EB[p][ps_:pe_, :, :], op=ALU.mult),
                       r=[K(g, 'qT'), ('A', 'EB', p, half), ('A', 'qs', p)], w=[('A', 'qs', p)])
            sc.dve(lambda e: e.tensor_tensor(out=kw[p], in0=L['kTok'][:, j, :].rearrange("p (h d) -> p h d", h=4),
                                             in1=bc_last(WKc[:, c, :], 64), op=ALU.mult),
                   r=[K(g, 'kTok'), ('WKc', g)], w=[('A', 'kw', p)])

        def stage2(c):
            g, j, p = c // 4, c % 4, c % 2
            L = ld[g % 2]
            for h in range(4):
                pair, hb = h // 2, (h % 2) * 64
                sc.pe(lambda e, h=h: e.matmul(PS[6 + p][:, h * 65:(h + 1) * 65], lhsT=PT[p][:, h, :], rhs=L['vv'][:, j, h * 65:(h + 1) * 65],
                                              start=True, stop=(c == 0)), r=[('A', 'PT', p), K(g, 'vv')], w=[('ps', 6 + p)], tag='O')
                if c > 0:
                    sc.pe(lambda e, h=h, pair=pair, hb=hb: e.matmul(PS[6 + p][:, h * 65:(h + 1) * 65], lhsT=qs[p][:, h, :],
                                                                   rhs=Sb[:, pair, :], start=False, stop=True),
                          r=[('A', 'qs', p), ('A', 'Sb')], w=[('ps', 6 + p)], tag='I')
            if c < NT - 1:
                for h in range(4):
                    pair, hb = h // 2, (h % 2) * 64
                    sc.pe(lambda e, h=h, pair=pair, hb=hb: e.matmul(PS[5][hb:hb + 64, p * 130 + pair * 65:p * 130 + (pair + 1) * 65],
                                                                   lhsT=kw[p][:, h, :], rhs=L['vv'][:, j, h * 65:(h + 1) * 65],
                                                                   start=True, stop=True),
                          r=[('A', 'kw', p), K(g, 'vv')], w=[('ps', 5)], tag='U')
                for pair in range(2):
                    sc.dve(lambda e, pair=pair: e.scalar_tensor_tensor(out=Sf[:, pair, :], in0=Sf[:, pair, :], scalar=AA[:, pair, c:c + 1],
                                                                       in1=PS[5][:, p * 130 + pair * 65:p * 130 + (pair + 1) * 65],
                                                                       op0=ALU.mult, op1=ALU.add),
                           r=[('ps', 5), ('AA', g, 0), ('AA', g, 1), ('A', 'Sf')], w=[('A', 'Sf')])
                sc.act(lambda e: e.copy(out=Sb, in_=Sf), r=[('A', 'Sf')], w=[('A', 'Sb')])
            psO = PS[6 + p][:, 0:260].rearrange("p (h d) -> p h d", h=4)
            kO = ('ps', 6 + p)
            sc.dve(lambda e: e.tensor_scalar(out=den, in0=psO[:, :, 64], scalar1=-1.0, scalar2=None, op0=ALU.mult), r=[kO], w=[('A', 'den')])
            sc.dve(lambda e: e.tensor_tensor(out=den, in0=den, in1=psO[:, :, 64], op=ALU.max), r=[kO, ('A', 'den')], w=[('A', 'den')])
            sc.dve(lambda e: e.tensor_scalar(out=den, in0=den, scalar1=1.0, scalar2=None, op0=ALU.max), r=[('A', 'den')], w=[('A', 'den')])
            sc.dve(lambda e: e.reciprocal(out=rec, in_=den), r=[('A', 'den')], w=[('A', 'rec')])
            sc.dve(lambda e: e.tensor_tensor(out=hN, in0=psO[:, :, 0:64], in1=bc_last(rec, 64), op=ALU.mult), r=[kO, ('A', 'rec')], w=[('A', 'hN')])
            sc.dve(lambda e: e.tensor_reduce(out=msum, in_=hN, axis=AX.X, op=ALU.add), r=[('A', 'hN')], w=[('A', 'msum')])
            sc.dve(lambda e: e.tensor_scalar(out=msum, in0=msum, scalar1=-1.0 / 64, scalar2=None, op0=ALU.mult), r=[('A', 'msum')], w=[('A', 'msum')])
            sc.dve(lambda e: e.tensor_tensor(out=cen, in0=hN, in1=bc_last(msum, 64), op=ALU.add), r=[('A', 'hN'), ('A', 'msum')], w=[('A', 'cen')])
            sc.dve(lambda e: e.tensor_tensor(out=sq, in0=cen, in1=cen, op=ALU.mult), r=[('A', 'cen')], w=[('A', 'sq')])
            sc.dve(lambda e: e.tensor_reduce(out=vs, in_=sq, axis=AX.X, op=ALU.add), r=[('A', 'sq')], w=[('A', 'vs')])
            sc.act(lambda e: e.activation(out=rstd, in_=vs, func=AF.Ln, scale=1.0 / 64, bias=EPS), r=[('A', 'vs')], w=[('A', 'rstd')])
            sc.act(lambda e: e.activation(out=rstd, in_=rstd, func=AF.Exp, scale=-0.5), r=[('A', 'rstd')], w=[('A', 'rstd')])
            sc.pool(lambda e: e.tensor_tensor(out=gso, in0=L['so'][:, j, :], in1=MNBC[:, :], op=ALU.mult), r=[K(g, 'so'), 'MNBC'], w=[('A', 'gso')])
            sc.dve(lambda e: e.tensor_tensor(out=ybf.rearrange("p (h d) -> p h d", h=4), in0=cen, in1=bc_last(rstd, 64), op=ALU.mult),
                   r=[('A', 'cen'), ('A', 'rstd')], w=[('A', 'ybf')])
            yb = ybb[p]
            sc.dve(lambda e: e.tensor_tensor(out=yb, in0=ybf, in1=gso, op=ALU.mult), r=[('A', 'ybf'), ('A', 'gso')], w=[('A', 'yb', p)])

        def stage2_tr(c):
            g, j, p = c // 4, c % 4, c % 2
            yb = ybb[p]
            pT = PS[5][:, 384:512].bitcast(BF16).rearrange("p (c n) -> p c n", c=2)
            for ch in range(2):
                sc.pe(lambda e, ch=ch: e.transpose(out=pT[:, ch, :], in_=yb[:, ch * 128:(ch + 1) * 128], identity=IDB[:]),
                      r=[('A', 'yb', p), 'IDB'], w=[('ps', 5)], tag='T')
            sc.act(lambda e: e.copy(out=ybS[g % 2][:, :, j * 128:(j + 1) * 128], in_=pT), r=[('ps', 5)], w=[('A', 'ybS', g % 2)])
            if j == 3:
                sc.dma(lambda e: e.dma_start(out=ybT.ap().rearrange("c p s -> p c s")[:, :, g * 512:(g + 1) * 512], in_=ybS[g % 2]),
                       r=[('A', 'ybS', g % 2)], w=[('ybT', g)])

        stage1(0)
        for c in range(NT):
            if c + 1 < NT:
                stage1(c + 1)
            if c > 0:
                stage2_tr(c - 1)
            stage2(c)
        stage2_tr(NT - 1)
        if self.debug:
            lp = (NT - 1) % 2
            dbg = self.nc.dram_tensor("dbgM", [128, 2048], F32, kind="ExternalOutput")
            dbb = self.nc.dram_tensor("dbgMb", [128, 2048], BF16, kind="ExternalOutput")
            items = [(Dm[lp].rearrange("p h n -> p (h n)"), 0, 512, [('A', 'Dm', lp, h) for h in range(4)]),
                     (EB[lp].rearrange("p a n -> p (a n)"), 512, 256, [('A', 'EB', lp, 0), ('A', 'EB', lp, 1)]),
                     (hN.rearrange("p h n -> p (h n)"), 768, 256, [('A', 'hN')]),
                     (cen.rearrange("p h n -> p (h n)"), 1024, 256, [('A', 'cen')]),
                     (ybf, 1280, 256, [('A', 'ybf')]),
                     (Sf.rearrange("p a n -> p (a n)"), 1536, 130, [('A', 'Sf')]),
                     (den, 1700, 4, [('A', 'den')]), (rec, 1704, 4, [('A', 'rec')]), (msum, 1708, 4, [('A', 'msum')]),
                     (vs, 1712, 4, [('A', 'vs')]), (rstd, 1716, 4, [('A', 'rstd')]), (gso, 1792, 256, [('A', 'gso')])]
            for ap_, off, n, keys in items:
                sc.dma(lambda e, ap_=ap_, off=off, n=n: e.dma_start(out=dbg[:, off:off + n], in_=ap_), r=keys, pw=['dbgM'])
            itb = [(PT[lp].rearrange("p h n -> p (h n)"), 0, 512, [('A', 'PT', lp)]),
                   (qs[lp].rearrange("p h n -> p (h n)"), 512, 512, [('A', 'qs', lp)]),
                   (kw[lp].rearrange("p h n -> p (h n)"), 1024, 256, [('A', 'kw', lp)]),
                   (Sb.rearrange("p a n -> p (a n)"), 1280, 130, [('A', 'Sb')]),
                   (ybb[lp], 1536, 256, [('A', 'yb', lp)])]
            for ap_, off, n, keys in itb:
                sc.dma(lambda e, ap_=ap_, off=off, n=n: e.dma_start(out=dbb[:, off:off + n], in_=ap_), r=keys, pw=['dbgMb'])
        self.barrier()

    def attn_phase(self, l):
        sc, NG, NT, PS, IDB, MNB, S = self.sc, self.NG, self.NT, self.PS, self.IDB, self.MNB, self.S
        ar = self.arena
        LAMC, DNBC = self.LAMC, self.DNBC
        hd = [dict(qT=ar.alloc([128, S], BF16), kT=ar.alloc([128, 2, S], BF16), vv=ar.alloc([128, NT, 130], BF16)) for _ in range(2)]
        PTb = [ar.alloc([128, 2, 256], BF16) for _ in range(3)]
        ycS = [ar.alloc([128, S], BF16) for _ in range(2)]
        accS = [ar.alloc([128, 2, 132], F32) for _ in range(2)]
        rec2 = [ar.alloc([128, 2], F32) for _ in range(2)]
        l2 = [ar.alloc([128, 1], F32) for _ in range(2)]
        o1 = [ar.alloc([128, 128], F32) for _ in range(2)]
        oo = [ar.alloc([128, 128], F32) for _ in range(2)]
        sqj = [ar.alloc([128, 128], F32) for _ in range(2)]
        ssq = [ar.alloc([128, 1], F32) for _ in range(2)]
        rstd = [ar.alloc([128, 1], F32) for _ in range(2)]
        yc = [ar.alloc([128, 128], BF16) for _ in range(2)]
        deferred = []
        dqT, dkZ, dvv, ycT = self.dqT, self.dkZ, self.dvv, self.ycT
        gk = lambda n: [(n, g) for g in range(NG)]
        uctr = [0]
        for h in range(4):
            H = hd[h % 2]
            kh = ('A', 'hd', h % 2)
            sc.dma(lambda e, h=h, H=H: e.dma_start(out=H['qT'], in_=dqT[h]), r=gk('dqT'), w=[(kh, 'q')])
            sc.dma(lambda e, h=h, H=H: e.dma_start(out=H['kT'], in_=dkZ[2 * h:2 * h + 2].rearrange("c p s -> p c s")), r=gk('dkZ'), w=[(kh, 'k')])
            sc.dma(lambda e, h=h, H=H: e.dma_start(out=H['vv'], in_=dvv.ap().rearrange("n p (h f) -> p n h f", h=4)[:, :, h, :]),
                   r=gk('dvv'), w=[(kh, 'v')])
            units = [(sb, j) for sb in range(NT // 2) for j in range(2 * sb + 2)]
            ycs = ycS[h % 2]
            kyc = ('A', 'ycS', h % 2)

            def score(sb, j, r, H=H, kh=kh):
                bank = r % 2
                psS = PS[bank][:, :].rearrange("p (m n) -> p m n", m=2)
                t0, t1 = 2 * sb, 2 * sb + 1
                pt = PTb[r % 3]
                for m in range(2):
                    kTm = H['kT'][:, m, j * 128:(j + 1) * 128]
                    if j <= t0:
                        sc.pe(lambda e, m=m, kTm=kTm: e.matmul(psS[:, m, :], lhsT=kTm, rhs=H['qT'][:, sb * 256:(sb + 1) * 256],
                                                               start=True, stop=(j < t0)), r=[(kh, 'q'), (kh, 'k')], w=[('ps', bank)])
                        if j == t0:
                            sc.pe(lambda e, m=m: e.matmul(psS[:, m, 0:128], lhsT=IDB[:], rhs=MNB[:], start=False, stop=True),
                                  r=['IDB', 'MNB'], w=[('ps', bank)])
                    else:
                        sc.pe(lambda e, m=m, kTm=kTm: e.matmul(psS[:, m, 128:256], lhsT=kTm, rhs=H['qT'][:, t1 * 128:(t1 + 1) * 128],
                                                               start=True, stop=False), r=[(kh, 'q'), (kh, 'k')], w=[('ps', bank)])
                        sc.pe(lambda e, m=m: e.matmul(psS[:, m, 128:256], lhsT=IDB[:], rhs=MNB[:], start=False, stop=True),
                              r=['IDB', 'MNB'], w=[('ps', bank)])
                if j <= t0:
                    sc.act(lambda e: e.activation(out=pt, in_=psS, func=AF.Exp, scale=0.125), r=[('ps', bank)], w=[('A', 'PTb', r % 3)])
                else:
                    sc.act(lambda e: e.activation(out=pt[:, :, 128:256], in_=psS[:, :, 128:256], func=AF.Exp, scale=0.125),
                           r=[('ps', bank)], w=[('A', 'PTb', r % 3)])

            def pv(sb, j, r, H=H, kh=kh):
                t0, t1 = 2 * sb, 2 * sb + 1
                pt = PTb[r % 3]
                for m in range(2):
                    for ti in range(2):
                        if ti == 0 and j > t0:
                            continue
                        bank = 2 + ti * 2 + m
                        last = t0 if ti == 0 else t1
                        sc.pe(lambda e, m=m, ti=ti, bank=bank, last=last: e.matmul(
                            PS[bank][:, 0:129], lhsT=pt[:, m, ti * 128:(ti + 1) * 128], rhs=H['vv'][:, j, 0:129], start=(j == 0), stop=(j == last)),
                            r=[('A', 'PTb', r % 3), (kh, 'v')], w=[('ps', bank)])
                for ti in range(2):
                    if j == (t0 if ti == 0 else t1):
                        epilogue(sb, ti)
                for d in deferred:
                    d[0] -= 1
                while deferred and deferred[0][0] <= 0:
                    deferred.pop(0)[1]()

            def epilogue(sb, ti, ycs=ycs, kyc=kyc):
                tile = 2 * sb + ti
                q = tile % 2
                b0, b1 = 2 + ti * 2, 3 + ti * 2
                kacc = ('A', 'accS', q)
                sc.dve(lambda e: e.tensor_copy(out=accS[q][:, 0, 0:129], in_=PS[b0][:, 0:129]), r=[('ps', b0)], w=[(kacc, 0)])
                sc.dve(lambda e: e.tensor_copy(out=accS[q][:, 1, 0:129], in_=PS[b1][:, 0:129]), r=[('ps', b1)], w=[(kacc, 1)])
                sc.dve(lambda e: e.reciprocal(out=rec2[q], in_=accS[q][:, :, 128]), r=[(kacc, 0), (kacc, 1)], w=[('A', 'rec2', q)])
                sc.dve(lambda e: e.tensor_tensor(out=l2[q], in0=rec2[q][:, 1:2], in1=LAMC[:, 3:4], op=ALU.mult), r=[('A', 'rec2', q), 'LAMC'],
                       w=[('A', 'l2', q)])
                sc.dve(lambda e: e.tensor_scalar(out=o1[q], in0=accS[q][:, 0, 0:128], scalar1=rec2[q][:, 0:1], scalar2=None, op0=ALU.mult),
                       r=[(kacc, 0), ('A', 'rec2', q)], w=[('A', 'o1', q)])
                sc.dve(lambda e: e.scalar_tensor_tensor(out=oo[q], in0=accS[q][:, 1, 0:128], scalar=l2[q][:, 0:1], in1=o1[q], op0=ALU.mult, op1=ALU.add),
                       r=[(kacc, 1), ('A', 'l2', q), ('A', 'o1', q)], w=[('A', 'oo', q)])
                sc.dve(lambda e: e.tensor_tensor(out=sqj[q], in0=oo[q], in1=oo[q], op=ALU.mult), r=[('A', 'oo', q)], w=[('A', 'sqj', q)])
                sc.dve(lambda e: e.tensor_reduce(out=ssq[q], in_=sqj[q], axis=AX.X, op=ALU.add), r=[('A', 'sqj', q)], w=[('A', 'ssq', q)])

                def partB():
                    sc.act(lambda e: e.activation(out=rstd[q], in_=ssq[q], func=AF.Ln, scale=1.0 / 128, bias=EPS), r=[('A', 'ssq', q)],
                           w=[('A', 'rstd', q)])
                    sc.act(lambda e: e.activation(out=rstd[q], in_=rstd[q], func=AF.Exp, scale=-0.5), r=[('A', 'rstd', q)], w=[('A', 'rstd', q)])
                    y = yc[q]
                    ky = ('A', 'yc', q)
                    sc.dve(lambda e: e.scalar_tensor_tensor(out=y, in0=oo[q], scalar=rstd[q][:, 0:1], in1=DNBC[:, :], op0=ALU.mult, op1=ALU.mult),
                           r=[('A', 'oo', q), ('A', 'rstd', q), 'DNBC'], w=[ky])
                    tb = 6 + q
                    pT = PS[tb][:, 0:64].bitcast(BF16)
                    sc.pe(lambda e: e.transpose(out=pT, in_=y, identity=IDB[:]), r=[ky, 'IDB'], w=[('ps', tb)])
                    sc.dve(lambda e: e.tensor_copy(out=ycs[:, tile * 128:(tile + 1) * 128], in_=pT), r=[('ps', tb)], w=[kyc])
                deferred.append([3, partB])

            r0 = uctr[0]
            score(units[0][0], units[0][1], r0)
            for i, (sb, j) in enumerate(units):
                if i + 1 < len(units):
                    score(units[i + 1][0], units[i + 1][1], r0 + i + 1)
                pv(sb, j, r0 + i)
            uctr[0] = r0 + len(units)
            while deferred:
                deferred.pop(0)[1]()
            sc.dma(lambda e, h=h, ycs=ycs: e.dma_start(out=ycT[h], in_=ycs), r=[kyc], w=[('ycT', h)])
        self.barrier()

    def attn_phase2(self, l):
        sc, NG, NT, PS, IDB, MNB, S = self.sc, self.NG, self.NT, self.PS, self.IDB, self.MNB, self.S
        ar = self.arena
        LAMC, DNCOL = self.LAMC, self.DNCOL
        hd = [dict(qT=ar.alloc([128, S], BF16), kT=ar.alloc([128, 2, S], BF16), vv=ar.alloc([128, NT, 130], BF16)) for _ in range(2)]
        PTb = [ar.alloc([128, 2, 256], BF16) for _ in range(3)]
        ycS = [ar.alloc([128, S], BF16) for _ in range(2)]
        accD = [ar.alloc([128, 2, 256], F32) for _ in range(2)]
        accS = [ar.alloc([128, 2, 256], F32) for _ in range(2)]
        rec = ar.alloc([128, 2, 256], F32)
        o1 = ar.alloc([128, 256], F32)
        oo = ar.alloc([128, 256], F32)
        sqh = ar.alloc([128, 256], BF16)
        sql = ar.alloc([128, 256], BF16)
        rsb = ar.alloc([128, 256], F32)
        ONESB = ar.alloc([128, 128], BF16)
        sc.dve(lambda e: e.memset(ONESB, 1.0), w=[('A', 'ONESB')])
        deferred = []
        dqT, dkZ, dvv, ycT = self.dqT, self.dkZ, self.dvv, self.ycT
        gk = lambda n: [(n, g) for g in range(NG)]
        uctr = [0]
        for h in range(4):
            H = hd[h % 2]
            kh = ('A', 'hd', h % 2)
            sc.dma(lambda e, h=h, H=H: e.dma_start(out=H['qT'], in_=dqT[h]), r=gk('dqT'), w=[(kh, 'q')])
            sc.dma(lambda e, h=h, H=H: e.dma_start(out=H['kT'], in_=dkZ[2 * h:2 * h + 2].rearrange("c p s -> p c s")), r=gk('dkZ'), w=[(kh, 'k')])
            sc.dma(lambda e, h=h, H=H: e.dma_start(out=H['vv'], in_=dvv.ap().rearrange("n p (h f) -> p n h f", h=4)[:, :, h, :]),
                   r=gk('dvv'), w=[(kh, 'v')])
            units = [(sb, j) for sb in range(NT // 2) for j in range(2 * sb + 2)]
            ycs = ycS[h % 2]
            kyc = ('A', 'ycS', h % 2)

            def score(sb, j, r, H=H, kh=kh):
                bank = r % 2
                psS = PS[bank][:, :].rearrange("p (m n) -> p m n", m=2)
                t0, t1 = 2 * sb, 2 * sb + 1
                pt = PTb[r % 3]
                for m in range(2):
                    kTm = H['kT'][:, m, j * 128:(j + 1) * 128]
                    if j <= t0:
                        sc.pe(lambda e, m=m, kTm=kTm: e.matmul(psS[:, m, :], lhsT=kTm, rhs=H['qT'][:, sb * 256:(sb + 1) * 256],
                                                               start=True, stop=(j < t0)), r=[(kh, 'q'), (kh, 'k')], w=[('ps', bank)])
                        if j == t0:
                            sc.pe(lambda e, m=m: e.matmul(psS[:, m, 0:128], lhsT=IDB[:], rhs=MNB[:], start=False, stop=True),
                                  r=['IDB', 'MNB'], w=[('ps', bank)])
                    else:
                        sc.pe(lambda e, m=m, kTm=kTm: e.matmul(psS[:, m, 128:256], lhsT=kTm, rhs=H['qT'][:, t1 * 128:(t1 + 1) * 128],
                                                               start=True, stop=False), r=[(kh, 'q'), (kh, 'k')], w=[('ps', bank)])
                        sc.pe(lambda e, m=m: e.matmul(psS[:, m, 128:256], lhsT=IDB[:], rhs=MNB[:], start=False, stop=True),
                              r=['IDB', 'MNB'], w=[('ps', bank)])
                if j <= t0:
                    sc.act(lambda e: e.activation(out=pt, in_=psS, func=AF.Exp, scale=0.125), r=[('ps', bank)], w=[('A', 'PTb', r % 3)])
                else:
                    sc.act(lambda e: e.activation(out=pt[:, :, 128:256], in_=psS[:, :, 128:256], func=AF.Exp, scale=0.125),
                           r=[('ps', bank)], w=[('A', 'PTb', r % 3)])

            def pv(sb, j, r, H=H, kh=kh):
                t0, t1 = 2 * sb, 2 * sb + 1
                q = sb % 2
                pt = PTb[r % 3]
                kpt = ('A', 'PTb', r % 3)
                cols = slice(0, 256) if j <= t0 else slice(128, 256)
                for m in range(2):
                    sc.pe(lambda e, m=m: e.matmul(PS[2 + m][:, cols], lhsT=H['vv'][:, j, 0:128], rhs=pt[:, m, cols], start=(j == 0), stop=(j == t1)),
                          r=[kpt, (kh, 'v')], w=[('ps', 2 + m)])
                for m in range(2):
                    sc.pe(lambda e, m=m: e.matmul(PS[4 + m][:, cols], lhsT=ONESB, rhs=pt[:, m, cols], start=(j == 0), stop=(j == t1)),
                          r=[kpt, ('A', 'ONESB')], w=[('ps', 4 + m)])
                if j == t1:
                    epilogue(sb)
                for d in deferred:
                    d[0] -= 1
                while deferred and deferred[0][0] <= 0:
                    deferred.pop(0)[1]()

            def epilogue(sb, ycs=ycs, kyc=kyc):
                q = sb % 2
                kas = ('A', 'accS', q)
                for m in range(2):
                    sc.dve(lambda e, m=m: e.tensor_copy(out=accS[q][:, m, :], in_=PS[2 + m][:, 0:256]), r=[('ps', 2 + m)], w=[(kas, m)])
                for m in range(2):
                    sc.act(lambda e, m=m: e.copy(out=accD[q][:, m, :], in_=PS[4 + m][:, 0:256]), r=[('ps', 4 + m)], w=[('A', 'accD', q, m)])

                def partB():
                    sc.dve(lambda e: e.reciprocal(out=rec.rearrange("p m n -> p (m n)"), in_=accD[q].rearrange("p m n -> p (m n)")),
                           r=[('A', 'accD', q, 0), ('A', 'accD', q, 1)], w=[('A', 'rec', 0), ('A', 'rec', 1)])
                    sc.dve(lambda e: e.tensor_tensor(out=o1, in0=accS[q][:, 0, :], in1=rec[:, 0, :], op=ALU.mult),
                           r=[(kas, 0), ('A', 'rec', 0)], w=[('A', 'o1')])
                    sc.dve(lambda e: e.tensor_tensor(out=rec[:, 1, :], in0=accS[q][:, 1, :], in1=rec[:, 1, :], op=ALU.mult),
                           r=[(kas, 1), ('A', 'rec', 1)], w=[('A', 'rec', 1)])
                    sc.dve(lambda e: e.scalar_tensor_tensor(out=oo, in0=rec[:, 1, :], scalar=LAMC[:, 3:4], in1=o1, op0=ALU.mult, op1=ALU.add),
                           r=[('A', 'rec', 1), ('A', 'o1'), 'LAMC'], w=[('A', 'oo')])
                    sc.dve(lambda e: e.tensor_tensor(out=o1, in0=oo, in1=oo, op=ALU.mult), r=[('A', 'oo'), ('A', 'o1')], w=[('A', 'o1')])
                    sc.pool(lambda e: e.tensor_copy(out=sqh, in_=o1), r=[('A', 'o1')], w=[('A', 'sqh')])
                    sc.pool(lambda e: e.tensor_tensor(out=sql, in0=o1, in1=sqh, op=ALU.subtract), r=[('A', 'o1'), ('A', 'sqh')], w=[('A', 'sql')])
                    sc.pe(lambda e: e.matmul(PS[7][:, 0:256], lhsT=ONESB, rhs=sqh, start=True, stop=False), r=[('A', 'ONESB'), ('A', 'sqh')],
                          w=[('ps', 7)])
                    sc.pe(lambda e: e.matmul(PS[7][:, 0:256], lhsT=ONESB, rhs=sql, start=False, stop=True), r=[('A', 'ONESB'), ('A', 'sql')],
                          w=[('ps', 7)])
                    sc.act(lambda e: e.activation(out=rsb, in_=PS[7][:, 0:256], func=AF.Ln, scale=1.0 / 128, bias=EPS), r=[('ps', 7)],
                           w=[('A', 'rsb')])
                    sc.act(lambda e: e.activation(out=rsb, in_=rsb, func=AF.Exp, scale=-0.5), r=[('A', 'rsb')], w=[('A', 'rsb')])
                    sc.dve(lambda e: e.scalar_tensor_tensor(out=ycs[:, sb * 256:(sb + 1) * 256], in0=oo, scalar=DNCOL[:, 0:1], in1=rsb,
                                                            op0=ALU.mult, op1=ALU.mult), r=[('A', 'oo'), ('A', 'rsb'), 'DNCOL'], w=[kyc])
                deferred.append([2, partB])

            r0 = uctr[0]
            score(units[0][0], units[0][1], r0)
            for i, (sb, j) in enumerate(units):
                if i + 1 < len(units):
                    score(units[i + 1][0], units[i + 1][1], r0 + i + 1)
                pv(sb, j, r0 + i)
            uctr[0] = r0 + len(units)
            while deferred:
                deferred.pop(0)[1]()
            sc.dma(lambda e, h=h, ycs=ycs: e.dma_start(out=ycT[h], in_=ycs), r=[kyc], w=[('ycT', h)])
        self.barrier()

    def phaseD(self, l):
        sc, NG, NT, PS, HT = self.sc, self.NG, self.NT, self.PS, self.HT
        ar, W = self.arena, self.W
        G, gkey = self.load_gain(W["mix_norm"][l:l + 1, :])
        NRG = 10
        wG = [ar.alloc([128, 1024], BF16) for _ in range(NRG)]
        pA = ar.alloc([128, 8, 256], BF16)
        pB = ar.alloc([128, 8, 256], BF16)
        pC = ar.alloc([128, 8, 512], BF16)
        wO = ar.alloc([128, 2, 4096], BF16)
        yl = [dict(a=ar.alloc([128, 2, 512], BF16), b=ar.alloc([128, 2, 512], BF16), c=ar.alloc([128, 4, 512], BF16)) for _ in range(2)]
        sig = [ar.alloc([128, 512], F32) for _ in range(3)]
        u = [ar.alloc([128, 512], F32) for _ in range(3)]
        mT = ar.alloc([128, 8, 512], BF16)
        winF, pab, pbb, pcb, woutb = self.winF[l], self.pab[l], self.pbb[l], self.pcb[l], self.woutb[l]
        sc.dma(lambda e: e.dma_start(out=pA, in_=pab.ap().rearrange("u p f -> p u f")), r=[('pabc', l)], w=[('A', 'pA')])
        sc.dma(lambda e: e.dma_start(out=pB, in_=pbb.ap().rearrange("u p f -> p u f")), r=[('pabc', l)], w=[('A', 'pB')])
        sc.dma(lambda e: e.dma_start(out=pC, in_=pcb.ap().rearrange("u p f -> p u f")), r=[('pabc', l)], w=[('A', 'pC')])
        sc.dma(lambda e: e.dma_start(out=wO, in_=woutb.ap().rearrange("h p f -> p h f")), r=[('woutb', l)], w=[('A', 'wO')])
        yaT, ybT, ycT = self.yaT, self.ybT, self.ycT
        hkeys = [('HT', j) for j in range(4)]
        gctr = 0
        items_per_group = min(7, (len(self.castq) + NG - 1) // NG) if self.castq else 0
        nxt = self.front_a(self.xs, 'xs', 0, G, gkey)
        self.norm_b()
        for g in range(NG):
            XG, kx = nxt
            Y = yl[g % 2]
            ky = ('A', 'yl', g % 2)
            sl = slice(g * 512, (g + 1) * 512)
            sc.dma(lambda e, Y=Y, sl=sl: e.dma_start(out=Y['a'], in_=yaT.ap().rearrange("c p s -> p c s")[:, :, sl]), r=[('yaT', g)], w=[(ky, 'a')])
            sc.dma(lambda e, Y=Y, sl=sl: e.dma_start(out=Y['b'], in_=ybT.ap().rearrange("c p s -> p c s")[:, :, sl]), r=[('ybT', g)], w=[(ky, 'b')])
            sc.dma(lambda e, Y=Y, sl=sl: e.dma_start(out=Y['c'], in_=ycT.ap().rearrange("c p s -> p c s")[:, :, sl]),
                   r=[('ycT', h) for h in range(4)], w=[(ky, 'c')])
            for fo in range(8):
                bset = (4, 5, 6) if fo % 2 == 0 else (7, 3, 2)
                for br, (pw_, yk, nk) in enumerate(((pA, 'a', 2), (pB, 'b', 2), (pC, 'c', 4))):
                    bank = bset[br]
                    for k in range(nk):
                        sc.pe(lambda e, pw_=pw_, yk=yk, k=k, nk=nk, fo=fo, bank=bank, Y=Y: e.matmul(
                            PS[bank][:, :], lhsT=pw_[:, fo, k * 128:(k + 1) * 128], rhs=Y[yk][:, k, :], start=(k == 0), stop=(k == nk - 1)),
                            r=[('A', 'pA'), ('A', 'pB'), ('A', 'pC'), (ky, yk)], w=[('ps', bank)])
                for br in range(3):
                    s = gctr % NRG
                    gctr += 1
                    wt = wG[s]
                    kw = ('A', 'wG', s)
                    sc.dma(lambda e, wt=wt, br=br, fo=fo: e.dma_start(out=wt, in_=winF[6 + br * 8 + fo]), r=[('winF', l)], w=[kw])
                    bank = gctr % 2
                    for k in range(8):
                        sc.pe(lambda e, wt=wt, k=k, bank=bank: e.matmul(PS[bank][:, :], lhsT=wt[:, k * 128:(k + 1) * 128], rhs=HT[:, k, :],
                                                                        start=(k == 0), stop=(k == 7)), r=[kw] + hkeys, w=[('ps', bank)])
                    sc.act(lambda e, br=br, bank=bank: e.activation(out=sig[br], in_=PS[bank][:, :], func=AF.Sigmoid),
                           r=[('ps', bank)], w=[('A', 'sig', br)])
                    sc.dve(lambda e, br=br, bb=bset[br]: e.tensor_tensor(out=u[br], in0=PS[bb][:, :], in1=sig[br], op=ALU.mult),
                           r=[('ps', bset[br]), ('A', 'sig', br)], w=[('A', 'u', br)])
                sc.pool(lambda e: e.tensor_tensor(out=u[0], in0=u[0], in1=u[1], op=ALU.add), r=[('A', 'u', 0), ('A', 'u', 1)], w=[('A', 'u', 0)])
                sc.pool(lambda e, fo=fo: e.tensor_tensor(out=mT[:, fo, :], in0=u[0], in1=u[2], op=ALU.add),
                        r=[('A', 'u', 0), ('A', 'u', 2)], w=[('A', 'mT', fo)])
            if g + 1 < NG:
                nxt = self.front_a(self.xs, 'xs', g + 1, G, gkey)
            for j in range(4):
                for half in range(2):
                    bank = (j * 2 + half) % 2
                    for fo in range(8):
                        sc.pe(lambda e, j=j, half=half, fo=fo, bank=bank: e.matmul(
                            PS[bank][:, :], lhsT=mT[:, fo, j * 128:(j + 1) * 128], rhs=wO[:, half, fo * 512:(fo + 1) * 512],
                            start=(fo == 0), stop=(fo == 7)), r=[('A', 'mT', fo), ('A', 'wO')], w=[('ps', bank)])
                    sc.dve(lambda e, j=j, half=half, bank=bank, XG=XG: e.tensor_tensor(
                        out=XG[:, j, half * 512:(half + 1) * 512], in0=PS[bank][:, :], in1=XG[:, j, half * 512:(half + 1) * 512], op=ALU.add),
                        r=[('ps', bank), kx], w=[kx])
            dst = self.xrows(self.xs, g)
            sc.dma(lambda e, XG=XG, dst=dst: e.dma_start(out=dst, in_=XG[:]), r=[kx], w=[('xs', g)])
            if g + 1 < NG:
                self.norm_b()
            self.drain(items_per_group)
        self.barrier()

    def program(self):
        for l in range(NL):
            self.queue_casts(l)
        nf = 2 * NFF + NFF // 2
        per_layer = 2 * nf + 47
        for l in range(NL):
            base = l * per_layer
            if l == 0:
                self.ffn_phase(l, 0, self.x_in, 'x_in', lazy_base=0)
            else:
                self.need_casts(base + nf)
                self.ffn_phase(l, 0, self.xs, 'xs')
            self.layer_params(l)
            self.need_casts(base + nf + 47)
            self.phaseB(l)
            self.mlstm_phase(l)
            self.attn_phase(l)
            self.phaseD(l)
            self.need_casts(base + per_layer)
            self.ffn_phase(l, 1, self.xs, 'xs', final=(l == NL - 1))


def build(S, debug=False, phases=None):
    from contextlib import ExitStack
    b = Builder(S, debug=debug, phases=phases)
    b.declare()
    ph = phases
    with ExitStack() as st:
        b.init_consts()
        if ph is None:
            b.program()
        else:
            ph(b)
        b.sc.emit(b.nc, st)
    return b


_BUILD_CACHE = {}
SEQ = 4096
NCORES = 8


def kernel(**inputs):
    if SEQ not in _BUILD_CACHE:
        _BUILD_CACHE[SEQ] = build(SEQ)
    b = _BUILD_CACHE[SEQ]
    x = np.asarray(inputs["x"], dtype=np.float32)
    pos = np.asarray(inputs["positions"], dtype=np.int32)
    cst = make_consts()
    shared = {"cst": cst}
    for n, s in WSHAPES:
        shared[n] = np.ascontiguousarray(np.asarray(inputs[n], dtype=np.float32).reshape(s))
    in_maps = []
    for c in range(NCORES):
        m = dict(shared)
        m["x"] = np.ascontiguousarray(x[c])
        m["pos"] = np.ascontiguousarray(pos[c].reshape(SEQ // 128, 128).T)
        in_maps.append(m)
    res = run_bass_kernel_spmd(b.nc, in_maps, core_ids=list(range(NCORES)))
    out = np.stack([np.asarray(r["out"], dtype=np.float32) for r in res.results], axis=0)
    return out
```
